# Optimizing a Trainium2 kernel written in Bass

```python
import math
import jax, jax.numpy as jnp
from jax import lax
import numpy as np

D_MODEL = 2048
BATCH = 2
SEQ = 8192
DEPTH = 2

CHUNK = 64
N_META = 16

DN_ALPHA = (2 * DEPTH) ** 0.25
DN_BETA = (8 * DEPTH) ** -0.25
LN_EPS = 1e-5
RMS_EPS = 1e-6

S5_WIDTH = D_MODEL // 2
S5_GROUP = 16
S5_GROUPS = S5_WIDTH // S5_GROUP
S5_STATE = 64

MLA_HEADS = D_MODEL // 256
MLA_NOPE = 128
MLA_ROPE = 64
MLA_V = 128
MLA_Q_RANK = D_MODEL // 4
MLA_KV_RANK = D_MODEL // 8
ROPE_BASE = 10000.0
Q_BLOCK = 128

L0_IN = S5_WIDTH + MLA_Q_RANK + MLA_KV_RANK + MLA_ROPE
L0_MIX = S5_WIDTH + MLA_HEADS * MLA_V

SSD_INNER = 2 * D_MODEL
SSD_HEAD_DIM = 64
SSD_HEADS = SSD_INNER // SSD_HEAD_DIM
SSD_GROUPS = 8
SSD_HPG = SSD_HEADS // SSD_GROUPS
SSD_STATE = 128
SSD_CONV = 4
SSD_BLOCK = 128
SSD_CONV_DIM = SSD_INNER + 2 * SSD_GROUPS * SSD_STATE
L1_IN = SSD_INNER + SSD_CONV_DIM + SSD_HEADS

FFN_HIDDEN = -(-(8 * D_MODEL) // (3 * 256)) * 256

kernel_name = "hybrid_s5_mla_ssd_deepnorm_encoder"


def layer_norm(x, g, b):
    xf = x.astype(jnp.float32)
    mu = jnp.mean(xf, axis=-1, keepdims=True)
    var = jnp.mean(jnp.square(xf - mu), axis=-1, keepdims=True)
    y = (xf - mu) * lax.rsqrt(var + LN_EPS) * g.astype(jnp.float32) + b.astype(jnp.float32)
    return y.astype(x.dtype)


def rms_norm(x, g):
    xf = x.astype(jnp.float32)
    y = xf * lax.rsqrt(jnp.mean(jnp.square(xf), axis=-1, keepdims=True) + RMS_EPS)
    return (y * g.astype(jnp.float32)).astype(x.dtype)


def chunk_ids(n):
    p = jnp.arange(n)
    return jnp.where(p < N_META, 0, 1 + (p - N_META) // CHUNK)


def rope_tables(n):
    pos = jnp.arange(n, dtype=jnp.float32)
    inv = ROPE_BASE ** (-jnp.arange(0, MLA_ROPE, 2, dtype=jnp.float32) / MLA_ROPE)
    ang = pos[:, None] * inv[None, :]
    return jnp.cos(ang), jnp.sin(ang)


def apply_rope(x, cos, sin):
    half = x.shape[-1] // 2
    x1, x2 = x[..., :half], x[..., half:]
    cos = cos.astype(x.dtype)
    sin = sin.astype(x.dtype)
    return jnp.concatenate([x1 * cos - x2 * sin, x2 * cos + x1 * sin], axis=-1)


def _complex_affine_combine(e1, e2):
    a1r, a1i, b1r, b1i = e1
    a2r, a2i, b2r, b2i = e2
    ar = a2r * a1r - a2i * a1i
    ai = a2r * a1i + a2i * a1r
    br = a2r * b1r - a2i * b1i + b2r
    bi = a2r * b1i + a2i * b1r + b2i
    return ar, ai, br, bi


def s5_mixer(u, log_dt, a_re, a_im, b_re, b_im, c_re, c_im, d, w_glu):
    bsz, n, _ = u.shape
    ug = u.reshape(bsz, n, S5_GROUPS, S5_GROUP)
    dt = jnp.exp(log_dt)[:, None]
    mag = jnp.exp(dt * a_re)
    ab_re = mag * jnp.cos(dt * a_im)
    ab_im = mag * jnp.sin(dt * a_im)
    den = a_re * a_re + a_im * a_im
    nr = ab_re - 1.0
    f_re = (nr * a_re + ab_im * a_im) / den
    f_im = (ab_im * a_re - nr * a_im) / den
    bb_re = f_re[..., None] * b_re - f_im[..., None] * b_im
    bb_im = f_re[..., None] * b_im + f_im[..., None] * b_re
    bu_re = jnp.einsum('blgj,gpj->lbgp', ug, bb_re)
    bu_im = jnp.einsum('blgj,gpj->lbgp', ug, bb_im)
    a_seq_re = jnp.broadcast_to(ab_re, (n, 1) + ab_re.shape)
    a_seq_im = jnp.broadcast_to(ab_im, (n, 1) + ab_im.shape)
    _, _, s_re, s_im = lax.associative_scan(
        _complex_affine_combine, (a_seq_re, a_seq_im, bu_re, bu_im), axis=0)
    y = (jnp.einsum('lbgp,gjp->blgj', s_re, c_re)
         - jnp.einsum('lbgp,gjp->blgj', s_im, c_im))
    y = y.reshape(bsz, n, S5_WIDTH) + d * u
    g = jax.nn.gelu(y)
    return g * jax.nn.sigmoid(g @ w_glu)


def mla_mixer(q_lat, kv_lat, k_rope_raw, q_norm, w_uq, kv_norm, w_ukv, cos, sin, cid):
    bsz, n, _ = q_lat.shape
    q = (rms_norm(q_lat, q_norm) @ w_uq).reshape(bsz, n, MLA_HEADS, MLA_NOPE + MLA_ROPE)
    q_nope = q[..., :MLA_NOPE]
    q_rope = apply_rope(q[..., MLA_NOPE:], cos[:, None, :], sin[:, None, :])
    kv = (rms_norm(kv_lat, kv_norm) @ w_ukv).reshape(bsz, n, MLA_HEADS, MLA_NOPE + MLA_V)
    k_nope = kv[..., :MLA_NOPE]
    v = kv[..., MLA_NOPE:]
    k_rope = apply_rope(k_rope_raw, cos, sin)
    n_pad = -(-n // Q_BLOCK) * Q_BLOCK
    nb = n_pad // Q_BLOCK
    pad = ((0, 0), (0, n_pad - n), (0, 0), (0, 0))
    qn_b = jnp.pad(q_nope, pad).reshape(bsz, nb, Q_BLOCK, MLA_HEADS, MLA_NOPE).transpose(1, 0, 2, 3, 4)
    qr_b = jnp.pad(q_rope, pad).reshape(bsz, nb, Q_BLOCK, MLA_HEADS, MLA_ROPE).transpose(1, 0, 2, 3, 4)
    cid_q = chunk_ids(n_pad).reshape(nb, Q_BLOCK)
    scale = (MLA_NOPE + MLA_ROPE) ** -0.5

    def attend(blk):
        qn, qr, cq = blk
        s = (jnp.einsum('bqhd,bkhd->bhqk', qn, k_nope)
             + jnp.einsum('bqhd,bkd->bhqk', qr, k_rope))
        s = s.astype(jnp.float32) * scale
        visible = cid[None, :] <= cq[:, None]
        s = jnp.where(visible, s, -jnp.inf)
        p = jax.nn.softmax(s, axis=-1).astype(v.dtype)
        return jnp.einsum('bhqk,bkhd->bqhd', p, v)

    o = lax.map(attend, (qn_b, qr_b, cid_q))
    o = o.transpose(1, 0, 2, 3, 4).reshape(bsz, n_pad, MLA_HEADS * MLA_V)
    return o[:, :n]


def s5_mla_mixer(h, cos, sin, cid, w_in, s5_log_dt, s5_a_re, s5_a_im, s5_b_re, s5_b_im,
                 s5_c_re, s5_c_im, s5_d, s5_w_glu, mla_q_norm, mla_w_uq, mla_kv_norm,
                 mla_w_ukv, w_out):
    proj = h @ w_in
    o1 = S5_WIDTH
    o2 = o1 + MLA_Q_RANK
    o3 = o2 + MLA_KV_RANK
    a_out = s5_mixer(proj[..., :o1], s5_log_dt, s5_a_re, s5_a_im, s5_b_re, s5_b_im,
                     s5_c_re, s5_c_im, s5_d, s5_w_glu)
    b_out = mla_mixer(proj[..., o1:o2], proj[..., o2:o3], proj[..., o3:], mla_q_norm,
                      mla_w_uq, mla_kv_norm, mla_w_ukv, cos, sin, cid)
    return jnp.concatenate([a_out, b_out], axis=-1) @ w_out


def causal_depthwise_conv(x, w, b):
    k = w.shape[0]
    y = lax.conv_general_dilated(x, w[:, None, :], window_strides=(1,), padding=((k - 1, 0),),
                                 dimension_numbers=('NWC', 'WIO', 'NWC'),
                                 feature_group_count=x.shape[-1])
    return y + b


def ssd_scan(x, dt, a, bm, cm):
    bsz, n, g, r, p = x.shape
    q = SSD_BLOCK
    nc = n // q
    x = x.reshape(bsz, nc, q, g, r, p)
    dt = dt.reshape(bsz, nc, q, g, r)
    bm = bm.reshape(bsz, nc, q, g, -1)
    cm = cm.reshape(bsz, nc, q, g, -1)
    xdt = x * dt[..., None]
    da = jnp.moveaxis(dt * a, 2, -1)
    cs = jnp.cumsum(da, axis=-1)
    tri = jnp.tril(jnp.ones((q, q), dtype=bool))
    seg = cs[..., :, None] - cs[..., None, :]
    decay = jnp.exp(jnp.where(tri, seg, -jnp.inf))
    cb = jnp.einsum('bclgn,bcsgn->bcgls', cm, bm)
    y_diag = jnp.einsum('bcgls,bcgrls,bcsgrp->bclgrp', cb, decay, xdt)
    decay_states = jnp.exp(cs[..., -1:] - cs)
    states = jnp.einsum('bclgn,bcgrl,bclgrp->bcgrpn', bm, decay_states, xdt)
    chunk_decay = jnp.exp(cs[..., -1])

    def carry_step(hs, inp):
        s_c, a_c = inp
        return hs * a_c[..., None, None] + s_c, hs

    h0 = jnp.zeros_like(states[:, 0])
    _, prev = lax.scan(carry_step, h0, (jnp.moveaxis(states, 1, 0), jnp.moveaxis(chunk_decay, 1, 0)))
    y_off = jnp.einsum('bclgn,cbgrpn,bcgrl->bclgrp', cm, prev, jnp.exp(cs))
    return (y_diag + y_off).reshape(bsz, n, g, r, p)


def mamba2_mixer(h, w_in, conv_w, conv_b, dt_bias, a_log, d, norm_g, w_out):
    bsz, n, _ = h.shape
    zxbcdt = h @ w_in
    z = zxbcdt[..., :SSD_INNER]
    xbc = zxbcdt[..., SSD_INNER:SSD_INNER + SSD_CONV_DIM]
    dt_raw = zxbcdt[..., SSD_INNER + SSD_CONV_DIM:]
    xbc = jax.nn.silu(causal_depthwise_conv(xbc, conv_w, conv_b))
    gn = SSD_GROUPS * SSD_STATE
    xs = xbc[..., :SSD_INNER].reshape(bsz, n, SSD_GROUPS, SSD_HPG, SSD_HEAD_DIM)
    bm = xbc[..., SSD_INNER:SSD_INNER + gn].reshape(bsz, n, SSD_GROUPS, SSD_STATE)
    cm = xbc[..., SSD_INNER + gn:].reshape(bsz, n, SSD_GROUPS, SSD_STATE)
    dt = jax.nn.softplus(dt_raw + dt_bias).reshape(bsz, n, SSD_GROUPS, SSD_HPG)
    a = -jnp.exp(a_log).reshape(SSD_GROUPS, SSD_HPG)
    n_pad = -(-n // SSD_BLOCK) * SSD_BLOCK
    tp = n_pad - n
    y = ssd_scan(jnp.pad(xs, ((0, 0), (0, tp), (0, 0), (0, 0), (0, 0))),
                 jnp.pad(dt, ((0, 0), (0, tp), (0, 0), (0, 0))), a,
                 jnp.pad(bm, ((0, 0), (0, tp), (0, 0), (0, 0))),
                 jnp.pad(cm, ((0, 0), (0, tp), (0, 0), (0, 0))))[:, :n]
    y = y + d.reshape(SSD_GROUPS, SSD_HPG)[..., None] * xs
    y = y.reshape(bsz, n, SSD_INNER) * jax.nn.silu(z)
    y = rms_norm(y.reshape(bsz, n, SSD_GROUPS, SSD_INNER // SSD_GROUPS),
                 norm_g.reshape(SSD_GROUPS, SSD_INNER // SSD_GROUPS)).reshape(bsz, n, SSD_INNER)
    return y @ w_out


def swiglu_ffn(h, w_gate, w_up, w_down):
    return (jax.nn.silu(h @ w_gate) * (h @ w_up)) @ w_down


def setup_inputs(seed: int = 0) -> dict:
    key = jax.random.key(seed)
    ks = iter(jax.random.split(key, 64))

    def nrm(shape, scale):
        return jax.random.normal(next(ks), shape, jnp.float32) * scale

    def uni(shape, lo, hi):
        return jax.random.uniform(next(ks), shape, jnp.float32, lo, hi)

    d = D_MODEL
    f = FFN_HIDDEN
    G, P, J = S5_GROUPS, S5_STATE, S5_GROUP
    inp = {}
    inp['x'] = nrm((BATCH, SEQ, d), 1.0)
    inp['meta_tokens'] = nrm((N_META, d), 1.0)
    inp['l0_w_in'] = nrm((d, L0_IN), d ** -0.5)
    inp['l0_s5_log_dt'] = uni((G,), math.log(0.001), math.log(0.1))
    inp['l0_s5_a_re'] = -0.5 + nrm((G, P), 0.01)
    inp['l0_s5_a_im'] = jnp.pi * jnp.arange(P, dtype=jnp.float32)[None, :] + nrm((G, P), 0.01)
    inp['l0_s5_b_re'] = nrm((G, P, J), (2 * J) ** -0.5)
    inp['l0_s5_b_im'] = nrm((G, P, J), (2 * J) ** -0.5)
    inp['l0_s5_c_re'] = nrm((G, J, P), 0.5 ** 0.5)
    inp['l0_s5_c_im'] = nrm((G, J, P), 0.5 ** 0.5)
    inp['l0_s5_d'] = nrm((S5_WIDTH,), 1.0)
    inp['l0_s5_w_glu'] = nrm((S5_WIDTH, S5_WIDTH), S5_WIDTH ** -0.5)
    inp['l0_mla_q_norm'] = 1.0 + nrm((MLA_Q_RANK,), 0.02)
    inp['l0_mla_w_uq'] = nrm((MLA_Q_RANK, MLA_HEADS * (MLA_NOPE + MLA_ROPE)), MLA_Q_RANK ** -0.5)
    inp['l0_mla_kv_norm'] = 1.0 + nrm((MLA_KV_RANK,), 0.02)
    inp['l0_mla_w_ukv'] = nrm((MLA_KV_RANK, MLA_HEADS * (MLA_NOPE + MLA_V)), MLA_KV_RANK ** -0.5)
    inp['l0_w_out'] = nrm((L0_MIX, d), L0_MIX ** -0.5 * DN_BETA)
    inp['l0_ln1_g'] = 1.0 + nrm((d,), 0.02)
    inp['l0_ln1_b'] = nrm((d,), 0.02)
    inp['l0_ffn_w_gate'] = nrm((d, f), d ** -0.5)
    inp['l0_ffn_w_up'] = nrm((d, f), d ** -0.5 * DN_BETA)
    inp['l0_ffn_w_down'] = nrm((f, d), f ** -0.5 * DN_BETA)
    inp['l0_ln2_g'] = 1.0 + nrm((d,), 0.02)
    inp['l0_ln2_b'] = nrm((d,), 0.02)
    inp['l1_w_in'] = nrm((d, L1_IN), d ** -0.5)
    inp['l1_conv_w'] = nrm((SSD_CONV, SSD_CONV_DIM), SSD_CONV ** -0.5)
    inp['l1_conv_b'] = nrm((SSD_CONV_DIM,), 0.02)
    dt0 = jnp.exp(uni((SSD_HEADS,), math.log(0.001), math.log(0.1)))
    inp['l1_dt_bias'] = dt0 + jnp.log(-jnp.expm1(-dt0))
    inp['l1_a_log'] = jnp.log(uni((SSD_HEADS,), 1.0, 16.0))
    inp['l1_d'] = 1.0 + nrm((SSD_HEADS,), 0.02)
    inp['l1_norm_g'] = 1.0 + nrm((SSD_INNER,), 0.02)
    inp['l1_w_out'] = nrm((SSD_INNER, d), SSD_INNER ** -0.5 * DN_BETA)
    inp['l1_ln1_g'] = 1.0 + nrm((d,), 0.02)
    inp['l1_ln1_b'] = nrm((d,), 0.02)
    inp['l1_ffn_w_gate'] = nrm((d, f), d ** -0.5)
    inp['l1_ffn_w_up'] = nrm((d, f), d ** -0.5 * DN_BETA)
    inp['l1_ffn_w_down'] = nrm((f, d), f ** -0.5 * DN_BETA)
    inp['l1_ln2_g'] = 1.0 + nrm((d,), 0.02)
    inp['l1_ln2_b'] = nrm((d,), 0.02)
    return inp


def reference(x, meta_tokens,
              l0_w_in, l0_s5_log_dt, l0_s5_a_re, l0_s5_a_im, l0_s5_b_re, l0_s5_b_im,
              l0_s5_c_re, l0_s5_c_im, l0_s5_d, l0_s5_w_glu, l0_mla_q_norm, l0_mla_w_uq,
              l0_mla_kv_norm, l0_mla_w_ukv, l0_w_out, l0_ln1_g, l0_ln1_b,
              l0_ffn_w_gate, l0_ffn_w_up, l0_ffn_w_down, l0_ln2_g, l0_ln2_b,
              l1_w_in, l1_conv_w, l1_conv_b, l1_dt_bias, l1_a_log, l1_d, l1_norm_g,
              l1_w_out, l1_ln1_g, l1_ln1_b, l1_ffn_w_gate, l1_ffn_w_up, l1_ffn_w_down,
              l1_ln2_g, l1_ln2_b):
    bsz = x.shape[0]
    meta = jnp.broadcast_to(meta_tokens[None].astype(x.dtype), (bsz, N_META, D_MODEL))
    h = jnp.concatenate([meta, x], axis=1)
    n = h.shape[1]
    cos, sin = rope_tables(n)
    cid = chunk_ids(n)

    mixers = [
        lambda t: s5_mla_mixer(t, cos, sin, cid, l0_w_in, l0_s5_log_dt, l0_s5_a_re, l0_s5_a_im,
                               l0_s5_b_re, l0_s5_b_im, l0_s5_c_re, l0_s5_c_im, l0_s5_d,
                               l0_s5_w_glu, l0_mla_q_norm, l0_mla_w_uq, l0_mla_kv_norm,
                               l0_mla_w_ukv, l0_w_out),
        lambda t: mamba2_mixer(t, l1_w_in, l1_conv_w, l1_conv_b, l1_dt_bias, l1_a_log, l1_d,
                               l1_norm_g, l1_w_out),
    ]
    post = [
        (l0_ln1_g, l0_ln1_b, l0_ffn_w_gate, l0_ffn_w_up, l0_ffn_w_down, l0_ln2_g, l0_ln2_b),
        (l1_ln1_g, l1_ln1_b, l1_ffn_w_gate, l1_ffn_w_up, l1_ffn_w_down, l1_ln2_g, l1_ln2_b),
    ]
    for i in range(DEPTH):
        ln1_g, ln1_b, w_gate, w_up, w_down, ln2_g, ln2_b = post[i]
        h = layer_norm(DN_ALPHA * h + mixers[i](h), ln1_g, ln1_b)
        h = layer_norm(DN_ALPHA * h + swiglu_ffn(h, w_gate, w_up, w_down), ln2_g, ln2_b)
    return h[:, N_META:]
```

```python
import math
import numpy as np
import concourse.bass as bass
import concourse.mybir as mybir
from concourse.bass_utils import run_bass_kernel_spmd

F32 = mybir.dt.float32
BF16 = mybir.dt.bfloat16
I32 = mybir.dt.int32
AF = mybir.ActivationFunctionType
ALU = mybir.AluOpType

D = 2048
NMETA = 16
SEQ = 8192
W = 512
FF = 5632
FC = FF // 128
ALPHA = 4.0 ** 0.25
LN_EPS = 1e-5
RMS_EPS = 1e-6
ATT_SCALE = 192.0 ** -0.5
TWO_PI = 6.283185
GELU_C = 1.5957691216057308


class Tok:
    __slots__ = ("w", "r", "excl")

    def __init__(self, excl=False):
        self.w = None
        self.r = {}
        self.excl = excl


class KB:
    NRING = 8

    def __init__(self, nc):
        self.nc = nc
        self.E = {"pe": nc.tensor, "act": nc.scalar, "dve": nc.vector, "pool": nc.gpsimd, "sp": nc.sync}
        self.sems = {}
        self.cnt = {}
        for e in ("pe", "act", "dve", "pool"):
            self.sems[e] = nc.alloc_semaphore("s_" + e)
            self.cnt[e] = 0
        self.seen = {e: {} for e in self.E}
        self.ring = {}
        self.ring_i = {}
        for q in ("sp", "pool"):
            ks = []
            for i in range(self.NRING):
                k = "d_%s_%d" % (q, i)
                self.sems[k] = nc.alloc_semaphore(k)
                self.cnt[k] = 0
                ks.append(k)
            self.ring[q] = ks
            self.ring_i[q] = 0
        self.ninst = 0
        self.dead = False
        self.mute = False
        self.fast = False
        self.ncc = 0

    def _wait(self, eng, key, val):
        if val <= 0:
            return
        if key == eng and (eng == "pe" or self.fast):
            return
        s = self.seen[eng]
        if s.get(key, 0) >= val:
            return
        s[key] = val
        self.E[eng].wait_ge(self.sems[key], val)

    def _deps(self, eng, reads, writes):
        for t in reads:
            if t.w is not None:
                self._wait(eng, t.w[0], t.w[1])
            if t.excl:
                for k, v in t.r.items():
                    if k != eng:
                        self._wait(eng, k, v)
        for t in writes:
            if t.w is not None:
                self._wait(eng, t.w[0], t.w[1])
            for k, v in t.r.items():
                if k == eng:
                    continue
                self._wait(eng, k, v)

    def _mark(self, ev, reads, writes):
        for t in reads:
            if t.r.get(ev[0], 0) < ev[1]:
                t.r[ev[0]] = ev[1]
        for t in writes:
            t.w = ev
            t.r = {}

    def op(self, eng, fn, reads=(), writes=()):
        if self.dead or self.mute:
            return None
        self._deps(eng, reads, writes)
        inst = fn()
        self.cnt[eng] += 1
        inst.then_inc(self.sems[eng], 1)
        self._mark((eng, self.cnt[eng]), reads, writes)
        self.ninst += 1
        return inst

    def dma(self, q, out, in_, reads=(), writes=()):
        if self.dead or self.mute:
            return None
        i = self.ring_i[q]
        self.ring_i[q] = i + 1
        key = self.ring[q][i % self.NRING]
        self._wait(q, key, self.cnt[key])
        self._deps(q, reads, writes)
        inst = self.E[q].dma_start(out=out, in_=in_)
        self.cnt[key] += 16
        inst.then_inc(self.sems[key], 16)
        self._mark((key, self.cnt[key]), reads, writes)
        self.ninst += 1
        return inst

    def collective(self, kind, groups, in_ap, out_ap, reads=(), writes=()):
        if self.dead or self.mute:
            return None
        key = "cc%d" % self.ncc
        self.ncc += 1
        self.sems[key] = self.nc.alloc_semaphore(key)
        self.cnt[key] = 0
        self._deps("pool", reads, writes)
        inst = self.nc.gpsimd.collective_compute(kind, ALU.bypass, replica_groups=groups, ins=[in_ap], outs=[out_ap])
        self.cnt[key] = 1
        inst.then_inc(self.sems[key], 1)
        self._mark((key, 1), reads, writes)
        return inst

    def barrier(self, final=False):
        if self.dead or self.mute:
            return
        for e in ("pe", "act", "dve", "pool", "sp"):
            for k in self.sems:
                if k.startswith("d_pool") and not final:
                    continue
                self._wait(e, k, self.cnt[k])


def build(NT, dbg=False, stop=None):
    NTOK = NMETA + W * NT
    NKT = 1 + 4 * NT
    NFR = W * NT
    NPK = 4 * NT
    GROUPS = [[0, 1, 2, 3], [4, 5, 6, 7]]
    nc = bass.Bass("TRN2", target_bir_lowering=False)
    kb = KB(nc)
    V = nc.vector
    A = nc.scalar
    P = nc.gpsimd
    PE = nc.tensor

    def chk(name):
        if stop == name and not kb.dead:
            kb.barrier()
            kb.dead = True
            print("STOP at", name, "ninst", kb.ninst)

    def din(name, shape, dt=F32):
        return nc.dram_tensor(name, list(shape), dt, kind="ExternalInput").ap()

    def dscr(name, shape, dt=BF16):
        return nc.dram_tensor(name, list(shape), dt, kind="Internal").ap()

    xT = din("xT", [D, NTOK])
    outT = nc.dram_tensor("outT", [D, W * NT], F32, kind="ExternalOutput").ap()
    ropeC_d = din("ropeC", [128, NTOK])
    ropeS_d = din("ropeS", [128, NTOK])
    cst_d = din("cst", [128, 5 * 128])
    iota_d = din("iota", [128, W])
    WSPEC = {
        "w0in": (16, 2048), "wglu": (8, 1024), "wuq": (16, 512), "wuk": (8, 256),
        "w0out": (16, 2048), "w0g": (FC, 2048), "w0u": (FC, 2048), "w0d0": (16, FF // 2), "w0d1": (16, FF // 2),
        "w1in": (81, 2048), "w1out": (16, 4096), "w1g": (FC, 2048), "w1u": (FC, 2048), "w1d0": (16, FF // 2), "w1d1": (16, FF // 2),
    }
    wf = {}
    wbf = {}
    for n, (mb, k) in WSPEC.items():
        wf[n] = din(n, [mb, 128, k])
        wbf[n] = dscr(n + "_b", [mb, 128, k])
    wuv_d = din("wuv", [128, 2, 1024])
    vec_d = din("vec", [128, 8, 16])
    s5d_d = din("s5d", [128, 8])
    qn_d = din("qn", [128, 4])
    kvn_d = din("kvn", [128, 2])
    cw_d = din("cw", [128, 48, 4])
    cb_d = din("cb", [128, 48])
    dtb_d = din("dtb", [128, 2])
    l1d_d = din("l1d", [128, 32])
    ng_d = din("ng", [128, 32])
    s5p_d = din("s5p", [128, 3, 32])
    s5r_d = din("s5r", [3, 128, 32, 128])
    s5b_d = din("s5b", [2, 128, 32, 128])
    s5c_d = din("s5c", [32, 128, 2, 128])
    BL_d = dscr("BL", [32, 128, 2, 128])
    CL_d = dscr("CL", [32, 128, 3, 128])
    TAB_d = dscr("TAB", [32, 128, 2, W], F32)
    Kc_d = [dscr("Kc%d" % h, [128, NTOK]) for h in range(8)]
    Kg_d = [dscr("Kg%d" % h, [4, 128, NTOK]) for h in range(8)]
    Vm_d = dscr("Vm", [128, 1024])
    Vf_d = [dscr("Vf%d" % q, [128, 4, 1024]) for q in range(NT)]
    Vgf_d = [dscr("Vgf%d" % q, [4, 128, 4, 1024]) for q in range(NT)]
    KR_d = dscr("KR", [128, NTOK])
    KRg_d = dscr("KRg", [4, 128, NTOK])
    Gl_d = dscr("Gl", [128, 64], F32)
    Gg_d = dscr("Gg", [4, 128, 64], F32)
    Hl_d = dscr("Hl", [128, 192], F32)
    Hg_d = dscr("Hg", [4, 128, 192], F32)
    STl_d = [dscr("STl%d" % i, [128, 2048], F32) for i in range(2)]
    STg_d = [dscr("STg%d" % i, [4, 128, 2048], F32) for i in range(2)]
    STin_d = dscr("STin", [128, 4096], F32)
    dsl_d = dscr("dsl", [128, 64], F32)
    dsg_d = dscr("dsg", [4, 128, 64], F32)
    H0_d = dscr("H0", [D, NTOK], F32)
    sel_d = din("sel", [128, 8])
    segb_d = din("segb", [128, 4])
    dbg_d = {}
    if dbg:
        for n in ("d_l0", "d_mix", "d_l0a", "d_l1mix"):
            dbg_d[n] = nc.dram_tensor(n, [D, NTOK], F32, kind="ExternalOutput").ap()

    def sb(name, shape, dt=F32):
        return nc.alloc_sbuf_tensor("sb_" + name, list(shape), dt)

    hT = sb("hT", [128, 16, W])
    hb = sb("hb", [128, 16, W], BF16)
    X = sb("X", [128, 32, W], BF16)
    NSLAB = 3
    slabs = [sb("slab%d" % i, [128, 4096], BF16) for i in range(NSLAB)]
    t_slab = [Tok() for _ in range(NSLAB)]
    slab_i = [0]
    ST = sb("ST", [128, 64, 64])
    STb = sb("STb", [128, 64, 64], BF16)
    cst = sb("cst32", [128, 5 * 128])
    cstb = sb("cstb", [128, 5 * 128], BF16)
    iota = sb("iota", [128, W])
    wuv = sb("wuv", [128, 2, 1024], BF16)
    vec = sb("vec", [128, 8, 16])
    s5d = sb("s5d", [128, 8])
    qn = sb("qn", [128, 4])
    kvn_g = sb("kvn_g", [128, 2])
    cw = sb("cw", [128, 48, 4])
    cb = sb("cb", [128, 48])
    dtb = sb("dtb", [128, 2])
    aneg = sb("aneg", [128, 1])
    l1d = sb("l1d", [128, 32])
    ng = sb("ng", [128, 32])
    halo = sb("halo", [128, 48, 4])
    halo_in = sb("halo_in", [128, 48, 4])
    Gin = sb("Gin", [128, 2, 32])
    ANc = sb("ANc", [128, 2, 32])
    sel = sb("sel", [128, 8])
    segb = sb("segb", [128, 4])
    dsum = sb("dsum", [128, 64])
    s5R = sb("s5R", [128, 32])
    s5F = sb("s5F", [128, 32])
    rotc = sb("rotc", [128, 2, 32])
    rots = sb("rots", [128, 2, 32])
    ZR = sb("ZR", [128, 32])
    ZI = sb("ZI", [128, 32])
    ZRi = sb("ZRi", [128, 32])
    ZIi = sb("ZIi", [128, 32])
    ztmp = sb("ztmp", [128, 4, 32])
    t_const = Tok()
    t_hT, t_hb, t_X = Tok(), Tok(), Tok()
    t_ST, t_STb, t_halo, t_Z = Tok(), Tok(), Tok(), Tok()
    ident = cst[:, 0:128]
    U1o = cst[:, 128:384]
    U2 = cst[:, 384:512]
    ones32 = cst[:, 512:640]
    identb = cstb[:, 0:128]
    onesb = cstb[:, 512:640]

    a0, a1 = nc.bump_sbuf(nc.sbuf_bytes_remaining - 64)
    ARENA = a1 - a0

    class Arena:
        def __init__(self):
            self.off = 0
            self.n = 0

        def reset(self, off=0):
            self.off = off

        def get(self, shape, dt=F32):
            size = int(np.prod(shape[1:])) * (4 if dt in (F32, I32) else 2)
            size = (size + 31) // 32 * 32
            assert self.off + size <= ARENA, ("arena overflow", self.off, size, ARENA)
            self.n += 1
            t = nc.alloc_sbuf_tensor_at("ar%d" % self.n, list(shape), dt, offset=a0 + self.off)
            self.off += size
            return t

    ar = Arena()

    banks = [nc.alloc_psum_tensor("bank%d" % i, [128, 512], F32) for i in range(8)]
    t_bank = [Tok(excl=True) for _ in range(8)]

    def load_slab(name, mb, K):
        i = slab_i[0] % NSLAB
        slab_i[0] += 1
        kb.dma("sp", slabs[i][:, 0:K], wbf[name][mb], reads=[t_wbn[name]], writes=[t_slab[i]])
        return slabs[i], t_slab[i]

    def linear(names, rhs_fn, rhs_toks, Wt, bank_ids, epi):
        if isinstance(names, str):
            names = [names]
        MB = WSPEC[names[0]][0]
        KCT = sum(WSPEC[n][1] // 128 for n in names)
        for mb in range(MB):
            b = bank_ids[mb % len(bank_ids)]
            k0 = 0
            for n in names:
                K = WSPEC[n][1]
                sl, tsl = load_slab(n, mb, K)
                for kc in range(K // 128):
                    kb.op("pe", lambda: PE.matmul(banks[b][:, 0:Wt], lhsT=sl[:, kc * 128:(kc + 1) * 128], rhs=rhs_fn(k0 + kc),
                                                  start=(k0 + kc == 0), stop=(k0 + kc == KCT - 1)),
                          reads=[tsl] + rhs_toks, writes=[t_bank[b]])
                k0 += K // 128
            epi(mb, b)

    def act(out, in_, func, reads, writes, eng="act", **kw):
        return kb.op("act", lambda: A.activation(out=out, in_=in_, func=func, **kw), reads=reads, writes=writes)

    nopool = [False]

    def tt(eng, out, in0, in1, op, reads, writes):
        if nopool[0]:
            eng = "dve"
        e = V if eng == "dve" else P
        return kb.op(eng, lambda: e.tensor_tensor(out=out, in0=in0, in1=in1, op=op), reads=reads, writes=writes)

    def ts(eng, out, in0, s1, s2, op0, op1, reads, writes):
        if nopool[0]:
            eng = "dve"
        e = V if eng == "dve" else P
        if op1 is None:
            return kb.op(eng, lambda: e.tensor_scalar(out=out, in0=in0, scalar1=s1, scalar2=None, op0=op0), reads=reads, writes=writes)
        return kb.op(eng, lambda: e.tensor_scalar(out=out, in0=in0, scalar1=s1, scalar2=s2, op0=op0, op1=op1), reads=reads, writes=writes)

    def stt(eng, out, in0, scalar, in1, op0, op1, reads, writes):
        eng = "dve"
        e = V
        return kb.op(eng, lambda: e.scalar_tensor_tensor(out=out, in0=in0, scalar=scalar, in1=in1, op0=op0, op1=op1), reads=reads, writes=writes)

    def cp(eng, out, in_, reads, writes):
        if eng == "act":
            return kb.op("act", lambda: A.copy(out=out, in_=in_), reads=reads, writes=writes)
        if nopool[0]:
            eng = "dve"
        e = V if eng == "dve" else P
        return kb.op(eng, lambda: e.tensor_copy(out=out, in_=in_), reads=reads, writes=writes)

    def frac_sin(out, x, tmpi, tmpf, toks, eng="dve"):
        cp(eng, tmpi, x, toks, toks)
        cp(eng, tmpf, tmpi, toks, toks)
        tt(eng, tmpf, x, tmpf, ALU.subtract, toks, toks)
        act(out, tmpf, AF.Sin, toks, toks, scale=TWO_PI)

    t_wb = Tok()
    t_wbn = {n: Tok() for n in WSPEC}
    kb.dma("pool", wuv[:], wuv_d, writes=[t_const])
    for n in ("w0in", "wuk", "wuq", "wglu", "w0out", "w0g", "w0u", "w0d0", "w0d1", "w1in", "w1out", "w1g", "w1u", "w1d0", "w1d1"):
        for m in range(WSPEC[n][0]):
            kb.dma("pool", wbf[n][m], wf[n][m], writes=[t_wbn[n]])
    for (dst, src) in ((cst, cst_d), (iota, iota_d), (vec, vec_d), (s5d, s5d_d), (qn, qn_d), (kvn_g, kvn_d),
                       (cw, cw_d), (cb, cb_d), (dtb, dtb_d), (l1d, l1d_d), (ng, ng_d), (sel, sel_d), (segb, segb_d)):
        kb.dma("sp", dst[:], src, writes=[t_const])
    cp("dve", cstb[:], cst[:], [t_const], [t_const])
    act(aneg[:], dtb[:, 1:2], AF.Exp, [t_const], [t_const])
    ts("dve", aneg[:], aneg[:], -1.0, None, ALU.mult, None, [t_const], [t_const])
    kb.op("dve", lambda: V.memset(ST[:], 0.0), writes=[t_ST])
    kb.op("dve", lambda: V.memset(STb[:], 0.0), writes=[t_STb])
    kb.op("dve", lambda: V.memset(halo[:], 0.0), writes=[t_halo])
    kb.op("dve", lambda: V.memset(ZR[:], 0.0), writes=[t_Z])
    kb.op("dve", lambda: V.memset(ZI[:], 0.0), writes=[t_Z])

    ar.reset()
    t_s = Tok()
    sp_ = ar.get([128, 3, 32])
    kb.dma("sp", sp_[:], s5p_d, writes=[t_s])
    dtp = ar.get([128, 32])
    xim = ar.get([128, 32])
    tmpi = ar.get([128, 32], I32)
    tmpf = ar.get([128, 32])
    tmpx = ar.get([128, 32])
    act(dtp[:], sp_[:, 0, :], AF.Exp, [t_s], [t_s])
    tt("dve", tmpx[:], dtp[:], sp_[:, 1, :], ALU.mult, [t_s], [t_s])
    act(s5R[:], tmpx[:], AF.Exp, [t_s], [t_s, t_const])
    tt("dve", xim[:], dtp[:], sp_[:, 2, :], ALU.mult, [t_s], [t_s])
    ts("dve", s5F[:], xim[:], 1.0 / (2 * math.pi), None, ALU.mult, None, [t_s], [t_s, t_const])
    for wi, wd in enumerate((16.0, float(W))):
        ts("dve", tmpx[:], s5F[:], wd, None, ALU.mult, None, [t_s], [t_s])
        frac_sin(rots[:, wi, :], tmpx[:], tmpi[:], tmpf[:], [t_s])
        ts("dve", tmpx[:], s5F[:], wd, 0.25, ALU.mult, ALU.add, [t_s], [t_s])
        frac_sin(rotc[:, wi, :], tmpx[:], tmpi[:], tmpf[:], [t_s])
    tt("dve", tmpx[:], dtp[:], sp_[:, 1, :], ALU.mult, [t_s], [t_s])
    act(tmpf[:], tmpx[:], AF.Exp, [t_s], [t_s], scale=float(NFR))
    cN, sN, tA, tB = ar.get([128, 32]), ar.get([128, 32]), ar.get([128, 32]), ar.get([128, 32])
    cp("dve", cN[:], rotc[:, 1, :], [t_s], [t_s])
    cp("dve", sN[:], rots[:, 1, :], [t_s], [t_s])
    for _ in range(int(round(math.log2(NT)))):
        tt("dve", tA[:], cN[:], cN[:], ALU.mult, [t_s], [t_s])
        tt("dve", tB[:], sN[:], sN[:], ALU.mult, [t_s], [t_s])
        tt("dve", sN[:], cN[:], sN[:], ALU.mult, [t_s], [t_s])
        ts("dve", sN[:], sN[:], 2.0, None, ALU.mult, None, [t_s], [t_s])
        tt("dve", cN[:], tA[:], tB[:], ALU.subtract, [t_s], [t_s])
    tt("dve", ANc[:, 0, :], tmpf[:], cN[:], ALU.mult, [t_s], [t_s, t_const])
    tt("dve", ANc[:, 1, :], tmpf[:], sN[:], ALU.mult, [t_s], [t_s, t_const])
    tabw = [ar.get([128, 2, W]) for _ in range(2)]
    t_tabw = [Tok(), Tok()]
    xw_ = ar.get([128, W])
    xi_ = ar.get([128, W], I32)
    xf_ = ar.get([128, W])
    t_tab = Tok()
    for j in range(32):
        tb, ttb = tabw[j % 2], t_tabw[j % 2]
        ts("dve", xw_[:], iota[:], s5F[:, j:j + 1], None, ALU.mult, None, [t_s, t_const], [t_s])
        cp("dve", xi_[:], xw_[:], [t_s], [t_s])
        cp("dve", xf_[:], xi_[:], [t_s], [t_s])
        tt("dve", xf_[:], xw_[:], xf_[:], ALU.subtract, [t_s], [t_s])
        act(tb[:, 1, :], xf_[:], AF.Sin, [t_s], [ttb], scale=TWO_PI)
        ts("dve", xw_[:], xw_[:], 0.25, None, ALU.add, None, [t_s], [t_s])
        cp("dve", xi_[:], xw_[:], [t_s], [t_s])
        cp("dve", xf_[:], xi_[:], [t_s], [t_s])
        tt("dve", xf_[:], xw_[:], xf_[:], ALU.subtract, [t_s], [t_s])
        act(tb[:, 0, :], xf_[:], AF.Sin, [t_s], [ttb], scale=TWO_PI)
        kb.dma("sp", TAB_d[j], tb[:], reads=[ttb], writes=[t_tab])
    kb.barrier()
    ar.reset()
    RW = 8 * 128
    r_ = [ar.get([128, RW]) for _ in range(3)]
    braw = [ar.get([128, RW]) for _ in range(2)]
    e_ = [ar.get([128, RW]) for _ in range(8)]
    ei_ = ar.get([128, RW], I32)
    bout = ar.get([128, 8, 2, 128], BF16)
    t_r = Tok()
    for q4 in range(4):
        js = slice(q4 * 8, q4 * 8 + 8)
        for i in range(3):
            kb.dma("sp", r_[i][:].rearrange("p (j s) -> p j s", j=8), s5r_d[i][:, js, :], writes=[t_r])
        for i in range(2):
            kb.dma("sp", braw[i][:].rearrange("p (j s) -> p j s", j=8), s5b_d[i][:, js, :], writes=[t_r])
        T = [t_r]
        dt_, xre, xim_, mag, cs_, sn_, t7, t8 = [e[:] for e in e_]
        act(dt_, r_[0][:], AF.Exp, T, T)
        tt("dve", xre, dt_, r_[1][:], ALU.mult, T, T)
        act(mag, xre, AF.Exp, T, T)
        tt("dve", xim_, dt_, r_[2][:], ALU.mult, T, T)
        ts("dve", xim_, xim_, 1.0 / (2 * math.pi), None, ALU.mult, None, T, T)
        frac_sin(sn_, xim_, ei_[:], t7, T)
        ts("dve", xim_, xim_, 0.25, None, ALU.add, None, T, T)
        frac_sin(cs_, xim_, ei_[:], t7, T)
        abre, abim = cs_, sn_
        tt("dve", abre, mag, cs_, ALU.mult, T, T)
        tt("dve", abim, mag, sn_, ALU.mult, T, T)
        den = dt_
        tt("dve", den, r_[1][:], r_[1][:], ALU.mult, T, T)
        tt("dve", t7, r_[2][:], r_[2][:], ALU.mult, T, T)
        tt("dve", den, den, t7, ALU.add, T, T)
        kb.op("dve", lambda: V.reciprocal(out=den, in_=den), reads=T, writes=T)
        nr = xre
        ts("dve", nr, abre, -1.0, None, ALU.add, None, T, T)
        fre, fim = mag, xim_
        tt("dve", t7, nr, r_[1][:], ALU.mult, T, T)
        tt("dve", t8, abim, r_[2][:], ALU.mult, T, T)
        tt("dve", t7, t7, t8, ALU.add, T, T)
        tt("dve", fre, t7, den, ALU.mult, T, T)
        tt("dve", t7, abim, r_[1][:], ALU.mult, T, T)
        tt("dve", t8, nr, r_[2][:], ALU.mult, T, T)
        tt("dve", t7, t7, t8, ALU.subtract, T, T)
        tt("dve", fim, t7, den, ALU.mult, T, T)
        tt("dve", t7, fre, braw[0][:], ALU.mult, T, T)
        tt("dve", t8, fim, braw[1][:], ALU.mult, T, T)
        tt("dve", bout[:, :, 0, :], t7.rearrange("p (j s) -> p j s", j=8), t8.rearrange("p (j s) -> p j s", j=8), ALU.subtract, T, T)
        tt("dve", t7, fre, braw[1][:], ALU.mult, T, T)
        tt("dve", t8, fim, braw[0][:], ALU.mult, T, T)
        tt("dve", bout[:, :, 1, :], t7.rearrange("p (j s) -> p j s", j=8), t8.rearrange("p (j s) -> p j s", j=8), ALU.add, T, T)
        kb.dma("sp", BL_d[js].rearrange("j p c s -> p j c s"), bout[:], reads=T, writes=[t_tab])
    kb.barrier()
    ar.reset()
    craw = ar.get([128, 8, 2, 128])
    cout = ar.get([128, 8, 3, 128], BF16)
    t_c = Tok()
    for q4 in range(4):
        js = slice(q4 * 8, q4 * 8 + 8)
        kb.dma("sp", craw[:], s5c_d[js].rearrange("j p c s -> p j c s"), writes=[t_c])
        cp("dve", cout[:, :, 0, :], craw[:, :, 0, :], [t_c], [t_c])
        ts("dve", cout[:, :, 1, :], craw[:, :, 0, :], -1.0, None, ALU.mult, None, [t_c], [t_c])
        ts("dve", cout[:, :, 2, :], craw[:, :, 1, :], -1.0, None, ALU.mult, None, [t_c], [t_c])
        kb.dma("sp", CL_d[js].rearrange("j p c s -> p j c s"), cout[:], reads=[t_c], writes=[t_tab])
    kb.barrier()
    chk("setup")

    def layer_norm(Wt, gi, bi, bq_s1, bq_s2, wk):
        sq, mean, rstd, nmr, tmp = wk
        tw = Tok()
        for c in range(16):
            act(sq[:, 0:Wt], hT[:, c, 0:Wt], AF.Square, [t_hT], [tw])
            kb.op("pe", lambda: PE.matmul(banks[bq_s1][:, 0:Wt], lhsT=ones32, rhs=hT[:, c, 0:Wt], start=(c == 0), stop=(c == 15)),
                  reads=[t_hT, t_const], writes=[t_bank[bq_s1]])
            kb.op("pe", lambda: PE.matmul(banks[bq_s2][:, 0:Wt], lhsT=ones32, rhs=sq[:, 0:Wt], start=(c == 0), stop=(c == 15)),
                  reads=[tw, t_const], writes=[t_bank[bq_s2]])
        ts("dve", mean[:, 0:Wt], banks[bq_s1][:, 0:Wt], 1.0 / D, None, ALU.mult, None, [t_bank[bq_s1]], [tw])
        tt("dve", tmp[:, 0:Wt], mean[:, 0:Wt], mean[:, 0:Wt], ALU.mult, [tw], [tw])
        stt("dve", tmp[:, 0:Wt], banks[bq_s2][:, 0:Wt], 1.0 / D, tmp[:, 0:Wt], ALU.mult, ALU.subtract, [t_bank[bq_s2], tw], [tw])
        ts("dve", tmp[:, 0:Wt], tmp[:, 0:Wt], LN_EPS, None, ALU.add, None, [tw], [tw])
        act(tmp[:, 0:Wt], tmp[:, 0:Wt], AF.Sqrt, [tw], [tw])
        kb.op("dve", lambda: V.reciprocal(out=rstd[:, 0:Wt], in_=tmp[:, 0:Wt]), reads=[tw], writes=[tw])
        stt("dve", nmr[:, 0:Wt], mean[:, 0:Wt], -1.0, rstd[:, 0:Wt], ALU.mult, ALU.mult, [tw], [tw])
        for c in range(16):
            e = "dve" if c % 2 == 0 else "pool"
            tt(e, hT[:, c, 0:Wt], hT[:, c, 0:Wt], rstd[:, 0:Wt], ALU.mult, [tw, t_hT], [t_hT])
            tt(e, hT[:, c, 0:Wt], hT[:, c, 0:Wt], nmr[:, 0:Wt], ALU.add, [tw, t_hT], [t_hT])
            act(hT[:, c, 0:Wt], hT[:, c, 0:Wt], AF.Identity, [t_hT, t_const], [t_hT],
                scale=vec[:, gi, c:c + 1], bias=vec[:, bi, c:c + 1])
            cp("pool" if c % 2 == 0 else "dve", hb[:, c, 0:Wt], hT[:, c, 0:Wt], [t_hT], [t_hb])

    def ffn(Wt, gname, uname, dname):
        ar.reset()
        actb = ar.get([128, FC, W], BF16)
        sa = [ar.get([128, W]) for _ in range(2)]
        t_act, t_sa = Tok(), [Tok(), Tok()]
        for fb in range(FC):
            bg, bu = (0, 1) if fb % 2 == 0 else (2, 3)
            for (nm, b) in ((gname, bg), (uname, bu)):
                sl, tsl = load_slab(nm, fb, 2048)
                for kc in range(16):
                    kb.op("pe", lambda: PE.matmul(banks[b][:, 0:Wt], lhsT=sl[:, kc * 128:(kc + 1) * 128], rhs=hb[:, kc, 0:Wt],
                                                  start=(kc == 0), stop=(kc == 15)),
                          reads=[tsl, t_hb], writes=[t_bank[b]])
            s_, ts_ = sa[fb % 2], t_sa[fb % 2]
            act(s_[:, 0:Wt], banks[bg][:, 0:Wt], AF.Silu, [t_bank[bg]], [ts_])
            tt("dve", actb[:, fb, 0:Wt], s_[:, 0:Wt], banks[bu][:, 0:Wt], ALU.mult, [ts_, t_bank[bu]], [t_act])

        def epi(mb, b):
            stt("dve", hT[:, mb, 0:Wt], hT[:, mb, 0:Wt], ALPHA, banks[b][:, 0:Wt], ALU.mult, ALU.add, [t_hT, t_bank[b]], [t_hT])
        linear([dname + "0", dname + "1"], lambda kc: actb[:, kc, 0:Wt], [t_act], Wt, [4, 5, 6, 7], epi)
        return [sa[0], sa[1]]

    def dump(name, pos0, Wt):
        if dbg:
            kb.dma("sp", dbg_d[name][:, pos0:pos0 + Wt].rearrange("(c p) w -> p c w", p=128), hT[:, :, 0:Wt], reads=[t_hT])

    t_Kc = [Tok() for _ in range(8)]
    t_Vc, t_KR = Tok(), Tok()
    t_out = Tok()

    t_Kg, t_Vg, t_KRg, t_Gl, t_Gg, t_Hl, t_Hg = Tok(), Tok(), Tok(), Tok(), Tok(), Tok(), Tok()
    t_STl, t_STg, t_STin, t_dsl, t_dsg, t_H0 = Tok(), Tok(), Tok(), Tok(), Tok(), Tok()
    t_hin, t_Gin, t_dsum = Tok(), Tok(), Tok()
    sched = [(ph, t) for ph in (1, 2, 3, 4) for t in range(NT + 1)]
    for (phase, ti) in sched:
        Wt = NMETA if ti == 0 else W
        pos0 = 0 if ti == 0 else NMETA + W * (ti - 1)
        wi_prev = 0 if ti == 1 else 1
        nsub = (Wt + 127) // 128
        m12, m1, m2 = phase not in (1, 2), phase != 1, phase != 2
        m34, m3, m4 = phase not in (3, 4), phase != 3, phase != 4
        kb.mute = False
        nopool[0] = (phase == 1)
        kb.barrier()
        ar.reset()
        if ti == 0 and phase in (1, 2):
            kb.op("dve", lambda: V.memset(ZR[:], 0.0), writes=[t_Z])
            kb.op("dve", lambda: V.memset(ZI[:], 0.0), writes=[t_Z])
        if ti == 0 and phase == 2:
            Gg = ar.get([128, 4, 2, 32])
            Tc = ar.get([128, 2, 32])
            t4 = [ar.get([128, 32]) for _ in range(4)]
            tg = Tok()
            kb.dma("sp", Gg[:].rearrange("p r c j -> p r (c j)"), Gg_d.rearrange("r p x -> p r x"), reads=[t_Gg], writes=[tg])
            kb.op("dve", lambda: V.memset(Gin[:], 0.0), writes=[t_Gin])
            cp("dve", Tc[:], Gg[:, 0, :, :], [tg], [tg])
            for k in range(1, 4):
                stt("dve", Gin[:], Tc[:], sel[:, k:k + 1], Gin[:], ALU.mult, ALU.add, [tg, t_const, t_Gin], [t_Gin])
                if k < 3:
                    tt("dve", t4[0][:], Tc[:, 0, :], ANc[:, 0, :], ALU.mult, [tg, t_const], [tg])
                    tt("dve", t4[1][:], Tc[:, 1, :], ANc[:, 1, :], ALU.mult, [tg, t_const], [tg])
                    tt("dve", t4[2][:], Tc[:, 0, :], ANc[:, 1, :], ALU.mult, [tg, t_const], [tg])
                    tt("dve", t4[3][:], Tc[:, 1, :], ANc[:, 0, :], ALU.mult, [tg, t_const], [tg])
                    tt("dve", Tc[:, 0, :], t4[0][:], t4[1][:], ALU.subtract, [tg], [tg])
                    tt("dve", Tc[:, 1, :], t4[2][:], t4[3][:], ALU.add, [tg], [tg])
                    tt("dve", Tc[:], Tc[:], Gg[:, k, :, :], ALU.add, [tg], [tg])
            kb.barrier()
            ar.reset()
        if ti == 0 and phase in (3, 4):
            kb.op("dve", lambda: V.memset(ST[:], 0.0), writes=[t_ST])
            kb.op("dve", lambda: V.memset(STb[:], 0.0), writes=[t_STb])
            kb.op("dve", lambda: V.memset(halo[:], 0.0), writes=[t_halo])
            kb.op("dve", lambda: V.memset(dsum[:], 0.0), writes=[t_dsum])
        if ti == 0 and phase == 3:
            Hg = ar.get([128, 4, 192])
            th = Tok()
            kb.dma("sp", Hg[:], Hg_d.rearrange("r p x -> p r x"), reads=[t_Hg], writes=[th])
            kb.op("dve", lambda: V.memset(halo_in[:], 0.0), writes=[t_hin])
            for k in range(1, 4):
                stt("dve", halo_in[:].rearrange("p c k -> p (c k)"), Hg[:, k - 1, :], sel[:, k:k + 1], halo_in[:].rearrange("p c k -> p (c k)"),
                    ALU.mult, ALU.add, [th, t_const, t_hin], [t_hin])
            kb.barrier()
            ar.reset()
        if ti == 0 and phase == 4:
            Eg = ar.get([128, 1, 64, 64])
            Tst = ar.get([128, 64, 64])
            Sin = ar.get([128, 64, 64])
            Dg = ar.get([128, 4, 64])
            te = Tok()
            kb.dma("sp", Dg[:], dsg_d.rearrange("r p x -> p r x"), reads=[t_dsg], writes=[te])
            act(Dg[:], Dg[:], AF.Exp, [te], [te])
            for i_ in range(2):
                kb.dma("sp", Tst[:, 32 * i_:32 * i_ + 32, :].rearrange("p h q -> p (h q)"), STg_d[i_][0], reads=[t_STg], writes=[te])
            kb.op("dve", lambda: V.memset(Sin[:], 0.0), writes=[te])
            for k in range(1, 4):
                stt("dve", Sin[:], Tst[:], sel[:, k:k + 1], Sin[:], ALU.mult, ALU.add, [te, t_const], [te])
                if k < 3:
                    for i_ in range(2):
                        kb.dma("sp", Eg[:, 0, 32 * i_:32 * i_ + 32, :].rearrange("p h q -> p (h q)"), STg_d[i_][k], reads=[t_STg], writes=[te])
                    tt("dve", Tst[:], Tst[:], Dg[:, k, :].unsqueeze(2).broadcast_to([128, 64, 64]), ALU.mult, [te], [te])
                    tt("dve", Tst[:], Tst[:], Eg[:, 0], ALU.add, [te], [te])
            kb.dma("sp", STin_d, Sin[:].rearrange("p h q -> p (h q)"), reads=[te], writes=[t_STin])
            kb.barrier()
            ar.reset()
        kb.mute = m12
        kb.dma("sp", hT[:, :, 0:Wt], xT[:, pos0:pos0 + Wt].rearrange("(c p) w -> p c w", p=128), writes=[t_hT])
        cp("dve", hb[:, 0:8, 0:Wt], hT[:, 0:8, 0:Wt], [t_hT], [t_hb])
        cp("pool", hb[:, 8:16, 0:Wt], hT[:, 8:16, 0:Wt], [t_hT], [t_hb])
        qT = ar.get([128, 12, W], BF16)
        q_end = ar.off
        u32 = ar.get([128, 8, W])
        ub = ar.get([128, 8, W], BF16)
        pers_end = ar.off
        ql = ar.get([128, 4, W])
        kvl = ar.get([128, 2, W])
        krr = ar.get([128, 2, W])
        rC = ar.get([128, W])
        rS = ar.get([128, W])
        yv = ar.get([128, W])
        g2 = ar.get([128, W])
        t_qT, t_yv = Tok(), Tok()
        t_u32, t_ub, t_ql, t_kvl, t_krr, t_rope = Tok(), Tok(), Tok(), Tok(), Tok(), Tok()
        kb.dma("sp", rC[:, 0:Wt], ropeC_d[:, pos0:pos0 + Wt], writes=[t_rope])
        kb.dma("sp", rS[:, 0:Wt], ropeS_d[:, pos0:pos0 + Wt], writes=[t_rope])

        def epi_in(mb, b):
            src = banks[b][:, 0:Wt]
            if mb < 8:
                cp("act", u32[:, mb, 0:Wt], src, [t_bank[b]], [t_u32])
                cp("dve", ub[:, mb, 0:Wt], src, [t_bank[b]], [t_ub])
            elif mb < 12:
                cp("act", ql[:, mb - 8, 0:Wt], src, [t_bank[b]], [t_ql])
            elif mb < 14:
                cp("act", kvl[:, mb - 12, 0:Wt], src, [t_bank[b]], [t_kvl])
            else:
                cp("act", krr[:, mb - 14, 0:Wt], src, [t_bank[b]], [t_krr])
        linear("w0in", lambda kc: hb[:, kc, 0:Wt], [t_hb], Wt, [0, 1, 2, 3], epi_in)
        mixT = X
        chk("inproj%d" % ti)

        def rms(src, t_src, nch, dim, gain, dstb, t_dst, bk):
            for c in range(nch):
                act(yv[:, 0:Wt], src[:, c, 0:Wt], AF.Square, [t_src], [t_yv])
                kb.op("pe", lambda: PE.matmul(banks[bk][:, 0:Wt], lhsT=ones32, rhs=yv[:, 0:Wt], start=(c == 0), stop=(c == nch - 1)),
                      reads=[t_yv, t_const], writes=[t_bank[bk]])
            ts("dve", g2[:, 0:Wt], banks[bk][:, 0:Wt], 1.0 / dim, RMS_EPS, ALU.mult, ALU.add, [t_bank[bk]], [t_yv])
            act(g2[:, 0:Wt], g2[:, 0:Wt], AF.Sqrt, [t_yv], [t_yv])
            kb.op("dve", lambda: V.reciprocal(out=g2[:, 0:Wt], in_=g2[:, 0:Wt]), reads=[t_yv], writes=[t_yv])
            for c in range(nch):
                stt("dve", dstb[:, c, 0:Wt], src[:, c, 0:Wt], gain[:, c:c + 1], g2[:, 0:Wt], ALU.mult, ALU.mult,
                    [t_src, t_yv, t_const], [t_dst])

        qnb = X[:, 16:20, :]
        kvb = X[:, 20:22, :]
        t_qnb, t_kvb = Tok(), Tok()
        kb.mute = m2
        rms(ql, t_ql, 4, 512.0, qn, qnb, t_qnb, 4)
        kb.mute = m1
        rms(kvl, t_kvl, 2, 256.0, kvn_g, kvb, t_kvb, 5)
        kr2 = X[:, 22, :]
        t_kr2 = Tok()
        tt("dve", yv[:, 0:Wt], krr[:, 0, 0:Wt], rC[:, 0:Wt], ALU.mult, [t_krr, t_rope], [t_yv])
        tt("dve", g2[:, 0:Wt], krr[:, 1, 0:Wt], rS[:, 0:Wt], ALU.mult, [t_krr, t_rope], [t_yv])
        tt("dve", kr2[:, 0:Wt], yv[:, 0:Wt], g2[:, 0:Wt], ALU.add, [t_yv], [t_kr2])
        kb.dma("sp", KR_d[:, pos0:pos0 + Wt], kr2[:, 0:Wt], reads=[t_kr2], writes=[t_KR])
        knew = [X[:, 23, :], X[:, 24, :]]
        t_knew = [Tok(), Tok()]

        def epi_k(mb, b):
            cp("act", knew[mb % 2][:, 0:Wt], banks[b][:, 0:Wt], [t_bank[b]], [t_knew[mb % 2]])
            kb.dma("sp", Kc_d[mb][:, pos0:pos0 + Wt], knew[mb % 2][:, 0:Wt], reads=[t_knew[mb % 2]], writes=[t_Kc[mb]])
        linear("wuk", lambda kc: kvb[:, kc, 0:Wt], [t_kvb], Wt, [0, 1], epi_k)
        vnew = [X[:, 25:27, :].rearrange("p a w -> p (a w)"), X[:, 27:29, :].rearrange("p a w -> p (a w)")]
        t_vnew = [Tok(), Tok()]
        for sj in range(nsub):
            L = min(128, Wt - sj * 128)
            kt = 0 if ti == 0 else 1 + 4 * (ti - 1) + sj
            vb, tvb = vnew[sj % 2], t_vnew[sj % 2]
            for half in range(2):
                b = 2 + half
                for kc in range(2):
                    kb.op("pe", lambda: PE.matmul(banks[b][0:L, :], lhsT=kvb[:, kc, sj * 128:sj * 128 + L],
                                                  rhs=wuv[:, kc, half * 512:(half + 1) * 512], start=(kc == 0), stop=(kc == 1)),
                          reads=[t_kvb, t_const], writes=[t_bank[b]])
                cp("act" if half == 0 else "dve", vb[0:L, half * 512:(half + 1) * 512], banks[b][0:L, :], [t_bank[b]], [tvb])
            if ti == 0:
                kb.dma("sp", Vm_d[0:L, :], vb[0:L, :], reads=[tvb], writes=[t_Vc])
            else:
                kb.dma("sp", Vf_d[ti - 1][0:L, sj, :], vb[0:L, :], reads=[tvb], writes=[t_Vc])
        kb.mute = m2

        def epi_q(mb, b):
            hp, r = mb // 4, mb % 4
            src = banks[b][:, 0:Wt]
            if r < 2:
                act(qT[:, hp * 3 + r, 0:Wt], src, AF.Copy, [t_bank[b]], [t_qT], scale=ATT_SCALE)
            elif r == 2:
                tt("dve", yv[:, 0:Wt], src, rC[:, 0:Wt], ALU.mult, [t_bank[b], t_rope], [t_yv])
            else:
                tt("dve", g2[:, 0:Wt], src, rS[:, 0:Wt], ALU.mult, [t_bank[b], t_rope], [t_yv])
                tt("dve", g2[:, 0:Wt], g2[:, 0:Wt], yv[:, 0:Wt], ALU.add, [t_yv], [t_yv])
                act(qT[:, hp * 3 + 2, 0:Wt], g2[:, 0:Wt], AF.Copy, [t_yv], [t_qT], scale=ATT_SCALE)
        linear("wuq", lambda kc: qnb[:, kc, 0:Wt], [t_qnb], Wt, [0, 1, 2, 3], epi_q)

        kb.mute = m12
        chk("a1_%d" % ti)
        kb.barrier()
        ar.reset(pers_end)
        if ti > 0:
            c_, s_ = rotc[:, wi_prev, :], rots[:, wi_prev, :]
            tt("dve", ztmp[:, 0, :], ZR[:], c_, ALU.mult, [t_Z, t_const], [t_Z])
            tt("dve", ztmp[:, 1, :], ZI[:], s_, ALU.mult, [t_Z, t_const], [t_Z])
            tt("dve", ztmp[:, 2, :], ZR[:], s_, ALU.mult, [t_Z, t_const], [t_Z])
            tt("dve", ztmp[:, 3, :], ZI[:], c_, ALU.mult, [t_Z, t_const], [t_Z])
            tt("dve", ZRi[:], ztmp[:, 0, :], ztmp[:, 1, :], ALU.subtract, [t_Z], [t_Z])
            tt("dve", ZIi[:], ztmp[:, 2, :], ztmp[:, 3, :], ALU.add, [t_Z], [t_Z])
            if ti == 1:
                ts("dve", ZRi[:], ZRi[:], sel[:, 0:1], None, ALU.mult, None, [t_Z, t_const], [t_Z])
                ts("dve", ZIi[:], ZIi[:], sel[:, 0:1], None, ALU.mult, None, [t_Z, t_const], [t_Z])
                kb.mute = m2
                tt("dve", ZRi[:], ZRi[:], Gin[:, 0, :], ALU.add, [t_Z, t_Gin], [t_Z])
                tt("dve", ZIi[:], ZIi[:], Gin[:, 1, :], ALU.add, [t_Z, t_Gin], [t_Z])
                kb.mute = m12
        else:
            kb.op("dve", lambda: V.memset(ZRi[:], 0.0), writes=[t_Z])
            kb.op("dve", lambda: V.memset(ZIi[:], 0.0), writes=[t_Z])
        NS = 3
        tabs = [ar.get([128, 2, W]) for _ in range(NS)]
        BLre = [X[:, 28, 0:128], X[:, 28, 256:384], X[:, 29, 384:512]]
        BLim = [X[:, 28, 128:256], X[:, 28, 384:512], X[:, 30, 384:512]]
        CLs = [X[:, 29 + i, 0:384].rearrange("p (c s) -> p c s", c=3) for i in range(3)]
        t_ld = [Tok() for _ in range(NS)]
        w1 = [ar.get([128, W]) for _ in range(4)]
        zz = [ar.get([128, W]) for _ in range(2)]
        pp = [X[:, 24 + i, :] for i in range(4)]
        t_w1 = [Tok() for _ in range(4)]
        t_zz = [Tok() for _ in range(2)]
        t_pp = [Tok() for _ in range(4)]
        yv = ar.get([128, W])
        g2 = ar.get([128, W])
        gb = X[:, 16:24, :]
        t_yv, t_gb = Tok(), Tok()
        for cc in range(8):
            yb = 4 + (cc % 2)
            for q in range(4):
                j = cc * 4 + q
                s = j % NS
                kb.fast = (Wt == W)
                kb.dma("sp", tabs[s][:, :, 0:Wt], TAB_d[j][:, :, 0:Wt], reads=[t_tab], writes=[t_ld[s]])
                kb.dma("sp", BLre[s], BL_d[j][:, 0, :], reads=[t_tab], writes=[t_ld[s]])
                kb.dma("sp", BLim[s], BL_d[j][:, 1, :], reads=[t_tab], writes=[t_ld[s]])
                kb.mute = m2
                kb.dma("sp", CLs[s], CL_d[j], reads=[t_tab], writes=[t_ld[s]])
                kb.mute = m12
                cs_t, sn_t = tabs[s][:, 0, 0:Wt], tabs[s][:, 1, 0:Wt]
                bre, bim = (0, 1) if j % 2 == 0 else (2, 3)
                kb.op("pe", lambda: PE.matmul(banks[bre][:, 0:Wt], lhsT=BLre[s], rhs=ub[:, cc, 0:Wt], start=True, stop=True),
                      reads=[t_ld[s], t_ub], writes=[t_bank[bre]])
                kb.op("pe", lambda: PE.matmul(banks[bim][:, 0:Wt], lhsT=BLim[s], rhs=ub[:, cc, 0:Wt], start=True, stop=True),
                      reads=[t_ld[s], t_ub], writes=[t_bank[bim]])
                tt("dve", w1[0][:, 0:Wt], banks[bre][:, 0:Wt], cs_t, ALU.mult, [t_bank[bre], t_ld[s]], [t_w1[0]])
                tt("dve", w1[1][:, 0:Wt], banks[bim][:, 0:Wt], sn_t, ALU.mult, [t_bank[bim], t_ld[s]], [t_w1[1]])
                tt("dve", w1[2][:, 0:Wt], banks[bim][:, 0:Wt], cs_t, ALU.mult, [t_bank[bim], t_ld[s]], [t_w1[2]])
                tt("dve", w1[3][:, 0:Wt], banks[bre][:, 0:Wt], sn_t, ALU.mult, [t_bank[bre], t_ld[s]], [t_w1[3]])
                tt("pool", w1[0][:, 0:Wt], w1[0][:, 0:Wt], w1[1][:, 0:Wt], ALU.add, [t_w1[1]], [t_w1[0]])
                tt("pool", w1[2][:, 0:Wt], w1[2][:, 0:Wt], w1[3][:, 0:Wt], ALU.subtract, [t_w1[3]], [t_w1[2]])
                kb.op("dve", lambda: V.tensor_tensor_scan(out=zz[0][:, 0:Wt], data0=s5R[:, j:j + 1].broadcast_to([128, Wt]),
                                                          data1=w1[0][:, 0:Wt], initial=ZRi[:, j:j + 1], op0=ALU.mult, op1=ALU.add),
                      reads=[t_w1[0], t_Z, t_const], writes=[t_zz[0]])
                kb.op("dve", lambda: V.tensor_tensor_scan(out=zz[1][:, 0:Wt], data0=s5R[:, j:j + 1].broadcast_to([128, Wt]),
                                                          data1=w1[2][:, 0:Wt], initial=ZIi[:, j:j + 1], op0=ALU.mult, op1=ALU.add),
                      reads=[t_w1[2], t_Z, t_const], writes=[t_zz[1]])
                cp("act", ZR[:, j:j + 1], zz[0][:, Wt - 1:Wt], [t_zz[0]], [t_Z])
                cp("act", ZI[:, j:j + 1], zz[1][:, Wt - 1:Wt], [t_zz[1]], [t_Z])
                kb.mute = m2
                tt("dve", pp[0][:, 0:Wt], zz[0][:, 0:Wt], cs_t, ALU.mult, [t_zz[0], t_ld[s]], [t_pp[0]])
                tt("dve", pp[1][:, 0:Wt], zz[1][:, 0:Wt], sn_t, ALU.mult, [t_zz[1], t_ld[s]], [t_pp[1]])
                tt("pool", pp[2][:, 0:Wt], zz[0][:, 0:Wt], sn_t, ALU.mult, [t_zz[0], t_ld[s]], [t_pp[2]])
                tt("pool", pp[3][:, 0:Wt], zz[1][:, 0:Wt], cs_t, ALU.mult, [t_zz[1], t_ld[s]], [t_pp[3]])
                for k in range(4):
                    kb.op("pe", lambda: PE.matmul(banks[yb][:, 0:Wt], lhsT=CLs[s][:, (0, 1, 2, 2)[k], :], rhs=pp[k][:, 0:Wt],
                                                  start=(q == 0 and k == 0), stop=(q == 3 and k == 3)),
                          reads=[t_ld[s], t_pp[k]], writes=[t_bank[yb]])
                kb.mute = m12
                kb.fast = False
            kb.mute = m2
            stt("dve", yv[:, 0:Wt], u32[:, cc, 0:Wt], s5d[:, cc:cc + 1], banks[yb][:, 0:Wt], ALU.mult, ALU.add,
                [t_u32, t_bank[yb], t_const], [t_yv])
            act(g2[:, 0:Wt], yv[:, 0:Wt], AF.Square, [t_yv], [t_yv])
            ts("dve", g2[:, 0:Wt], g2[:, 0:Wt], 0.044715, 1.0, ALU.mult, ALU.add, [t_yv], [t_yv])
            tt("dve", g2[:, 0:Wt], g2[:, 0:Wt], yv[:, 0:Wt], ALU.mult, [t_yv], [t_yv])
            act(g2[:, 0:Wt], g2[:, 0:Wt], AF.Sigmoid, [t_yv], [t_yv], scale=GELU_C)
            tt("dve", u32[:, cc, 0:Wt], g2[:, 0:Wt], yv[:, 0:Wt], ALU.mult, [t_yv, t_u32], [t_u32])
            cp("pool", gb[:, cc, 0:Wt], u32[:, cc, 0:Wt], [t_u32], [t_gb])
            kb.mute = m12

        def epi_glu(mb, b):
            act(g2[:, 0:Wt], banks[b][:, 0:Wt], AF.Sigmoid, [t_bank[b]], [t_yv])
            tt("dve", mixT[:, mb, 0:Wt], g2[:, 0:Wt], u32[:, mb, 0:Wt], ALU.mult, [t_yv, t_u32], [t_X])
        kb.mute = m2
        linear("wglu", lambda kc: gb[:, kc, 0:Wt], [t_gb], Wt, [6, 7], epi_glu)
        kb.mute = m1
        if ti == NT:
            c_, s_ = rotc[:, 1, :], rots[:, 1, :]
            tt("dve", ztmp[:, 0, :], ZR[:], c_, ALU.mult, [t_Z, t_const], [t_Z])
            tt("dve", ztmp[:, 1, :], ZI[:], s_, ALU.mult, [t_Z, t_const], [t_Z])
            tt("dve", ztmp[:, 2, :], ZR[:], s_, ALU.mult, [t_Z, t_const], [t_Z])
            tt("dve", ztmp[:, 3, :], ZI[:], c_, ALU.mult, [t_Z, t_const], [t_Z])
            tt("dve", ZRi[:], ztmp[:, 0, :], ztmp[:, 1, :], ALU.subtract, [t_Z], [t_Z])
            tt("dve", ZIi[:], ztmp[:, 2, :], ztmp[:, 3, :], ALU.add, [t_Z], [t_Z])
            kb.dma("sp", Gl_d[:, 0:32], ZRi[:], reads=[t_Z], writes=[t_Gl])
            kb.dma("sp", Gl_d[:, 32:64], ZIi[:], reads=[t_Z], writes=[t_Gl])
            kb.collective("AllGather", GROUPS, Gl_d, Gg_d.rearrange("r p x -> (r p) x"), reads=[t_Gl], writes=[t_Gg])
            for h_ in range(8):
                kb.collective("AllGather", GROUPS, Kc_d[h_], Kg_d[h_].rearrange("r p c -> (r p) c"), reads=[t_Kc[h_]], writes=[t_Kg])
            for q_ in range(NT):
                kb.collective("AllGather", GROUPS, Vf_d[q_].rearrange("p k d -> p (k d)"), Vgf_d[q_].rearrange("r p k d -> (r p) (k d)"),
                              reads=[t_Vc], writes=[t_Vg])
            kb.collective("AllGather", GROUPS, KR_d, KRg_d.rearrange("r p c -> (r p) c"), reads=[t_KR], writes=[t_KRg])
        kb.mute = m2

        chk("a2_%d" % ti)
        kb.barrier()
        nk = pos0 + Wt
        nkt = 1 if ti == 0 else 1 + 4 * ti
        ar.reset(q_end)
        NKA = NTOK + 3 * NFR
        KRs = ar.get([128, NKA], BF16)
        KBf = [ar.get([128, NTOK], BF16) for _ in range(2)]
        VBf = [ar.get([128, NKT, 128], BF16) for _ in range(2)]
        PT = [X[:, 16 + i, :] for i in range(4)]
        rec = ar.get([128, W])
        sacc = [ar.get([128, W]) for _ in range(2)]
        t_sacc = [Tok(), Tok()]
        t_KRs, t_rec = Tok(), Tok()
        t_KBf, t_VBf = [Tok(), Tok()], [Tok(), Tok()]
        t_PT = [Tok() for _ in range(4)]
        kb.dma("sp", KRs[:, 0:nk], KR_d[:, 0:nk], reads=[t_KR], writes=[t_KRs])
        if ti > 0:
            kb.dma("sp", KRs[:, NTOK:NKA].rearrange("p (r c) -> p r c", r=3), KRg_d[0:3, :, NMETA:NTOK].rearrange("r p c -> p r c"),
                   reads=[t_KRg], writes=[t_KRs])
        steps = []
        for h in range(8):
            steps.append((h, -1))
            if ti > 0:
                for r in range(3):
                    steps.append((h, r))

        def seg_load(si):
            h, r = steps[si]
            kbuf, vbuf, tk, tv = KBf[si % 2], VBf[si % 2], t_KBf[si % 2], t_VBf[si % 2]
            if r < 0:
                kb.dma("sp", kbuf[:, 0:nk], Kc_d[h][:, 0:nk], reads=[t_Kc[h]], writes=[tk])
                kb.dma("sp", vbuf[:, 0, :], Vm_d[:, h * 128:(h + 1) * 128], reads=[t_Vc], writes=[tv])
                for q_ in range(ti):
                    kb.dma("sp", vbuf[:, 1 + 4 * q_:5 + 4 * q_, :], Vf_d[q_][:, :, h * 128:(h + 1) * 128], reads=[t_Vc], writes=[tv])
            else:
                kb.dma("sp", kbuf[:, 0:NFR], Kg_d[h][r, :, NMETA:NTOK], reads=[t_Kg], writes=[tk])
                for q_ in range(NT):
                    kb.dma("sp", vbuf[:, 4 * q_:4 * q_ + 4, :], Vgf_d[q_][r, :, :, h * 128:(h + 1) * 128], reads=[t_Vg], writes=[tv])

        pti = 0

        def seg_compute(si):
            nonlocal_pti = pti_box
            h, r = steps[si]
            hp, jh = h // 2, h % 2
            kbuf, vbuf, tk, tv = KBf[si % 2], VBf[si % 2], t_KBf[si % 2], t_VBf[si % 2]
            ob, sbk = (4, 5) if h % 2 == 0 else (6, 7)
            qn_ = qT[:, hp * 3 + jh, 0:Wt]
            qr_ = qT[jh * 64:(jh + 1) * 64, hp * 3 + 2, 0:Wt]
            first_seg = (r < 0)
            last_seg = (ti == 0) or (r == 2)
            tiles = []
            if r < 0:
                tiles.append((0, 0, 0, NMETA, 0, None, None))
                for kt in range(1, nkt):
                    kc0 = NMETA + 128 * (kt - 1)
                    rr = kt - (1 + 4 * (ti - 1))
                    if rr < 0:
                        tiles.append((kt, kc0, kc0, 128, 0, None, None))
                    else:
                        tiles.append((kt, kc0, kc0, 128, 128 * rr, (64, 128 * rr), None))
            else:
                for k in range(NPK):
                    tiles.append((k, 128 * k, NTOK + r * NFR + 128 * k, 128, 0, None, r))
            nmm = len(tiles)
            slot0 = nonlocal_pti[0]
            nonlocal_pti[0] += nmm
            if first_seg:
                kb.op("pool", lambda: P.memset(sacc[0][:, 0:Wt], 0.0), writes=[t_sacc[0]])
                kb.op("dve", lambda: V.memset(sacc[1][:, 0:Wt], 0.0), writes=[t_sacc[1]])

            def qk_exp(im):
                (vt, kc0, krc, nkeys, c0, zero, bcol) = tiles[im]
                sl_ = (slot0 + im) % 4
                p_, tp_ = PT[sl_], t_PT[sl_]
                kb.op("pe", lambda: PE.matmul(banks[sl_][0:nkeys, 0:Wt], lhsT=kbuf[:, kc0:kc0 + nkeys], rhs=qn_, start=True, stop=False),
                      reads=[tk, t_qT], writes=[t_bank[sl_]])
                kb.op("pe", lambda: PE.matmul(banks[sl_][0:nkeys, 0:Wt], lhsT=KRs[jh * 64:(jh + 1) * 64, krc:krc + nkeys], rhs=qr_,
                                              start=False, stop=True),
                      reads=[t_KRs, t_qT], writes=[t_bank[sl_]])
                if bcol is None:
                    act(p_[0:nkeys, 0:Wt], banks[sl_][0:nkeys, 0:Wt], AF.Exp, [t_bank[sl_]], [tp_])
                else:
                    act(p_[0:nkeys, 0:Wt], banks[sl_][0:nkeys, 0:Wt], AF.Exp, [t_bank[sl_], t_const], [tp_], bias=segb[0:nkeys, bcol:bcol + 1])
                if zero is not None:
                    kb.op("pool", lambda: P.memset(p_[zero[0]:zero[0] + 64, zero[1]:zero[1] + 64], 0.0), reads=[], writes=[tp_])

            def pv_sum(im):
                (vt, kc0, krc, nkeys, c0, zero, bcol) = tiles[im]
                sl_ = (slot0 + im) % 4
                p_, tp_ = PT[sl_], t_PT[sl_]
                st_ = first_seg and im == 0
                sp_ = last_seg and im == nmm - 1
                kb.op("pe", lambda: PE.matmul(banks[ob][:, c0:Wt], lhsT=vbuf[0:nkeys, vt, :], rhs=p_[0:nkeys, c0:Wt], start=st_, stop=sp_),
                      reads=[tv, tp_], writes=[t_bank[ob]])
                a_ = (slot0 + im) % 2
                tt("pool" if a_ == 0 else "dve", sacc[a_][0:nkeys, c0:Wt], sacc[a_][0:nkeys, c0:Wt], p_[0:nkeys, c0:Wt], ALU.add,
                   [tp_, t_sacc[a_]], [t_sacc[a_]])
            LAG = 3
            for im in range(nmm + LAG):
                if im < nmm:
                    qk_exp(im)
                if im - LAG >= 0:
                    pv_sum(im - LAG)
            if last_seg:
                kb.op("pe", lambda: PE.matmul(banks[sbk][:, 0:Wt], lhsT=ones32, rhs=sacc[0][:, 0:Wt], start=True, stop=False),
                      reads=[t_const, t_sacc[0]], writes=[t_bank[sbk]])
                kb.op("pe", lambda: PE.matmul(banks[sbk][:, 0:Wt], lhsT=ones32, rhs=sacc[1][:, 0:Wt], start=False, stop=True),
                      reads=[t_const, t_sacc[1]], writes=[t_bank[sbk]])
                kb.op("dve", lambda: V.reciprocal(out=rec[:, 0:Wt], in_=banks[sbk][:, 0:Wt]), reads=[t_bank[sbk]], writes=[t_rec])
                tt("dve", mixT[:, 8 + h, 0:Wt], banks[ob][:, 0:Wt], rec[:, 0:Wt], ALU.mult, [t_bank[ob], t_rec], [t_X])

        pti_box = [0]
        seg_load(0)
        for si in range(len(steps)):
            if si + 1 < len(steps):
                seg_load(si + 1)
            seg_compute(si)

        chk("attn%d" % ti)
        kb.barrier()

        def epi_res(mb, b):
            stt("dve", hT[:, mb, 0:Wt], hT[:, mb, 0:Wt], ALPHA, banks[b][:, 0:Wt], ALU.mult, ALU.add, [t_hT, t_bank[b]], [t_hT])
        linear("w0out", lambda kc: mixT[:, kc, 0:Wt], [t_X], Wt, [0, 1, 2, 3], epi_res)
        ar.reset()
        wk = [ar.get([128, W]) for _ in range(5)]
        layer_norm(Wt, 0, 1, 4, 5, wk)
        kb.barrier()
        ffn(Wt, "w0g", "w0u", "w0d")
        kb.barrier()
        ar.reset()
        wk = [ar.get([128, W]) for _ in range(5)]
        layer_norm(Wt, 2, 3, 4, 5, wk)
        dump("d_l0", pos0, Wt)
        kb.dma("sp", H0_d[:, pos0:pos0 + Wt].rearrange("(c p) w -> p c w", p=128), hT[:, :, 0:Wt], reads=[t_hT], writes=[t_H0])
        if ti == NT:
            Hl = ar.get([128, 48, 4])
            t_Hl_s = Tok()
            for g_ in range(8):
                for r_ in (0, 1, 2, 3, 8, 9):
                    ch = g_ * 4 + r_ if r_ < 4 else 32 + (r_ - 8) * 8 + g_
                    sl, tsl = load_slab("w1in", 1 + g_ * 10 + r_, 2048)
                    b = ch % 2
                    for kc in range(16):
                        kb.op("pe", lambda: PE.matmul(banks[b][:, 0:4], lhsT=sl[:, kc * 128:(kc + 1) * 128], rhs=hb[:, kc, Wt - 4:Wt],
                                                      start=(kc == 0), stop=(kc == 15)),
                              reads=[tsl, t_hb], writes=[t_bank[b]])
                    cp("act", Hl[:, ch, :], banks[b][:, 0:4], [t_bank[b]], [t_Hl_s])
            kb.dma("sp", Hl_d, Hl[:].rearrange("p c k -> p (c k)"), reads=[t_Hl_s], writes=[t_Hl])
            kb.collective("AllGather", GROUPS, Hl_d, Hg_d.rearrange("r p x -> (r p) x"), reads=[t_Hl], writes=[t_Hg])
        chk("l0_%d" % ti)

        kb.mute = m34
        kb.barrier()
        ar.reset()
        ynT = X
        kb.dma("sp", hT[:, :, 0:Wt], H0_d[:, pos0:pos0 + Wt].rearrange("(c p) w -> p c w", p=128), reads=[t_H0], writes=[t_hT])
        cp("dve", hb[:, 0:8, 0:Wt], hT[:, 0:8, 0:Wt], [t_hT], [t_hb])
        cp("pool", hb[:, 8:16, 0:Wt], hT[:, 8:16, 0:Wt], [t_hT], [t_hb])
        if ti == 1:
            ts("dve", halo[:], halo[:], sel[:, 0:1], None, ALU.mult, None, [t_halo, t_const], [t_halo])
            tt("dve", halo[:, :, 0:3], halo[:, :, 0:3], halo_in[:, :, 1:4], ALU.add, [t_halo, t_hin], [t_halo])
            ts("dve", ST[:], ST[:], sel[:, 0:1], None, ALU.mult, None, [t_ST, t_const], [t_ST])
            kb.mute = m4
            Sin_ = ar.get([128, 64, 64])
            t_sin = Tok()
            kb.dma("sp", Sin_[:].rearrange("p h q -> p (h q)"), STin_d, reads=[t_STin], writes=[t_sin])
            tt("dve", ST[:], ST[:], Sin_[:], ALU.add, [t_ST, t_sin], [t_ST])
            kb.mute = m34
            cp("act", STb[:], ST[:], [t_ST], [t_STb])
            kb.barrier()
            ar.reset()
        dtT = ar.get([128, W])
        daT = ar.get([128, W])
        dt_tok = ar.get([128, 4, 64])
        da_tok = ar.get([128, 4, 64])
        cs_tok = ar.get([128, 4, 64])
        wg_tok = ar.get([128, 4, 64])
        ece = ar.get([128, 4, 64])
        t_dt = Tok()
        sl, tsl = load_slab("w1in", 0, 2048)
        for kc in range(16):
            kb.op("pe", lambda: PE.matmul(banks[0][:, 0:Wt], lhsT=sl[:, kc * 128:(kc + 1) * 128], rhs=hb[:, kc, 0:Wt], start=(kc == 0), stop=(kc == 15)),
                  reads=[tsl, t_hb], writes=[t_bank[0]])
        act(dtT[0:64, 0:Wt], banks[0][0:64, 0:Wt], AF.Exp, [t_bank[0], t_const], [t_dt], bias=dtb[0:64, 0:1])
        act(dtT[0:64, 0:Wt], dtT[0:64, 0:Wt], AF.Ln, [t_dt], [t_dt], bias=1.0)
        ts("dve", daT[0:64, 0:Wt], dtT[0:64, 0:Wt], aneg[0:64, 0:1], None, ALU.mult, None, [t_dt, t_const], [t_dt])
        for sj in range(nsub):
            L = min(128, Wt - sj * 128)
            cs_ = slice(sj * 128, sj * 128 + L)
            kb.op("pe", lambda: PE.transpose(banks[2][0:L, 0:64], dtT[0:64, cs_], ident[0:64, 0:64]), reads=[t_dt, t_const], writes=[t_bank[2]])
            kb.op("pe", lambda: PE.transpose(banks[2][0:L, 64:128], daT[0:64, cs_], ident[0:64, 0:64]), reads=[t_dt, t_const], writes=[t_bank[2]])
            cp("act", dt_tok[0:L, sj, :], banks[2][0:L, 0:64], [t_bank[2]], [t_dt])
            cp("dve", da_tok[0:L, sj, :], banks[2][0:L, 64:128], [t_bank[2]], [t_dt])
            kb.op("pe", lambda: PE.matmul(banks[3][0:L, 0:64], lhsT=U2[0:L, 0:L], rhs=da_tok[0:L, sj, :], start=True, stop=True),
                  reads=[t_dt, t_const], writes=[t_bank[3]])
            kb.op("pe", lambda: PE.matmul(banks[3][:, 64:128], lhsT=ones32[0:L, :], rhs=da_tok[0:L, sj, :], start=True, stop=True),
                  reads=[t_dt, t_const], writes=[t_bank[3]])
            cp("act", cs_tok[0:L, sj, :], banks[3][0:L, 0:64], [t_bank[3]], [t_dt])
            act(ece[:, sj, :], banks[3][:, 64:128], AF.Exp, [t_bank[3]], [t_dt])
            if ti > 0:
                kb.mute = m3
                tt("dve", dsum[:], dsum[:], banks[3][:, 64:128], ALU.add, [t_dsum, t_bank[3]], [t_dsum])
                kb.mute = m34
            tt("dve", wg_tok[0:L, sj, :], banks[3][0:L, 64:128], cs_tok[0:L, sj, :], ALU.subtract, [t_bank[3], t_dt], [t_dt])
            act(wg_tok[0:L, sj, :], wg_tok[0:L, sj, :], AF.Exp, [t_dt], [t_dt])
            tt("dve", wg_tok[0:L, sj, :], wg_tok[0:L, sj, :], dt_tok[0:L, sj, :], ALU.mult, [t_dt], [t_dt])

        chk("l1dt%d" % ti)
        xg = ar.get([128, 4, W])
        xgb = ar.get([128, 4, W], BF16)
        szg = ar.get([128, 4, W])
        yvg = ar.get([128, 4, W])
        BCf = ar.get([128, 2, W])
        BCb = ar.get([128, 2, W], BF16)
        xin = [ar.get([128, W + 4]) for _ in range(2)]
        cacc = [ar.get([128, W]) for _ in range(2)]
        xdt = ar.get([128, 512], BF16)
        xwg = ar.get([128, 512], BF16)
        Btok = ar.get([128, 128], BF16)
        CC = ar.get([128, 256])
        Lh = [ar.get([128, 256]) for _ in range(2)]
        Eh = [ar.get([128, 256]) for _ in range(2)]
        MC = [ar.get([128, 256], BF16) for _ in range(2)]
        sqg = ar.get([128, W])
        rsg = ar.get([128, W])
        t_xg, t_xgb, t_szg, t_yvg, t_BC = Tok(), Tok(), Tok(), Tok(), Tok()
        t_xin, t_cacc = [Tok(), Tok()], [Tok(), Tok()]
        t_xdt, t_xwg, t_Btok, t_CC = Tok(), Tok(), Tok(), Tok()
        t_Lh, t_Eh, t_MC = [Tok(), Tok()], [Tok(), Tok()], [Tok(), Tok()]
        t_sqg = Tok()
        cvi = [0]

        def conv_silu(b, ch, outs):
            i = cvi[0] % 2
            cvi[0] += 1
            xi, txi, ca, tca = xin[i], t_xin[i], cacc[i], t_cacc[i]
            cp("pool", xi[:, 0:3], halo[:, ch, 0:3], [t_halo], [txi])
            cp("act", xi[:, 3:3 + Wt], banks[b][:, 0:Wt], [t_bank[b]], [txi])
            cp("pool", halo[:, ch, 0:3], xi[:, Wt:Wt + 3], [txi], [t_halo])
            ts("dve", ca[:, 0:Wt], xi[:, 0:Wt], cw[:, ch, 0:1], None, ALU.mult, None, [txi, t_const], [tca])
            for k in range(1, 4):
                stt("dve" if k < 3 else "pool", ca[:, 0:Wt], xi[:, k:k + Wt], cw[:, ch, k:k + 1], ca[:, 0:Wt], ALU.mult, ALU.add,
                    [txi, t_const], [tca])
            for (o, to) in outs[:1]:
                act(o, ca[:, 0:Wt], AF.Silu, [tca, t_const], [to], bias=cb[:, ch:ch + 1])
            for (o, to) in outs[1:]:
                cp("pool", o, outs[0][0], [outs[0][1]], [to])

        for g in range(8):
            base = 1 + g * 10
            for r in range(10):
                mb = base + r
                kb.mute = m34 or (phase == 3 and 4 <= r < 8)
                sl, tsl = load_slab("w1in", mb, 2048)
                b = r % 2
                for kc in range(16):
                    kb.op("pe", lambda: PE.matmul(banks[b][:, 0:Wt], lhsT=sl[:, kc * 128:(kc + 1) * 128], rhs=hb[:, kc, 0:Wt],
                                                  start=(kc == 0), stop=(kc == 15)),
                          reads=[tsl, t_hb], writes=[t_bank[b]])
                if r < 4:
                    conv_silu(b, g * 4 + r, [(xg[:, r, 0:Wt], t_xg), (xgb[:, r, 0:Wt], t_xgb)])
                elif r < 8:
                    act(szg[:, r - 4, 0:Wt], banks[b][:, 0:Wt], AF.Silu, [t_bank[b]], [t_szg])
                else:
                    conv_silu(b, 32 + (r - 8) * 8 + g, [(BCf[:, r - 8, 0:Wt], t_BC), (BCb[:, r - 8, 0:Wt], t_BC)])
            kb.mute = m34
            if g == 0:
                chk("l1conv%d" % ti)
            for sj in range(nsub):
                L = min(128, Wt - sj * 128)
                cs_ = slice(sj * 128, sj * 128 + L)
                trb = banks[2].bitcast(BF16) if hasattr(banks[2], "bitcast") else None
                for c in range(4):
                    kb.op("pe", lambda: PE.transpose(trb[0:L, c * 128:(c + 1) * 128], xgb[:, c, cs_], identb), reads=[t_xgb, t_const], writes=[t_bank[2]])
                kb.op("pe", lambda: PE.transpose(trb[0:L, 512:640], BCb[:, 0, cs_], identb), reads=[t_BC, t_const], writes=[t_bank[2]])
                if g == 0 and sj == 0:
                    chk("ssd_a%d" % ti)
                kb.mute = m4
                tt("dve", xdt[0:L, :].rearrange("p (h q) -> p h q", h=8), trb[0:L, 0:512].rearrange("p (h q) -> p h q", h=8),
                   dt_tok[0:L, sj, g * 8:g * 8 + 8].unsqueeze(2).broadcast_to([L, 8, 64]), ALU.mult, [t_bank[2], t_dt], [t_xdt])
                kb.mute = m34
                tt("dve", xwg[0:L, :].rearrange("p (h q) -> p h q", h=8), trb[0:L, 0:512].rearrange("p (h q) -> p h q", h=8),
                   wg_tok[0:L, sj, g * 8:g * 8 + 8].unsqueeze(2).broadcast_to([L, 8, 64]), ALU.mult, [t_bank[2], t_dt], [t_xwg])
                cp("act", Btok[0:L, :], trb[0:L, 512:640], [t_bank[2]], [t_Btok])
                if g == 0 and sj == 0:
                    chk("ssd_b%d" % ti)
                kb.mute = m4
                kb.op("pe", lambda: PE.matmul(banks[3][0:L, 0:L], lhsT=BCb[:, 0, cs_], rhs=BCb[:, 1, cs_], start=True, stop=True),
                      reads=[t_BC], writes=[t_bank[3]])
                tt("dve", CC[0:L, 0:L], banks[3][0:L, 0:L], U2[0:L, 0:L], ALU.mult, [t_bank[3], t_const], [t_CC])
                cp("pool", CC[:, 128:128 + L], BCf[:, 1, cs_], [t_BC], [t_CC])
                if g == 0 and sj == 0:
                    chk("ssd_c%d" % ti)
                def st_a1(hh):
                    h = g * 8 + hh
                    i2 = hh % 2
                    lh = Lh[i2]
                    sgb = 4 if hh % 2 == 0 else 5
                    ts("dve", lh[0:L, :], U1o[0:L, :], da_tok[0:L, sj, h:h + 1], None, ALU.mult, None, [t_const, t_dt], [t_Lh[i2]])
                    kb.op("pe", lambda: PE.matmul(banks[sgb][0:L, 0:L], lhsT=lh[0:L, 0:L], rhs=U2[0:L, 0:L], start=True, stop=True),
                          reads=[t_Lh[i2], t_const], writes=[t_bank[sgb]])
                    kb.op("pe", lambda: PE.matmul(banks[sgb][:, 128:128 + L], lhsT=lh[0:L, 128:256], rhs=U2[0:L, 0:L], start=True, stop=True),
                          reads=[t_Lh[i2], t_const], writes=[t_bank[sgb]])

                def st_a2(hh):
                    i2 = hh % 2
                    eh, mc = Eh[i2], MC[i2]
                    sgb = 4 if hh % 2 == 0 else 5
                    act(eh[0:L, 0:L], banks[sgb][0:L, 0:L], AF.Exp, [t_bank[sgb]], [t_Eh[i2]])
                    act(eh[:, 128:128 + L], banks[sgb][:, 128:128 + L], AF.Exp, [t_bank[sgb]], [t_Eh[i2]])
                    tt("dve", mc[0:L, 0:L], eh[0:L, 0:L], CC[0:L, 0:L], ALU.mult, [t_Eh[i2], t_CC], [t_MC[i2]])
                    tt("pool", mc[:, 128:128 + L], eh[:, 128:128 + L], CC[:, 128:128 + L], ALU.mult, [t_Eh[i2], t_CC], [t_MC[i2]])

                def st_b(hh):
                    h = g * 8 + hh
                    i2 = hh % 2
                    mc = MC[i2]
                    yb = 6 + (hh // 4)
                    ycol = (hh % 4) * 128
                    pr = (hh // 2) * 128
                    kb.op("pe", lambda: PE.matmul(banks[yb][:, ycol:ycol + L], lhsT=xdt[0:L, pr:pr + 128], rhs=mc[0:L, 0:L], start=True, stop=False),
                          reads=[t_xdt, t_MC[i2]], writes=[t_bank[yb]])
                    kb.op("pe", lambda: PE.matmul(banks[yb][:, ycol:ycol + L], lhsT=STb[:, h - (h % 2):h - (h % 2) + 2, :].rearrange("p a b -> p (a b)"),
                                                  rhs=mc[:, 128:128 + L], start=False, stop=True),
                          reads=[t_STb, t_MC[i2]], writes=[t_bank[yb]])
                for st in range(8 + 1):
                    if st < 8:
                        st_a1(st)
                    if st - 1 >= 0:
                        st_a2(st - 1)
                        st_b(st - 1)
                if g == 0 and sj == 0:
                    chk("ssd_d%d" % ti)
                for hh in range(8):
                    yb = 6 + (hh // 4)
                    ycol = (hh % 4) * 128
                    c = hh // 2
                    pr = slice((hh % 2) * 64, (hh % 2) * 64 + 64)
                    stt("dve", yvg[pr, c, cs_], xg[pr, c, cs_], l1d[pr, g * 4 + c:g * 4 + c + 1], banks[yb][pr, ycol:ycol + L], ALU.mult, ALU.add,
                        [t_xg, t_bank[yb], t_const], [t_yvg])
                if g == 0 and sj == 0:
                    chk("ssd_e%d" % ti)
                kb.mute = m34
                kb.op("pe", lambda: PE.matmul(banks[3][:, :], lhsT=Btok[0:L, :], rhs=xwg[0:L, :], start=True, stop=True),
                      reads=[t_Btok, t_xwg], writes=[t_bank[3]])
                tt("dve", ST[:, g * 8:g * 8 + 8, :], ST[:, g * 8:g * 8 + 8, :], ece[:, sj, g * 8:g * 8 + 8].unsqueeze(2).broadcast_to([128, 8, 64]),
                   ALU.mult, [t_ST, t_dt], [t_ST])
                tt("dve", ST[:, g * 8:g * 8 + 8, :], ST[:, g * 8:g * 8 + 8, :], banks[3][:, :].rearrange("p (h q) -> p h q", h=8), ALU.add,
                   [t_ST, t_bank[3]], [t_ST])
                cp("act", STb[:, g * 8:g * 8 + 8, :], ST[:, g * 8:g * 8 + 8, :], [t_ST], [t_STb])
            if g == 0:
                chk("l1ssd%d" % ti)
            kb.mute = m4
            for c in range(4):
                tt("dve" if c % 2 == 0 else "pool", yvg[:, c, 0:Wt], yvg[:, c, 0:Wt], szg[:, c, 0:Wt], ALU.mult, [t_yvg, t_szg], [t_yvg])
                act(sqg[:, 0:Wt], yvg[:, c, 0:Wt], AF.Square, [t_yvg], [t_sqg])
                kb.op("pe", lambda: PE.matmul(banks[2][:, 0:Wt], lhsT=ones32, rhs=sqg[:, 0:Wt], start=(c == 0), stop=(c == 3)),
                      reads=[t_sqg, t_const], writes=[t_bank[2]])
            ts("dve", rsg[:, 0:Wt], banks[2][:, 0:Wt], 1.0 / 512.0, RMS_EPS, ALU.mult, ALU.add, [t_bank[2]], [t_sqg])
            act(rsg[:, 0:Wt], rsg[:, 0:Wt], AF.Sqrt, [t_sqg], [t_sqg])
            kb.op("dve", lambda: V.reciprocal(out=rsg[:, 0:Wt], in_=rsg[:, 0:Wt]), reads=[t_sqg], writes=[t_sqg])
            for c in range(4):
                stt("dve" if c % 2 == 0 else "pool", ynT[:, g * 4 + c, 0:Wt], yvg[:, c, 0:Wt], ng[:, g * 4 + c:g * 4 + c + 1], rsg[:, 0:Wt],
                    ALU.mult, ALU.mult, [t_yvg, t_sqg, t_const], [t_X])
        kb.mute = m3
        if ti == NT:
            for i_ in range(2):
                kb.dma("sp", STl_d[i_], ST[:, 32 * i_:32 * i_ + 32, :].rearrange("p h q -> p (h q)"), reads=[t_ST], writes=[t_STl])
            kb.dma("sp", dsl_d, dsum[:], reads=[t_dsum], writes=[t_dsl])
            for i_ in range(2):
                kb.collective("AllGather", GROUPS, STl_d[i_], STg_d[i_].rearrange("r p x -> (r p) x"), reads=[t_STl], writes=[t_STg])
            kb.collective("AllGather", GROUPS, dsl_d, dsg_d.rearrange("r p x -> (r p) x"), reads=[t_dsl], writes=[t_dsg])
        kb.mute = m4
        chk("l1mix%d" % ti)
        linear("w1out", lambda kc: ynT[:, kc, 0:Wt], [t_X], Wt, [0, 1, 2, 3], epi_res)
        kb.barrier()
        ar.reset()
        wk = [ar.get([128, W]) for _ in range(5)]
        layer_norm(Wt, 4, 5, 4, 5, wk)
        kb.barrier()
        ffn(Wt, "w1g", "w1u", "w1d")
        kb.barrier()
        ar.reset()
        wk = [ar.get([128, W]) for _ in range(5)]
        layer_norm(Wt, 6, 7, 4, 5, wk)
        if ti > 0:
            kb.dma("sp", outT[:, W * (ti - 1):W * ti].rearrange("(c p) w -> p c w", p=128), hT[:, :, 0:Wt], reads=[t_hT], writes=[t_out])
        dump("d_l1mix", pos0, Wt)
    kb.mute = False
    kb.barrier(final=True)
    return nc


def slab(Wm):
    K, M = Wm.shape
    return np.ascontiguousarray(Wm.reshape(K // 128, 128, M // 128, 128).transpose(2, 1, 0, 3)).reshape(M // 128, 128, K)


def pvec(v):
    return np.ascontiguousarray(v.reshape(-1, 128).T)


def prepare(inp, NT):
    f = np.float32
    NTOK = NMETA + W * NT
    com = {}
    w_in = inp["l0_w_in"]
    kr = w_in[:, 1792:1856]
    krs = np.concatenate([kr[:, 32:], kr[:, :32]], axis=1)
    com["w0in"] = slab(np.concatenate([w_in[:, :1792], kr, kr, krs, krs], axis=1))
    com["wglu"] = slab(inp["l0_s5_w_glu"])
    wuq = inp["l0_mla_w_uq"].reshape(512, 8, 192)
    cols = []
    for hp in range(4):
        h0, h1 = 2 * hp, 2 * hp + 1
        r0, r1 = wuq[:, h0, 128:], wuq[:, h1, 128:]
        sw = lambda r: np.concatenate([r[:, 32:], r[:, :32]], axis=1)
        cols += [wuq[:, h0, :128], wuq[:, h1, :128], np.concatenate([r0, r1], 1), np.concatenate([sw(r0), sw(r1)], 1)]
    com["wuq"] = slab(np.concatenate(cols, axis=1))
    wukv = inp["l0_mla_w_ukv"].reshape(256, 8, 256)
    com["wuk"] = slab(np.ascontiguousarray(wukv[:, :, :128]).reshape(256, 1024))
    com["wuv"] = np.ascontiguousarray(np.ascontiguousarray(wukv[:, :, 128:]).reshape(2, 128, 1024).transpose(1, 0, 2))
    com["w0out"] = slab(inp["l0_w_out"])
    com["w0g"] = slab(inp["l0_ffn_w_gate"])
    com["w0u"] = slab(inp["l0_ffn_w_up"])
    com["w0d0"] = slab(inp["l0_ffn_w_down"][:FF // 2])
    com["w0d1"] = slab(inp["l0_ffn_w_down"][FF // 2:])
    com["w1g"] = slab(inp["l1_ffn_w_gate"])
    com["w1u"] = slab(inp["l1_ffn_w_up"])
    com["w1d0"] = slab(inp["l1_ffn_w_down"][:FF // 2])
    com["w1d1"] = slab(inp["l1_ffn_w_down"][FF // 2:])
    w1 = inp["l1_w_in"]
    z, xs, Bm, Cm, dtc = w1[:, :4096], w1[:, 4096:8192], w1[:, 8192:9216], w1[:, 9216:10240], w1[:, 10240:]
    cols = [np.concatenate([dtc, np.zeros((2048, 64), f)], 1)]
    for g in range(8):
        cols += [xs[:, g * 512:(g + 1) * 512], z[:, g * 512:(g + 1) * 512], Bm[:, g * 128:(g + 1) * 128], Cm[:, g * 128:(g + 1) * 128]]
    com["w1in"] = slab(np.concatenate(cols, axis=1))
    com["w1out"] = slab(inp["l1_w_out"])
    com["vec"] = np.ascontiguousarray(np.stack([pvec(inp[k]) for k in (
        "l0_ln1_g", "l0_ln1_b", "l0_ln2_g", "l0_ln2_b", "l1_ln1_g", "l1_ln1_b", "l1_ln2_g", "l1_ln2_b")], axis=1))
    com["s5d"] = pvec(inp["l0_s5_d"])
    com["qn"] = pvec(inp["l0_mla_q_norm"])
    com["kvn"] = pvec(inp["l0_mla_kv_norm"])
    cwm = inp["l1_conv_w"]
    com["cw"] = np.ascontiguousarray(cwm.T.reshape(48, 128, 4).transpose(1, 0, 2))
    com["cb"] = pvec(inp["l1_conv_b"])
    dtb = np.zeros((128, 2), f)
    dtb[:64, 0] = inp["l1_dt_bias"]
    dtb[:64, 1] = inp["l1_a_log"]
    com["dtb"] = dtb
    com["l1d"] = pvec(np.repeat(inp["l1_d"], 64))
    com["ng"] = pvec(inp["l1_norm_g"])
    ldt = inp["l0_s5_log_dt"]
    are, aim = inp["l0_s5_a_re"], inp["l0_s5_a_im"]
    pl = lambda a: np.ascontiguousarray(a.reshape(32, 128).T)
    com["s5p"] = np.ascontiguousarray(np.stack([pl(np.repeat(ldt[:, None], 64, 1)), pl(are), pl(aim)], axis=1))
    row = lambda a: np.ascontiguousarray(np.broadcast_to(a.reshape(1, 32, 128), (128, 32, 128)))
    com["s5r"] = np.stack([row(np.repeat(ldt[:, None], 64, 1)), row(are), row(aim)], axis=0)
    bl = np.zeros((2, 128, 32, 128), f)
    cl = np.zeros((32, 128, 2, 128), f)
    for g in range(64):
        j, a = g // 2, g % 2
        q = j % 4
        r0 = 32 * q + 16 * a
        for c_, (bsrc, csrc) in enumerate(((inp["l0_s5_b_re"], inp["l0_s5_c_re"]), (inp["l0_s5_b_im"], inp["l0_s5_c_im"]))):
            bl[c_, r0:r0 + 16, j, 64 * a:64 * a + 64] = bsrc[g].T
            cl[j, 64 * a:64 * a + 64, c_, r0:r0 + 16] = csrc[g].T
    com["s5b"] = bl
    com["s5c"] = cl
    k = np.arange(128)
    cst = np.zeros((128, 640), f)
    cst[:, 0:128] = np.eye(128)
    cst[:, 128:256] = (k[:, None] > k[None, :])
    cst[:, 256:384] = 1.0
    cst[:, 384:512] = (k[:, None] <= k[None, :])
    cst[:, 512:640] = 1.0
    com["cst"] = cst
    com["iota"] = np.ascontiguousarray(np.broadcast_to(np.arange(W, dtype=f)[None], (128, W)))
    com = {k_: np.ascontiguousarray(v, dtype=f) for k_, v in com.items()}
    NFR = W * NT
    inv = (10000.0 ** (-np.arange(0, 64, 2, dtype=f) / 64)).astype(f)
    maps = []
    for b in range(inp["x"].shape[0]):
        for c in range(4):
            m = dict(com)
            m["xT"] = np.ascontiguousarray(np.concatenate([inp["meta_tokens"].T, inp["x"][b, NFR * c:NFR * (c + 1)].T], axis=1), dtype=f)
            pos = np.concatenate([np.arange(NMETA), NMETA + NFR * c + np.arange(NFR)]).astype(f)
            ang = (pos[None, :] * inv[:, None]).astype(f)
            c32, s32 = np.cos(ang).astype(f), np.sin(ang).astype(f)
            m["ropeC"] = np.ascontiguousarray(np.concatenate([c32, c32, c32, c32], 0))
            m["ropeS"] = np.ascontiguousarray(np.concatenate([-s32, s32, -s32, s32], 0))
            sel = np.zeros((128, 8), f)
            sel[:, c] = 1.0
            m["sel"] = sel
            segb = np.zeros((128, 4), f)
            segb[:, c:] = -30000.0
            m["segb"] = segb
            maps.append(m)
    return maps


def kernel(**inputs):
    inp = {k: np.asarray(v) for k, v in inputs.items()}
    NT = SEQ // W // 4
    nc = build(NT)
    maps = prepare(inp, NT)
    res = run_bass_kernel_spmd(nc, maps, core_ids=list(range(8)))
    out = np.stack([np.concatenate([np.ascontiguousarray(res.results[b * 4 + c]["outT"].T) for c in range(4)], axis=0)
                    for b in range(inp["x"].shape[0])], axis=0)
    return out.astype(np.float32)
```

```python
import math
import numpy as np
import concourse.bass as bass
import concourse.mybir as mybir
from concourse.bass_utils import run_bass_kernel_spmd

F32 = mybir.dt.float32
BF16 = mybir.dt.bfloat16
I32 = mybir.dt.int32
AF = mybir.ActivationFunctionType
ALU = mybir.AluOpType

D = 2048
NMETA = 16
SEQ = 8192
W = 512
FF = 5632
FC = FF // 128
ALPHA = 4.0 ** 0.25
LN_EPS = 1e-5
RMS_EPS = 1e-6
ATT_SCALE = 192.0 ** -0.5
TWO_PI = 6.283185
GELU_C = 1.5957691216057308


class Tok:
    __slots__ = ("w", "r", "excl")

    def __init__(self, excl=False):
        self.w = None
        self.r = {}
        self.excl = excl


class KB:
    NRING = 8

    def __init__(self, nc):
        self.nc = nc
        self.E = {"pe": nc.tensor, "act": nc.scalar, "dve": nc.vector, "pool": nc.gpsimd, "sp": nc.sync}
        self.sems = {}
        self.cnt = {}
        for e in ("pe", "act", "dve", "pool"):
            self.sems[e] = nc.alloc_semaphore("s_" + e)
            self.cnt[e] = 0
        self.seen = {e: {} for e in self.E}
        self.ring = {}
        self.ring_i = {}
        for q in ("sp", "pool"):
            ks = []
            for i in range(self.NRING):
                k = "d_%s_%d" % (q, i)
                self.sems[k] = nc.alloc_semaphore(k)
                self.cnt[k] = 0
                ks.append(k)
            self.ring[q] = ks
            self.ring_i[q] = 0
        self.ninst = 0
        self.dead = False
        self.mute = False
        self.ncc = 0

    def _wait(self, eng, key, val):
        if val <= 0:
            return
        if key == eng and eng == "pe":
            return
        s = self.seen[eng]
        if s.get(key, 0) >= val:
            return
        s[key] = val
        self.E[eng].wait_ge(self.sems[key], val)

    def _deps(self, eng, reads, writes):
        for t in reads:
            if t.w is not None:
                self._wait(eng, t.w[0], t.w[1])
            if t.excl:
                for k, v in t.r.items():
                    if k != eng:
                        self._wait(eng, k, v)
        for t in writes:
            if t.w is not None:
                self._wait(eng, t.w[0], t.w[1])
            for k, v in t.r.items():
                if k == eng:
                    continue
                self._wait(eng, k, v)

    def _mark(self, ev, reads, writes):
        for t in reads:
            if t.r.get(ev[0], 0) < ev[1]:
                t.r[ev[0]] = ev[1]
        for t in writes:
            t.w = ev
            t.r = {}

    def op(self, eng, fn, reads=(), writes=()):
        if self.dead or self.mute:
            return None
        self._deps(eng, reads, writes)
        inst = fn()
        self.cnt[eng] += 1
        inst.then_inc(self.sems[eng], 1)
        self._mark((eng, self.cnt[eng]), reads, writes)
        self.ninst += 1
        return inst

    def dma(self, q, out, in_, reads=(), writes=()):
        if self.dead or self.mute:
            return None
        i = self.ring_i[q]
        self.ring_i[q] = i + 1
        key = self.ring[q][i % self.NRING]
        self._wait(q, key, self.cnt[key])
        self._deps(q, reads, writes)
        inst = self.E[q].dma_start(out=out, in_=in_)
        self.cnt[key] += 16
        inst.then_inc(self.sems[key], 16)
        self._mark((key, self.cnt[key]), reads, writes)
        self.ninst += 1
        return inst

    def collective(self, kind, groups, in_ap, out_ap, reads=(), writes=()):
        if self.dead or self.mute:
            return None
        key = "cc%d" % self.ncc
        self.ncc += 1
        self.sems[key] = self.nc.alloc_semaphore(key)
        self.cnt[key] = 0
        self._deps("pool", reads, writes)
        inst = self.nc.gpsimd.collective_compute(kind, ALU.bypass, replica_groups=groups, ins=[in_ap], outs=[out_ap])
        self.cnt[key] = 1
        inst.then_inc(self.sems[key], 1)
        self._mark((key, 1), reads, writes)
        return inst

    def barrier(self, final=False):
        if self.dead or self.mute:
            return
        for e in ("pe", "act", "dve", "pool", "sp"):
            for k in self.sems:
                if k.startswith("d_pool") and not final:
                    continue
                self._wait(e, k, self.cnt[k])


def build(NT, dbg=False, stop=None):
    NTOK = NMETA + W * NT
    NKT = 1 + 4 * NT
    NFR = W * NT
    NPK = 4 * NT
    GROUPS = [[0, 1, 2, 3], [4, 5, 6, 7]]
    nc = bass.Bass("TRN2", target_bir_lowering=False)
    kb = KB(nc)
    V = nc.vector
    A = nc.scalar
    P = nc.gpsimd
    PE = nc.tensor

    def chk(name):
        if stop == name and not kb.dead:
            kb.barrier()
            kb.dead = True
            print("STOP at", name, "ninst", kb.ninst)

    def din(name, shape, dt=F32):
        return nc.dram_tensor(name, list(shape), dt, kind="ExternalInput").ap()

    def dscr(name, shape, dt=BF16):
        return nc.dram_tensor(name, list(shape), dt, kind="Internal").ap()

    xT = din("xT", [D, NTOK])
    outT = nc.dram_tensor("outT", [D, W * NT], F32, kind="ExternalOutput").ap()
    ropeC_d = din("ropeC", [128, NTOK])
    ropeS_d = din("ropeS", [128, NTOK])
    cst_d = din("cst", [128, 5 * 128])
    iota_d = din("iota", [128, W])
    WSPEC = {
        "w0in": (16, 2048), "wglu": (8, 1024), "wuq": (16, 512), "wuk": (8, 256),
        "w0out": (16, 2048), "w0g": (FC, 2048), "w0u": (FC, 2048), "w0d0": (16, FF // 2), "w0d1": (16, FF // 2),
        "w1in": (81, 2048), "w1out": (16, 4096), "w1g": (FC, 2048), "w1u": (FC, 2048), "w1d0": (16, FF // 2), "w1d1": (16, FF // 2),
    }
    wf = {}
    wbf = {}
    for n, (mb, k) in WSPEC.items():
        wf[n] = din(n, [mb, 128, k])
        wbf[n] = dscr(n + "_b", [mb, 128, k])
    wuv_d = din("wuv", [128, 2, 1024])
    vec_d = din("vec", [128, 8, 16])
    s5d_d = din("s5d", [128, 8])
    qn_d = din("qn", [128, 4])
    kvn_d = din("kvn", [128, 2])
    cw_d = din("cw", [128, 48, 4])
    cb_d = din("cb", [128, 48])
    dtb_d = din("dtb", [128, 2])
    l1d_d = din("l1d", [128, 32])
    ng_d = din("ng", [128, 32])
    s5p_d = din("s5p", [128, 3, 32])
    s5r_d = din("s5r", [3, 128, 32, 128])
    s5b_d = din("s5b", [2, 128, 32, 128])
    s5c_d = din("s5c", [32, 128, 2, 128])
    BL_d = dscr("BL", [32, 128, 2, 128])
    CL_d = dscr("CL", [32, 128, 3, 128])
    TAB_d = dscr("TAB", [32, 128, 2, W], F32)
    Kc_d = [dscr("Kc%d" % h, [128, NTOK]) for h in range(8)]
    Kg_d = [dscr("Kg%d" % h, [4, 128, NTOK]) for h in range(8)]
    Vm_d = dscr("Vm", [128, 1024])
    Vf_d = [dscr("Vf%d" % q, [128, 4, 1024]) for q in range(NT)]
    Vgf_d = [dscr("Vgf%d" % q, [4, 128, 4, 1024]) for q in range(NT)]
    KR_d = dscr("KR", [128, NTOK])
    KRg_d = dscr("KRg", [4, 128, NTOK])
    Gl_d = dscr("Gl", [128, 64], F32)
    Gg_d = dscr("Gg", [4, 128, 64], F32)
    Hl_d = dscr("Hl", [128, 192], F32)
    Hg_d = dscr("Hg", [4, 128, 192], F32)
    STl_d = [dscr("STl%d" % i, [128, 2048], F32) for i in range(2)]
    STg_d = [dscr("STg%d" % i, [4, 128, 2048], F32) for i in range(2)]
    STin_d = dscr("STin", [128, 4096], F32)
    dsl_d = dscr("dsl", [128, 64], F32)
    dsg_d = dscr("dsg", [4, 128, 64], F32)
    H0_d = dscr("H0", [D, NTOK], F32)
    sel_d = din("sel", [128, 8])
    segb_d = din("segb", [128, 4])
    dbg_d = {}
    if dbg:
        for n in ("d_l0", "d_mix", "d_l0a", "d_l1mix"):
            dbg_d[n] = nc.dram_tensor(n, [D, NTOK], F32, kind="ExternalOutput").ap()

    def sb(name, shape, dt=F32):
        return nc.alloc_sbuf_tensor("sb_" + name, list(shape), dt)

    hT = sb("hT", [128, 16, W])
    hb = sb("hb", [128, 16, W], BF16)
    X = sb("X", [128, 32, W], BF16)
    NSLAB = 3
    slabs = [sb("slab%d" % i, [128, 4096], BF16) for i in range(NSLAB)]
    t_slab = [Tok() for _ in range(NSLAB)]
    slab_i = [0]
    ST = sb("ST", [128, 64, 64])
    STb = sb("STb", [128, 64, 64], BF16)
    cst = sb("cst32", [128, 5 * 128])
    cstb = sb("cstb", [128, 5 * 128], BF16)
    iota = sb("iota", [128, W])
    wuv = sb("wuv", [128, 2, 1024], BF16)
    vec = sb("vec", [128, 8, 16])
    s5d = sb("s5d", [128, 8])
    qn = sb("qn", [128, 4])
    kvn_g = sb("kvn_g", [128, 2])
    cw = sb("cw", [128, 48, 4])
    cb = sb("cb", [128, 48])
    dtb = sb("dtb", [128, 2])
    aneg = sb("aneg", [128, 1])
    l1d = sb("l1d", [128, 32])
    ng = sb("ng", [128, 32])
    halo = sb("halo", [128, 48, 4])
    halo_in = sb("halo_in", [128, 48, 4])
    Gin = sb("Gin", [128, 2, 32])
    ANc = sb("ANc", [128, 2, 32])
    sel = sb("sel", [128, 8])
    segb = sb("segb", [128, 4])
    dsum = sb("dsum", [128, 64])
    s5R = sb("s5R", [128, 32])
    s5F = sb("s5F", [128, 32])
    rotc = sb("rotc", [128, 2, 32])
    rots = sb("rots", [128, 2, 32])
    ZR = sb("ZR", [128, 32])
    ZI = sb("ZI", [128, 32])
    ZRi = sb("ZRi", [128, 32])
    ZIi = sb("ZIi", [128, 32])
    ztmp = sb("ztmp", [128, 4, 32])
    t_const = Tok()
    t_hT, t_hb, t_X = Tok(), Tok(), Tok()
    t_ST, t_STb, t_halo, t_Z = Tok(), Tok(), Tok(), Tok()
    ident = cst[:, 0:128]
    U1o = cst[:, 128:384]
    U2 = cst[:, 384:512]
    ones32 = cst[:, 512:640]
    identb = cstb[:, 0:128]
    onesb = cstb[:, 512:640]

    a0, a1 = nc.bump_sbuf(nc.sbuf_bytes_remaining - 64)
    ARENA = a1 - a0

    class Arena:
        def __init__(self):
            self.off = 0
            self.n = 0

        def reset(self, off=0):
            self.off = off

        def get(self, shape, dt=F32):
            size = int(np.prod(shape[1:])) * (4 if dt in (F32, I32) else 2)
            size = (size + 31) // 32 * 32
            assert self.off + size <= ARENA, ("arena overflow", self.off, size, ARENA)
            self.n += 1
            t = nc.alloc_sbuf_tensor_at("ar%d" % self.n, list(shape), dt, offset=a0 + self.off)
            self.off += size
            return t

    ar = Arena()

    banks = [nc.alloc_psum_tensor("bank%d" % i, [128, 512], F32) for i in range(8)]
    t_bank = [Tok(excl=True) for _ in range(8)]

    def load_slab(name, mb, K):
        i = slab_i[0] % NSLAB
        slab_i[0] += 1
        kb.dma("sp", slabs[i][:, 0:K], wbf[name][mb], reads=[t_wbn[name]], writes=[t_slab[i]])
        return slabs[i], t_slab[i]

    def linear(names, rhs_fn, rhs_toks, Wt, bank_ids, epi):
        if isinstance(names, str):
            names = [names]
        MB = WSPEC[names[0]][0]
        KCT = sum(WSPEC[n][1] // 128 for n in names)
        for mb in range(MB):
            b = bank_ids[mb % len(bank_ids)]
            k0 = 0
            for n in names:
                K = WSPEC[n][1]
                sl, tsl = load_slab(n, mb, K)
                for kc in range(K // 128):
                    kb.op("pe", lambda: PE.matmul(banks[b][:, 0:Wt], lhsT=sl[:, kc * 128:(kc + 1) * 128], rhs=rhs_fn(k0 + kc),
                                                  start=(k0 + kc == 0), stop=(k0 + kc == KCT - 1)),
                          reads=[tsl] + (rhs_toks(k0 + kc) if callable(rhs_toks) else rhs_toks), writes=[t_bank[b]])
                k0 += K // 128
            epi(mb, b)

    t_hTg = [Tok() for _ in range(4)]
    t_hbg = [Tok() for _ in range(4)]

    def load_tile(src_ap, Wt, src_toks):
        for g_ in range(4):
            kb.dma("sp", hT[:, 4 * g_:4 * g_ + 4, 0:Wt], src_ap[4 * g_ * 128:(4 * g_ + 4) * 128, :].rearrange("(c p) w -> p c w", p=128),
                   reads=src_toks, writes=[t_hT, t_hTg[g_]])
        for g_ in range(4):
            cp("dve" if g_ % 2 == 0 else "pool", hb[:, 4 * g_:4 * g_ + 4, 0:Wt], hT[:, 4 * g_:4 * g_ + 4, 0:Wt], [t_hTg[g_]], [t_hbg[g_], t_hb])

    def act(out, in_, func, reads, writes, eng="act", **kw):
        return kb.op("act", lambda: A.activation(out=out, in_=in_, func=func, **kw), reads=reads, writes=writes)

    nopool = [False]

    def tt(eng, out, in0, in1, op, reads, writes):
        if nopool[0]:
            eng = "dve"
        e = V if eng == "dve" else P
        return kb.op(eng, lambda: e.tensor_tensor(out=out, in0=in0, in1=in1, op=op), reads=reads, writes=writes)

    def ts(eng, out, in0, s1, s2, op0, op1, reads, writes):
        if nopool[0]:
            eng = "dve"
        e = V if eng == "dve" else P
        if op1 is None:
            return kb.op(eng, lambda: e.tensor_scalar(out=out, in0=in0, scalar1=s1, scalar2=None, op0=op0), reads=reads, writes=writes)
        return kb.op(eng, lambda: e.tensor_scalar(out=out, in0=in0, scalar1=s1, scalar2=s2, op0=op0, op1=op1), reads=reads, writes=writes)

    def stt(eng, out, in0, scalar, in1, op0, op1, reads, writes):
        eng = "dve"
        e = V
        return kb.op(eng, lambda: e.scalar_tensor_tensor(out=out, in0=in0, scalar=scalar, in1=in1, op0=op0, op1=op1), reads=reads, writes=writes)

    def cp(eng, out, in_, reads, writes):
        if eng == "act":
            return kb.op("act", lambda: A.copy(out=out, in_=in_), reads=reads, writes=writes)
        if nopool[0]:
            eng = "dve"
        e = V if eng == "dve" else P
        return kb.op(eng, lambda: e.tensor_copy(out=out, in_=in_), reads=reads, writes=writes)

    def frac_sin(out, x, tmpi, tmpf, toks, eng="dve"):
        cp(eng, tmpi, x, toks, toks)
        cp(eng, tmpf, tmpi, toks, toks)
        tt(eng, tmpf, x, tmpf, ALU.subtract, toks, toks)
        act(out, tmpf, AF.Sin, toks, toks, scale=TWO_PI)

    t_wb = Tok()
    t_wbn = {n: Tok() for n in WSPEC}
    kb.dma("pool", wuv[:], wuv_d, writes=[t_const])
    for n in ("w0in", "wuk", "wuq", "wglu", "w0out", "w0g", "w0u", "w0d0", "w0d1", "w1in", "w1out", "w1g", "w1u", "w1d0", "w1d1"):
        for m in range(WSPEC[n][0]):
            kb.dma("pool", wbf[n][m], wf[n][m], writes=[t_wbn[n]])
    for (dst, src) in ((cst, cst_d), (iota, iota_d), (vec, vec_d), (s5d, s5d_d), (qn, qn_d), (kvn_g, kvn_d),
                       (cw, cw_d), (cb, cb_d), (dtb, dtb_d), (l1d, l1d_d), (ng, ng_d), (sel, sel_d), (segb, segb_d)):
        kb.dma("sp", dst[:], src, writes=[t_const])
    cp("dve", cstb[:], cst[:], [t_const], [t_const])
    act(aneg[:], dtb[:, 1:2], AF.Exp, [t_const], [t_const])
    ts("dve", aneg[:], aneg[:], -1.0, None, ALU.mult, None, [t_const], [t_const])
    kb.op("dve", lambda: V.memset(ST[:], 0.0), writes=[t_ST])
    kb.op("dve", lambda: V.memset(STb[:], 0.0), writes=[t_STb])
    kb.op("dve", lambda: V.memset(halo[:], 0.0), writes=[t_halo])
    kb.op("dve", lambda: V.memset(ZR[:], 0.0), writes=[t_Z])
    kb.op("dve", lambda: V.memset(ZI[:], 0.0), writes=[t_Z])

    ar.reset()
    t_s = Tok()
    sp_ = ar.get([128, 3, 32])
    kb.dma("sp", sp_[:], s5p_d, writes=[t_s])
    dtp = ar.get([128, 32])
    xim = ar.get([128, 32])
    tmpi = ar.get([128, 32], I32)
    tmpf = ar.get([128, 32])
    tmpx = ar.get([128, 32])
    act(dtp[:], sp_[:, 0, :], AF.Exp, [t_s], [t_s])
    tt("dve", tmpx[:], dtp[:], sp_[:, 1, :], ALU.mult, [t_s], [t_s])
    act(s5R[:], tmpx[:], AF.Exp, [t_s], [t_s, t_const])
    tt("dve", xim[:], dtp[:], sp_[:, 2, :], ALU.mult, [t_s], [t_s])
    ts("dve", s5F[:], xim[:], 1.0 / (2 * math.pi), None, ALU.mult, None, [t_s], [t_s, t_const])
    for wi, wd in enumerate((16.0, float(W))):
        ts("dve", tmpx[:], s5F[:], wd, None, ALU.mult, None, [t_s], [t_s])
        frac_sin(rots[:, wi, :], tmpx[:], tmpi[:], tmpf[:], [t_s])
        ts("dve", tmpx[:], s5F[:], wd, 0.25, ALU.mult, ALU.add, [t_s], [t_s])
        frac_sin(rotc[:, wi, :], tmpx[:], tmpi[:], tmpf[:], [t_s])
    tt("dve", tmpx[:], dtp[:], sp_[:, 1, :], ALU.mult, [t_s], [t_s])
    act(tmpf[:], tmpx[:], AF.Exp, [t_s], [t_s], scale=float(NFR))
    cN, sN, tA, tB = ar.get([128, 32]), ar.get([128, 32]), ar.get([128, 32]), ar.get([128, 32])
    cp("dve", cN[:], rotc[:, 1, :], [t_s], [t_s])
    cp("dve", sN[:], rots[:, 1, :], [t_s], [t_s])
    for _ in range(int(round(math.log2(NT)))):
        tt("dve", tA[:], cN[:], cN[:], ALU.mult, [t_s], [t_s])
        tt("dve", tB[:], sN[:], sN[:], ALU.mult, [t_s], [t_s])
        tt("dve", sN[:], cN[:], sN[:], ALU.mult, [t_s], [t_s])
        ts("dve", sN[:], sN[:], 2.0, None, ALU.mult, None, [t_s], [t_s])
        tt("dve", cN[:], tA[:], tB[:], ALU.subtract, [t_s], [t_s])
    tt("dve", ANc[:, 0, :], tmpf[:], cN[:], ALU.mult, [t_s], [t_s, t_const])
    tt("dve", ANc[:, 1, :], tmpf[:], sN[:], ALU.mult, [t_s], [t_s, t_const])
    tabw = [ar.get([128, 2, W]) for _ in range(2)]
    t_tabw = [Tok(), Tok()]
    xw_ = ar.get([128, W])
    xi_ = ar.get([128, W], I32)
    xf_ = ar.get([128, W])
    t_tab = Tok()
    for j in range(32):
        tb, ttb = tabw[j % 2], t_tabw[j % 2]
        ts("dve", xw_[:], iota[:], s5F[:, j:j + 1], None, ALU.mult, None, [t_s, t_const], [t_s])
        cp("dve", xi_[:], xw_[:], [t_s], [t_s])
        cp("dve", xf_[:], xi_[:], [t_s], [t_s])
        tt("dve", xf_[:], xw_[:], xf_[:], ALU.subtract, [t_s], [t_s])
        act(tb[:, 1, :], xf_[:], AF.Sin, [t_s], [ttb], scale=TWO_PI)
        ts("dve", xw_[:], xw_[:], 0.25, None, ALU.add, None, [t_s], [t_s])
        cp("dve", xi_[:], xw_[:], [t_s], [t_s])
        cp("dve", xf_[:], xi_[:], [t_s], [t_s])
        tt("dve", xf_[:], xw_[:], xf_[:], ALU.subtract, [t_s], [t_s])
        act(tb[:, 0, :], xf_[:], AF.Sin, [t_s], [ttb], scale=TWO_PI)
        kb.dma("sp", TAB_d[j], tb[:], reads=[ttb], writes=[t_tab])
    kb.barrier()
    ar.reset()
    RW = 8 * 128
    r_ = [ar.get([128, RW]) for _ in range(3)]
    braw = [ar.get([128, RW]) for _ in range(2)]
    e_ = [ar.get([128, RW]) for _ in range(8)]
    ei_ = ar.get([128, RW], I32)
    bout = ar.get([128, 8, 2, 128], BF16)
    t_r = Tok()
    for q4 in range(4):
        js = slice(q4 * 8, q4 * 8 + 8)
        for i in range(3):
            kb.dma("sp", r_[i][:].rearrange("p (j s) -> p j s", j=8), s5r_d[i][:, js, :], writes=[t_r])
        for i in range(2):
            kb.dma("sp", braw[i][:].rearrange("p (j s) -> p j s", j=8), s5b_d[i][:, js, :], writes=[t_r])
        T = [t_r]
        dt_, xre, xim_, mag, cs_, sn_, t7, t8 = [e[:] for e in e_]
        act(dt_, r_[0][:], AF.Exp, T, T)
        tt("dve", xre, dt_, r_[1][:], ALU.mult, T, T)
        act(mag, xre, AF.Exp, T, T)
        tt("dve", xim_, dt_, r_[2][:], ALU.mult, T, T)
        ts("dve", xim_, xim_, 1.0 / (2 * math.pi), None, ALU.mult, None, T, T)
        frac_sin(sn_, xim_, ei_[:], t7, T)
        ts("dve", xim_, xim_, 0.25, None, ALU.add, None, T, T)
        frac_sin(cs_, xim_, ei_[:], t7, T)
        abre, abim = cs_, sn_
        tt("dve", abre, mag, cs_, ALU.mult, T, T)
        tt("dve", abim, mag, sn_, ALU.mult, T, T)
        den = dt_
        tt("dve", den, r_[1][:], r_[1][:], ALU.mult, T, T)
        tt("dve", t7, r_[2][:], r_[2][:], ALU.mult, T, T)
        tt("dve", den, den, t7, ALU.add, T, T)
        kb.op("dve", lambda: V.reciprocal(out=den, in_=den), reads=T, writes=T)
        nr = xre
        ts("dve", nr, abre, -1.0, None, ALU.add, None, T, T)
        fre, fim = mag, xim_
        tt("dve", t7, nr, r_[1][:], ALU.mult, T, T)
        tt("dve", t8, abim, r_[2][:], ALU.mult, T, T)
        tt("dve", t7, t7, t8, ALU.add, T, T)
        tt("dve", fre, t7, den, ALU.mult, T, T)
        tt("dve", t7, abim, r_[1][:], ALU.mult, T, T)
        tt("dve", t8, nr, r_[2][:], ALU.mult, T, T)
        tt("dve", t7, t7, t8, ALU.subtract, T, T)
        tt("dve", fim, t7, den, ALU.mult, T, T)
        tt("dve", t7, fre, braw[0][:], ALU.mult, T, T)
        tt("dve", t8, fim, braw[1][:], ALU.mult, T, T)
        tt("dve", bout[:, :, 0, :], t7.rearrange("p (j s) -> p j s", j=8), t8.rearrange("p (j s) -> p j s", j=8), ALU.subtract, T, T)
        tt("dve", t7, fre, braw[1][:], ALU.mult, T, T)
        tt("dve", t8, fim, braw[0][:], ALU.mult, T, T)
        tt("dve", bout[:, :, 1, :], t7.rearrange("p (j s) -> p j s", j=8), t8.rearrange("p (j s) -> p j s", j=8), ALU.add, T, T)
        kb.dma("sp", BL_d[js].rearrange("j p c s -> p j c s"), bout[:], reads=T, writes=[t_tab])
    kb.barrier()
    ar.reset()
    craw = ar.get([128, 8, 2, 128])
    cout = ar.get([128, 8, 3, 128], BF16)
    t_c = Tok()
    for q4 in range(4):
        js = slice(q4 * 8, q4 * 8 + 8)
        kb.dma("sp", craw[:], s5c_d[js].rearrange("j p c s -> p j c s"), writes=[t_c])
        cp("dve", cout[:, :, 0, :], craw[:, :, 0, :], [t_c], [t_c])
        ts("dve", cout[:, :, 1, :], craw[:, :, 0, :], -1.0, None, ALU.mult, None, [t_c], [t_c])
        ts("dve", cout[:, :, 2, :], craw[:, :, 1, :], -1.0, None, ALU.mult, None, [t_c], [t_c])
        kb.dma("sp", CL_d[js].rearrange("j p c s -> p j c s"), cout[:], reads=[t_c], writes=[t_tab])
    kb.barrier()
    chk("setup")

    def layer_norm(Wt, gi, bi, bq_s1, bq_s2, wk):
        sq, mean, rstd, nmr, tmp = wk
        tw = Tok()
        for c in range(16):
            act(sq[:, 0:Wt], hT[:, c, 0:Wt], AF.Square, [t_hT], [tw])
            kb.op("pe", lambda: PE.matmul(banks[bq_s1][:, 0:Wt], lhsT=ones32, rhs=hT[:, c, 0:Wt], start=(c == 0), stop=(c == 15)),
                  reads=[t_hT, t_const], writes=[t_bank[bq_s1]])
            kb.op("pe", lambda: PE.matmul(banks[bq_s2][:, 0:Wt], lhsT=ones32, rhs=sq[:, 0:Wt], start=(c == 0), stop=(c == 15)),
                  reads=[tw, t_const], writes=[t_bank[bq_s2]])
        ts("dve", mean[:, 0:Wt], banks[bq_s1][:, 0:Wt], 1.0 / D, None, ALU.mult, None, [t_bank[bq_s1]], [tw])
        tt("dve", tmp[:, 0:Wt], mean[:, 0:Wt], mean[:, 0:Wt], ALU.mult, [tw], [tw])
        stt("dve", tmp[:, 0:Wt], banks[bq_s2][:, 0:Wt], 1.0 / D, tmp[:, 0:Wt], ALU.mult, ALU.subtract, [t_bank[bq_s2], tw], [tw])
        ts("dve", tmp[:, 0:Wt], tmp[:, 0:Wt], LN_EPS, None, ALU.add, None, [tw], [tw])
        act(tmp[:, 0:Wt], tmp[:, 0:Wt], AF.Sqrt, [tw], [tw])
        kb.op("dve", lambda: V.reciprocal(out=rstd[:, 0:Wt], in_=tmp[:, 0:Wt]), reads=[tw], writes=[tw])
        stt("dve", nmr[:, 0:Wt], mean[:, 0:Wt], -1.0, rstd[:, 0:Wt], ALU.mult, ALU.mult, [tw], [tw])
        for c in range(16):
            e = "dve" if c % 2 == 0 else "pool"
            tt(e, hT[:, c, 0:Wt], hT[:, c, 0:Wt], rstd[:, 0:Wt], ALU.mult, [tw, t_hT], [t_hT])
            tt(e, hT[:, c, 0:Wt], hT[:, c, 0:Wt], nmr[:, 0:Wt], ALU.add, [tw, t_hT], [t_hT])
            act(hT[:, c, 0:Wt], hT[:, c, 0:Wt], AF.Identity, [t_hT, t_const], [t_hT],
                scale=vec[:, gi, c:c + 1], bias=vec[:, bi, c:c + 1])
            cp("pool" if c % 2 == 0 else "dve", hb[:, c, 0:Wt], hT[:, c, 0:Wt], [t_hT], [t_hb])

    def ffn(Wt, gname, uname, dname):
        ar.reset()
        actb = ar.get([128, FC, W], BF16)
        sa = [ar.get([128, W]) for _ in range(2)]
        t_act, t_sa = Tok(), [Tok(), Tok()]
        for fb in range(FC):
            bg, bu = (0, 1) if fb % 2 == 0 else (2, 3)
            for (nm, b) in ((gname, bg), (uname, bu)):
                sl, tsl = load_slab(nm, fb, 2048)
                for kc in range(16):
                    kb.op("pe", lambda: PE.matmul(banks[b][:, 0:Wt], lhsT=sl[:, kc * 128:(kc + 1) * 128], rhs=hb[:, kc, 0:Wt],
                                                  start=(kc == 0), stop=(kc == 15)),
                          reads=[tsl, t_hb], writes=[t_bank[b]])
            s_, ts_ = sa[fb % 2], t_sa[fb % 2]
            act(s_[:, 0:Wt], banks[bg][:, 0:Wt], AF.Silu, [t_bank[bg]], [ts_])
            tt("dve", actb[:, fb, 0:Wt], s_[:, 0:Wt], banks[bu][:, 0:Wt], ALU.mult, [ts_, t_bank[bu]], [t_act])

        def epi(mb, b):
            stt("dve", hT[:, mb, 0:Wt], hT[:, mb, 0:Wt], ALPHA, banks[b][:, 0:Wt], ALU.mult, ALU.add, [t_hT, t_bank[b]], [t_hT])
        linear([dname + "0", dname + "1"], lambda kc: actb[:, kc, 0:Wt], [t_act], Wt, [4, 5, 6, 7], epi)
        return [sa[0], sa[1]]

    def dump(name, pos0, Wt):
        if dbg:
            kb.dma("sp", dbg_d[name][:, pos0:pos0 + Wt].rearrange("(c p) w -> p c w", p=128), hT[:, :, 0:Wt], reads=[t_hT])

    t_Kc = [Tok() for _ in range(8)]
    t_Vc, t_KR = Tok(), Tok()
    t_out = Tok()

    t_Kg, t_Vg, t_KRg, t_Gl, t_Gg, t_Hl, t_Hg = Tok(), Tok(), Tok(), Tok(), Tok(), Tok(), Tok()
    t_STl, t_STg, t_STin, t_dsl, t_dsg, t_H0 = Tok(), Tok(), Tok(), Tok(), Tok(), Tok()
    t_hin, t_Gin, t_dsum = Tok(), Tok(), Tok()
    sched = [(ph, t) for ph in (1, 2, 3, 4) for t in range(NT + 1)]
    for (phase, ti) in sched:
        Wt = NMETA if ti == 0 else W
        pos0 = 0 if ti == 0 else NMETA + W * (ti - 1)
        wi_prev = 0 if ti == 1 else 1
        nsub = (Wt + 127) // 128
        m12, m1, m2 = phase not in (1, 2), phase != 1, phase != 2
        m34, m3, m4 = phase not in (3, 4), phase != 3, phase != 4
        kb.mute = False
        nopool[0] = (phase == 1)
        kb.barrier()
        ar.reset()
        if ti == 0 and phase in (1, 2):
            kb.op("dve", lambda: V.memset(ZR[:], 0.0), writes=[t_Z])
            kb.op("dve", lambda: V.memset(ZI[:], 0.0), writes=[t_Z])
        if ti == 0 and phase == 2:
            Gg = ar.get([128, 4, 2, 32])
            Tc = ar.get([128, 2, 32])
            t4 = [ar.get([128, 32]) for _ in range(4)]
            tg = Tok()
            kb.dma("sp", Gg[:].rearrange("p r c j -> p r (c j)"), Gg_d.rearrange("r p x -> p r x"), reads=[t_Gg], writes=[tg])
            kb.op("dve", lambda: V.memset(Gin[:], 0.0), writes=[t_Gin])
            cp("dve", Tc[:], Gg[:, 0, :, :], [tg], [tg])
            for k in range(1, 4):
                stt("dve", Gin[:], Tc[:], sel[:, k:k + 1], Gin[:], ALU.mult, ALU.add, [tg, t_const, t_Gin], [t_Gin])
                if k < 3:
                    tt("dve", t4[0][:], Tc[:, 0, :], ANc[:, 0, :], ALU.mult, [tg, t_const], [tg])
                    tt("dve", t4[1][:], Tc[:, 1, :], ANc[:, 1, :], ALU.mult, [tg, t_const], [tg])
                    tt("dve", t4[2][:], Tc[:, 0, :], ANc[:, 1, :], ALU.mult, [tg, t_const], [tg])
                    tt("dve", t4[3][:], Tc[:, 1, :], ANc[:, 0, :], ALU.mult, [tg, t_const], [tg])
                    tt("dve", Tc[:, 0, :], t4[0][:], t4[1][:], ALU.subtract, [tg], [tg])
                    tt("dve", Tc[:, 1, :], t4[2][:], t4[3][:], ALU.add, [tg], [tg])
                    tt("dve", Tc[:], Tc[:], Gg[:, k, :, :], ALU.add, [tg], [tg])
            kb.barrier()
            ar.reset()
        if ti == 0 and phase in (3, 4):
            kb.op("dve", lambda: V.memset(ST[:], 0.0), writes=[t_ST])
            kb.op("dve", lambda: V.memset(STb[:], 0.0), writes=[t_STb])
            kb.op("dve", lambda: V.memset(halo[:], 0.0), writes=[t_halo])
            kb.op("dve", lambda: V.memset(dsum[:], 0.0), writes=[t_dsum])
        if ti == 0 and phase == 3:
            Hg = ar.get([128, 4, 192])
            th = Tok()
            kb.dma("sp", Hg[:], Hg_d.rearrange("r p x -> p r x"), reads=[t_Hg], writes=[th])
            kb.op("dve", lambda: V.memset(halo_in[:], 0.0), writes=[t_hin])
            for k in range(1, 4):
                stt("dve", halo_in[:].rearrange("p c k -> p (c k)"), Hg[:, k - 1, :], sel[:, k:k + 1], halo_in[:].rearrange("p c k -> p (c k)"),
                    ALU.mult, ALU.add, [th, t_const, t_hin], [t_hin])
            kb.barrier()
            ar.reset()
        if ti == 0 and phase == 4:
            Eg = ar.get([128, 1, 64, 64])
            Tst = ar.get([128, 64, 64])
            Sin = ar.get([128, 64, 64])
            Dg = ar.get([128, 4, 64])
            te = Tok()
            kb.dma("sp", Dg[:], dsg_d.rearrange("r p x -> p r x"), reads=[t_dsg], writes=[te])
            act(Dg[:], Dg[:], AF.Exp, [te], [te])
            for i_ in range(2):
                kb.dma("sp", Tst[:, 32 * i_:32 * i_ + 32, :].rearrange("p h q -> p (h q)"), STg_d[i_][0], reads=[t_STg], writes=[te])
            kb.op("dve", lambda: V.memset(Sin[:], 0.0), writes=[te])
            for k in range(1, 4):
                stt("dve", Sin[:], Tst[:], sel[:, k:k + 1], Sin[:], ALU.mult, ALU.add, [te, t_const], [te])
                if k < 3:
                    for i_ in range(2):
                        kb.dma("sp", Eg[:, 0, 32 * i_:32 * i_ + 32, :].rearrange("p h q -> p (h q)"), STg_d[i_][k], reads=[t_STg], writes=[te])
                    tt("dve", Tst[:], Tst[:], Dg[:, k, :].unsqueeze(2).broadcast_to([128, 64, 64]), ALU.mult, [te], [te])
                    tt("dve", Tst[:], Tst[:], Eg[:, 0], ALU.add, [te], [te])
            kb.dma("sp", STin_d, Sin[:].rearrange("p h q -> p (h q)"), reads=[te], writes=[t_STin])
            kb.barrier()
            ar.reset()
        kb.mute = m12
        load_tile(xT[:, pos0:pos0 + Wt], Wt, [])
        qT = ar.get([128, 12, W], BF16)
        q_end = ar.off
        u32 = ar.get([128, 8, W])
        ub = ar.get([128, 8, W], BF16)
        pers_end = ar.off
        ql = ar.get([128, 4, W])
        kvl = ar.get([128, 2, W])
        krr = ar.get([128, 2, W])
        rC = ar.get([128, W])
        rS = ar.get([128, W])
        yv = ar.get([128, W])
        g2 = ar.get([128, W])
        t_qT, t_yv = Tok(), Tok()
        t_u32, t_ub, t_ql, t_kvl, t_krr, t_rope = Tok(), Tok(), Tok(), Tok(), Tok(), Tok()
        kb.dma("sp", rC[:, 0:Wt], ropeC_d[:, pos0:pos0 + Wt], writes=[t_rope])
        kb.dma("sp", rS[:, 0:Wt], ropeS_d[:, pos0:pos0 + Wt], writes=[t_rope])

        def epi_in(mb, b):
            src = banks[b][:, 0:Wt]
            if mb < 8:
                cp("act", u32[:, mb, 0:Wt], src, [t_bank[b]], [t_u32])
                cp("dve", ub[:, mb, 0:Wt], src, [t_bank[b]], [t_ub])
            elif mb < 12:
                cp("act", ql[:, mb - 8, 0:Wt], src, [t_bank[b]], [t_ql])
            elif mb < 14:
                cp("act", kvl[:, mb - 12, 0:Wt], src, [t_bank[b]], [t_kvl])
            else:
                cp("act", krr[:, mb - 14, 0:Wt], src, [t_bank[b]], [t_krr])
        linear("w0in", lambda kc: hb[:, kc, 0:Wt], lambda kc: [t_hbg[kc // 4]], Wt, [0, 1, 2, 3], epi_in)
        mixT = X
        chk("inproj%d" % ti)

        def rms(src, t_src, nch, dim, gain, dstb, t_dst, bk):
            for c in range(nch):
                act(yv[:, 0:Wt], src[:, c, 0:Wt], AF.Square, [t_src], [t_yv])
                kb.op("pe", lambda: PE.matmul(banks[bk][:, 0:Wt], lhsT=ones32, rhs=yv[:, 0:Wt], start=(c == 0), stop=(c == nch - 1)),
                      reads=[t_yv, t_const], writes=[t_bank[bk]])
            ts("dve", g2[:, 0:Wt], banks[bk][:, 0:Wt], 1.0 / dim, RMS_EPS, ALU.mult, ALU.add, [t_bank[bk]], [t_yv])
            act(g2[:, 0:Wt], g2[:, 0:Wt], AF.Sqrt, [t_yv], [t_yv])
            kb.op("dve", lambda: V.reciprocal(out=g2[:, 0:Wt], in_=g2[:, 0:Wt]), reads=[t_yv], writes=[t_yv])
            for c in range(nch):
                stt("dve", dstb[:, c, 0:Wt], src[:, c, 0:Wt], gain[:, c:c + 1], g2[:, 0:Wt], ALU.mult, ALU.mult,
                    [t_src, t_yv, t_const], [t_dst])

        qnb = X[:, 16:20, :]
        kvb = X[:, 20:22, :]
        t_qnb, t_kvb = Tok(), Tok()
        kb.mute = m2
        rms(ql, t_ql, 4, 512.0, qn, qnb, t_qnb, 4)
        kb.mute = m1
        rms(kvl, t_kvl, 2, 256.0, kvn_g, kvb, t_kvb, 5)
        kr2 = X[:, 22, :]
        t_kr2 = Tok()
        tt("dve", yv[:, 0:Wt], krr[:, 0, 0:Wt], rC[:, 0:Wt], ALU.mult, [t_krr, t_rope], [t_yv])
        tt("dve", g2[:, 0:Wt], krr[:, 1, 0:Wt], rS[:, 0:Wt], ALU.mult, [t_krr, t_rope], [t_yv])
        tt("dve", kr2[:, 0:Wt], yv[:, 0:Wt], g2[:, 0:Wt], ALU.add, [t_yv], [t_kr2])
        kb.dma("sp", KR_d[:, pos0:pos0 + Wt], kr2[:, 0:Wt], reads=[t_kr2], writes=[t_KR])
        knew = [X[:, 23, :], X[:, 24, :]]
        t_knew = [Tok(), Tok()]

        def epi_k(mb, b):
            cp("act", knew[mb % 2][:, 0:Wt], banks[b][:, 0:Wt], [t_bank[b]], [t_knew[mb % 2]])
            kb.dma("sp", Kc_d[mb][:, pos0:pos0 + Wt], knew[mb % 2][:, 0:Wt], reads=[t_knew[mb % 2]], writes=[t_Kc[mb]])
        linear("wuk", lambda kc: kvb[:, kc, 0:Wt], [t_kvb], Wt, [0, 1], epi_k)
        vnew = [X[:, 25:27, :].rearrange("p a w -> p (a w)"), X[:, 27:29, :].rearrange("p a w -> p (a w)")]
        t_vnew = [Tok(), Tok()]
        for sj in range(nsub):
            L = min(128, Wt - sj * 128)
            kt = 0 if ti == 0 else 1 + 4 * (ti - 1) + sj
            vb, tvb = vnew[sj % 2], t_vnew[sj % 2]
            for half in range(2):
                b = 2 + half
                for kc in range(2):
                    kb.op("pe", lambda: PE.matmul(banks[b][0:L, :], lhsT=kvb[:, kc, sj * 128:sj * 128 + L],
                                                  rhs=wuv[:, kc, half * 512:(half + 1) * 512], start=(kc == 0), stop=(kc == 1)),
                          reads=[t_kvb, t_const], writes=[t_bank[b]])
                cp("act" if half == 0 else "dve", vb[0:L, half * 512:(half + 1) * 512], banks[b][0:L, :], [t_bank[b]], [tvb])
            if ti == 0:
                kb.dma("sp", Vm_d[0:L, :], vb[0:L, :], reads=[tvb], writes=[t_Vc])
            else:
                kb.dma("sp", Vf_d[ti - 1][0:L, sj, :], vb[0:L, :], reads=[tvb], writes=[t_Vc])
        kb.mute = m2

        def epi_q(mb, b):
            hp, r = mb // 4, mb % 4
            src = banks[b][:, 0:Wt]
            if r < 2:
                act(qT[:, hp * 3 + r, 0:Wt], src, AF.Copy, [t_bank[b]], [t_qT], scale=ATT_SCALE)
            elif r == 2:
                tt("dve", yv[:, 0:Wt], src, rC[:, 0:Wt], ALU.mult, [t_bank[b], t_rope], [t_yv])
            else:
                tt("dve", g2[:, 0:Wt], src, rS[:, 0:Wt], ALU.mult, [t_bank[b], t_rope], [t_yv])
                tt("dve", g2[:, 0:Wt], g2[:, 0:Wt], yv[:, 0:Wt], ALU.add, [t_yv], [t_yv])
                act(qT[:, hp * 3 + 2, 0:Wt], g2[:, 0:Wt], AF.Copy, [t_yv], [t_qT], scale=ATT_SCALE)
        linear("wuq", lambda kc: qnb[:, kc, 0:Wt], [t_qnb], Wt, [0, 1, 2, 3], epi_q)

        kb.mute = m12
        chk("a1_%d" % ti)
        kb.barrier()
        ar.reset(pers_end)
        if ti > 0:
            c_, s_ = rotc[:, wi_prev, :], rots[:, wi_prev, :]
            tt("dve", ztmp[:, 0, :], ZR[:], c_, ALU.mult, [t_Z, t_const], [t_Z])
            tt("dve", ztmp[:, 1, :], ZI[:], s_, ALU.mult, [t_Z, t_const], [t_Z])
            tt("dve", ztmp[:, 2, :], ZR[:], s_, ALU.mult, [t_Z, t_const], [t_Z])
            tt("dve", ztmp[:, 3, :], ZI[:], c_, ALU.mult, [t_Z, t_const], [t_Z])
            tt("dve", ZRi[:], ztmp[:, 0, :], ztmp[:, 1, :], ALU.subtract, [t_Z], [t_Z])
            tt("dve", ZIi[:], ztmp[:, 2, :], ztmp[:, 3, :], ALU.add, [t_Z], [t_Z])
            if ti == 1:
                ts("dve", ZRi[:], ZRi[:], sel[:, 0:1], None, ALU.mult, None, [t_Z, t_const], [t_Z])
                ts("dve", ZIi[:], ZIi[:], sel[:, 0:1], None, ALU.mult, None, [t_Z, t_const], [t_Z])
                kb.mute = m2
                tt("dve", ZRi[:], ZRi[:], Gin[:, 0, :], ALU.add, [t_Z, t_Gin], [t_Z])
                tt("dve", ZIi[:], ZIi[:], Gin[:, 1, :], ALU.add, [t_Z, t_Gin], [t_Z])
                kb.mute = m12
        else:
            kb.op("dve", lambda: V.memset(ZRi[:], 0.0), writes=[t_Z])
            kb.op("dve", lambda: V.memset(ZIi[:], 0.0), writes=[t_Z])
        NS = 2
        tabs = [ar.get([128, 2, W]) for _ in range(NS)]
        BLs = [X[:, 28, i * 256:(i + 1) * 256].rearrange("p (c s) -> p c s", c=2) for i in range(NS)]
        CLs = [X[:, 29 + i, 0:384].rearrange("p (c s) -> p c s", c=3) for i in range(NS)]
        t_ld = [Tok() for _ in range(NS)]
        w1 = [ar.get([128, W]) for _ in range(4)]
        zz = [ar.get([128, W]) for _ in range(2)]
        pp = [X[:, 24 + i, :] for i in range(4)]
        t_w1 = [Tok() for _ in range(4)]
        t_zz = [Tok() for _ in range(2)]
        t_pp = [Tok() for _ in range(4)]
        yv = ar.get([128, W])
        g2 = ar.get([128, W])
        gb = X[:, 16:24, :]
        t_yv, t_gb = Tok(), Tok()
        for cc in range(8):
            yb = 4 + (cc % 2)
            for q in range(4):
                j = cc * 4 + q
                s = j % NS
                kb.dma("sp", tabs[s][:, :, 0:Wt], TAB_d[j][:, :, 0:Wt], reads=[t_tab], writes=[t_ld[s]])
                kb.dma("sp", BLs[s], BL_d[j], reads=[t_tab], writes=[t_ld[s]])
                kb.mute = m2
                kb.dma("sp", CLs[s], CL_d[j], reads=[t_tab], writes=[t_ld[s]])
                kb.mute = m12
                cs_t, sn_t = tabs[s][:, 0, 0:Wt], tabs[s][:, 1, 0:Wt]
                bre, bim = (0, 1) if j % 2 == 0 else (2, 3)
                kb.op("pe", lambda: PE.matmul(banks[bre][:, 0:Wt], lhsT=BLs[s][:, 0, :], rhs=ub[:, cc, 0:Wt], start=True, stop=True),
                      reads=[t_ld[s], t_ub], writes=[t_bank[bre]])
                kb.op("pe", lambda: PE.matmul(banks[bim][:, 0:Wt], lhsT=BLs[s][:, 1, :], rhs=ub[:, cc, 0:Wt], start=True, stop=True),
                      reads=[t_ld[s], t_ub], writes=[t_bank[bim]])
                tt("dve", w1[0][:, 0:Wt], banks[bre][:, 0:Wt], cs_t, ALU.mult, [t_bank[bre], t_ld[s]], [t_w1[0]])
                tt("dve", w1[1][:, 0:Wt], banks[bim][:, 0:Wt], sn_t, ALU.mult, [t_bank[bim], t_ld[s]], [t_w1[1]])
                tt("dve", w1[2][:, 0:Wt], banks[bim][:, 0:Wt], cs_t, ALU.mult, [t_bank[bim], t_ld[s]], [t_w1[2]])
                tt("dve", w1[3][:, 0:Wt], banks[bre][:, 0:Wt], sn_t, ALU.mult, [t_bank[bre], t_ld[s]], [t_w1[3]])
                tt("pool", w1[0][:, 0:Wt], w1[0][:, 0:Wt], w1[1][:, 0:Wt], ALU.add, [t_w1[1]], [t_w1[0]])
                tt("pool", w1[2][:, 0:Wt], w1[2][:, 0:Wt], w1[3][:, 0:Wt], ALU.subtract, [t_w1[3]], [t_w1[2]])
                kb.op("dve", lambda: V.tensor_tensor_scan(out=zz[0][:, 0:Wt], data0=s5R[:, j:j + 1].broadcast_to([128, Wt]),
                                                          data1=w1[0][:, 0:Wt], initial=ZRi[:, j:j + 1], op0=ALU.mult, op1=ALU.add),
                      reads=[t_w1[0], t_Z, t_const], writes=[t_zz[0]])
                kb.op("dve", lambda: V.tensor_tensor_scan(out=zz[1][:, 0:Wt], data0=s5R[:, j:j + 1].broadcast_to([128, Wt]),
                                                          data1=w1[2][:, 0:Wt], initial=ZIi[:, j:j + 1], op0=ALU.mult, op1=ALU.add),
                      reads=[t_w1[2], t_Z, t_const], writes=[t_zz[1]])
                cp("act", ZR[:, j:j + 1], zz[0][:, Wt - 1:Wt], [t_zz[0]], [t_Z])
                cp("act", ZI[:, j:j + 1], zz[1][:, Wt - 1:Wt], [t_zz[1]], [t_Z])
                kb.mute = m2
                tt("dve", pp[0][:, 0:Wt], zz[0][:, 0:Wt], cs_t, ALU.mult, [t_zz[0], t_ld[s]], [t_pp[0]])
                tt("dve", pp[1][:, 0:Wt], zz[1][:, 0:Wt], sn_t, ALU.mult, [t_zz[1], t_ld[s]], [t_pp[1]])
                tt("pool", pp[2][:, 0:Wt], zz[0][:, 0:Wt], sn_t, ALU.mult, [t_zz[0], t_ld[s]], [t_pp[2]])
                tt("pool", pp[3][:, 0:Wt], zz[1][:, 0:Wt], cs_t, ALU.mult, [t_zz[1], t_ld[s]], [t_pp[3]])
                for k in range(4):
                    kb.op("pe", lambda: PE.matmul(banks[yb][:, 0:Wt], lhsT=CLs[s][:, (0, 1, 2, 2)[k], :], rhs=pp[k][:, 0:Wt],
                                                  start=(q == 0 and k == 0), stop=(q == 3 and k == 3)),
                          reads=[t_ld[s], t_pp[k]], writes=[t_bank[yb]])
                kb.mute = m12
            kb.mute = m2
            stt("dve", yv[:, 0:Wt], u32[:, cc, 0:Wt], s5d[:, cc:cc + 1], banks[yb][:, 0:Wt], ALU.mult, ALU.add,
                [t_u32, t_bank[yb], t_const], [t_yv])
            act(g2[:, 0:Wt], yv[:, 0:Wt], AF.Square, [t_yv], [t_yv])
            ts("dve", g2[:, 0:Wt], g2[:, 0:Wt], 0.044715, 1.0, ALU.mult, ALU.add, [t_yv], [t_yv])
            tt("dve", g2[:, 0:Wt], g2[:, 0:Wt], yv[:, 0:Wt], ALU.mult, [t_yv], [t_yv])
            act(g2[:, 0:Wt], g2[:, 0:Wt], AF.Sigmoid, [t_yv], [t_yv], scale=GELU_C)
            tt("dve", u32[:, cc, 0:Wt], g2[:, 0:Wt], yv[:, 0:Wt], ALU.mult, [t_yv, t_u32], [t_u32])
            cp("pool", gb[:, cc, 0:Wt], u32[:, cc, 0:Wt], [t_u32], [t_gb])
            kb.mute = m12

        def epi_glu(mb, b):
            act(g2[:, 0:Wt], banks[b][:, 0:Wt], AF.Sigmoid, [t_bank[b]], [t_yv])
            tt("dve", mixT[:, mb, 0:Wt], g2[:, 0:Wt], u32[:, mb, 0:Wt], ALU.mult, [t_yv, t_u32], [t_X])
        kb.mute = m2
        linear("wglu", lambda kc: gb[:, kc, 0:Wt], [t_gb], Wt, [6, 7], epi_glu)
        kb.mute = m1
        if ti == NT:
            Gl = ar.get([128, 2, 32])
            c_, s_ = rotc[:, 1, :], rots[:, 1, :]
            tt("dve", ztmp[:, 0, :], ZR[:], c_, ALU.mult, [t_Z, t_const], [t_Z])
            tt("dve", ztmp[:, 1, :], ZI[:], s_, ALU.mult, [t_Z, t_const], [t_Z])
            tt("dve", ztmp[:, 2, :], ZR[:], s_, ALU.mult, [t_Z, t_const], [t_Z])
            tt("dve", ztmp[:, 3, :], ZI[:], c_, ALU.mult, [t_Z, t_const], [t_Z])
            tt("dve", Gl[:, 0, :], ztmp[:, 0, :], ztmp[:, 1, :], ALU.subtract, [t_Z], [t_Z])
            tt("dve", Gl[:, 1, :], ztmp[:, 2, :], ztmp[:, 3, :], ALU.add, [t_Z], [t_Z])
            kb.dma("sp", Gl_d, Gl[:].rearrange("p c j -> p (c j)"), reads=[t_Z], writes=[t_Gl])
            kb.collective("AllGather", GROUPS, Gl_d, Gg_d.rearrange("r p x -> (r p) x"), reads=[t_Gl], writes=[t_Gg])
            for h_ in range(8):
                kb.collective("AllGather", GROUPS, Kc_d[h_], Kg_d[h_].rearrange("r p c -> (r p) c"), reads=[t_Kc[h_]], writes=[t_Kg])
            for q_ in range(NT):
                kb.collective("AllGather", GROUPS, Vf_d[q_].rearrange("p k d -> p (k d)"), Vgf_d[q_].rearrange("r p k d -> (r p) (k d)"),
                              reads=[t_Vc], writes=[t_Vg])
            kb.collective("AllGather", GROUPS, KR_d, KRg_d.rearrange("r p c -> (r p) c"), reads=[t_KR], writes=[t_KRg])
        kb.mute = m2

        chk("a2_%d" % ti)
        kb.barrier()
        nk = pos0 + Wt
        nkt = 1 if ti == 0 else 1 + 4 * ti
        ar.reset(q_end)
        NKA = NTOK + 3 * NFR
        KRs = ar.get([128, NKA], BF16)
        KBf = [ar.get([128, NTOK], BF16) for _ in range(2)]
        VBf = [ar.get([128, NKT, 128], BF16) for _ in range(2)]
        PT = [X[:, 16 + i, :] for i in range(4)]
        rec = ar.get([128, W])
        t_KRs, t_rec = Tok(), Tok()
        t_KBf, t_VBf = [Tok(), Tok()], [Tok(), Tok()]
        t_PT = [Tok() for _ in range(4)]
        kb.dma("sp", KRs[:, 0:nk], KR_d[:, 0:nk], reads=[t_KR], writes=[t_KRs])
        if ti > 0:
            kb.dma("sp", KRs[:, NTOK:NKA].rearrange("p (r c) -> p r c", r=3), KRg_d[0:3, :, NMETA:NTOK].rearrange("r p c -> p r c"),
                   reads=[t_KRg], writes=[t_KRs])
        steps = []
        for h in range(8):
            steps.append((h, -1))
            if ti > 0:
                for r in range(3):
                    steps.append((h, r))

        def seg_load(si):
            h, r = steps[si]
            kbuf, vbuf, tk, tv = KBf[si % 2], VBf[si % 2], t_KBf[si % 2], t_VBf[si % 2]
            if r < 0:
                kb.dma("sp", kbuf[:, 0:nk], Kc_d[h][:, 0:nk], reads=[t_Kc[h]], writes=[tk])
                kb.dma("sp", vbuf[:, 0, :], Vm_d[:, h * 128:(h + 1) * 128], reads=[t_Vc], writes=[tv])
                for q_ in range(ti):
                    kb.dma("sp", vbuf[:, 1 + 4 * q_:5 + 4 * q_, :], Vf_d[q_][:, :, h * 128:(h + 1) * 128], reads=[t_Vc], writes=[tv])
            else:
                kb.dma("sp", kbuf[:, 0:NFR], Kg_d[h][r, :, NMETA:NTOK], reads=[t_Kg], writes=[tk])
                for q_ in range(NT):
                    kb.dma("sp", vbuf[:, 4 * q_:4 * q_ + 4, :], Vgf_d[q_][r, :, :, h * 128:(h + 1) * 128], reads=[t_Vg], writes=[tv])

        pti = 0

        def seg_compute(si):
            nonlocal_pti = pti_box
            h, r = steps[si]
            hp, jh = h // 2, h % 2
            kbuf, vbuf, tk, tv = KBf[si % 2], VBf[si % 2], t_KBf[si % 2], t_VBf[si % 2]
            ob, sbk = (4, 5) if h % 2 == 0 else (6, 7)
            qn_ = qT[:, hp * 3 + jh, 0:Wt]
            qr_ = qT[jh * 64:(jh + 1) * 64, hp * 3 + 2, 0:Wt]
            first_seg = (r < 0)
            last_seg = (ti == 0) or (r == 2)
            tiles = []
            if r < 0:
                tiles.append((0, 0, 0, NMETA, 0, None, None))
                for kt in range(1, nkt):
                    kc0 = NMETA + 128 * (kt - 1)
                    rr = kt - (1 + 4 * (ti - 1))
                    if rr < 0:
                        tiles.append((kt, kc0, kc0, 128, 0, None, None))
                    else:
                        tiles.append((kt, kc0, kc0, 128, 128 * rr, (64, 128 * rr), None))
            else:
                for k in range(NPK):
                    tiles.append((k, 128 * k, NTOK + r * NFR + 128 * k, 128, 0, None, r))
            nmm = len(tiles)
            slot0 = nonlocal_pti[0]
            nonlocal_pti[0] += nmm

            def qk_exp(im):
                (vt, kc0, krc, nkeys, c0, zero, bcol) = tiles[im]
                sl_ = (slot0 + im) % 4
                p_, tp_ = PT[sl_], t_PT[sl_]
                kb.op("pe", lambda: PE.matmul(banks[sl_][0:nkeys, 0:Wt], lhsT=kbuf[:, kc0:kc0 + nkeys], rhs=qn_, start=True, stop=False),
                      reads=[tk, t_qT], writes=[t_bank[sl_]])
                kb.op("pe", lambda: PE.matmul(banks[sl_][0:nkeys, 0:Wt], lhsT=KRs[jh * 64:(jh + 1) * 64, krc:krc + nkeys], rhs=qr_,
                                              start=False, stop=True),
                      reads=[t_KRs, t_qT], writes=[t_bank[sl_]])
                if bcol is None:
                    act(p_[0:nkeys, 0:Wt], banks[sl_][0:nkeys, 0:Wt], AF.Exp, [t_bank[sl_]], [tp_])
                else:
                    act(p_[0:nkeys, 0:Wt], banks[sl_][0:nkeys, 0:Wt], AF.Exp, [t_bank[sl_], t_const], [tp_], bias=segb[0:nkeys, bcol:bcol + 1])
                if zero is not None:
                    kb.op("pool", lambda: P.memset(p_[zero[0]:zero[0] + 64, zero[1]:zero[1] + 64], 0.0), reads=[], writes=[tp_])

            def pv_sum(im):
                (vt, kc0, krc, nkeys, c0, zero, bcol) = tiles[im]
                sl_ = (slot0 + im) % 4
                p_, tp_ = PT[sl_], t_PT[sl_]
                st_ = first_seg and im == 0
                sp_ = last_seg and im == nmm - 1
                kb.op("pe", lambda: PE.matmul(banks[ob][:, c0:Wt], lhsT=vbuf[0:nkeys, vt, :], rhs=p_[0:nkeys, c0:Wt], start=st_, stop=sp_),
                      reads=[tv, tp_], writes=[t_bank[ob]])
                kb.op("pe", lambda: PE.matmul(banks[sbk][:, c0:Wt], lhsT=onesb[0:nkeys, :], rhs=p_[0:nkeys, c0:Wt], start=st_, stop=sp_),
                      reads=[t_const, tp_], writes=[t_bank[sbk]])
            LAG = 2
            for im in range(nmm + LAG):
                if im < nmm:
                    qk_exp(im)
                if im - LAG >= 0:
                    pv_sum(im - LAG)
            if last_seg:
                kb.op("dve", lambda: V.reciprocal(out=rec[:, 0:Wt], in_=banks[sbk][:, 0:Wt]), reads=[t_bank[sbk]], writes=[t_rec])
                tt("dve", mixT[:, 8 + h, 0:Wt], banks[ob][:, 0:Wt], rec[:, 0:Wt], ALU.mult, [t_bank[ob], t_rec], [t_X])

        pti_box = [0]
        seg_load(0)
        for si in range(len(steps)):
            if si + 1 < len(steps):
                seg_load(si + 1)
            seg_compute(si)

        chk("attn%d" % ti)
        kb.barrier()

        def epi_res(mb, b):
            stt("dve", hT[:, mb, 0:Wt], hT[:, mb, 0:Wt], ALPHA, banks[b][:, 0:Wt], ALU.mult, ALU.add, [t_hT, t_bank[b]], [t_hT])
        linear("w0out", lambda kc: mixT[:, kc, 0:Wt], [t_X], Wt, [0, 1, 2, 3], epi_res)
        ar.reset()
        wk = [ar.get([128, W]) for _ in range(5)]
        layer_norm(Wt, 0, 1, 4, 5, wk)
        kb.barrier()
        ffn(Wt, "w0g", "w0u", "w0d")
        kb.barrier()
        ar.reset()
        wk = [ar.get([128, W]) for _ in range(5)]
        layer_norm(Wt, 2, 3, 4, 5, wk)
        dump("d_l0", pos0, Wt)
        kb.dma("sp", H0_d[:, pos0:pos0 + Wt].rearrange("(c p) w -> p c w", p=128), hT[:, :, 0:Wt], reads=[t_hT], writes=[t_H0])
        if ti == NT:
            Hl = ar.get([128, 48, 4])
            t_Hl_s = Tok()
            for g_ in range(8):
                for r_ in (0, 1, 2, 3, 8, 9):
                    ch = g_ * 4 + r_ if r_ < 4 else 32 + (r_ - 8) * 8 + g_
                    sl, tsl = load_slab("w1in", 1 + g_ * 10 + r_, 2048)
                    b = ch % 2
                    for kc in range(16):
                        kb.op("pe", lambda: PE.matmul(banks[b][:, 0:4], lhsT=sl[:, kc * 128:(kc + 1) * 128], rhs=hb[:, kc, Wt - 4:Wt],
                                                      start=(kc == 0), stop=(kc == 15)),
                              reads=[tsl, t_hb], writes=[t_bank[b]])
                    cp("act", Hl[:, ch, :], banks[b][:, 0:4], [t_bank[b]], [t_Hl_s])
            kb.dma("sp", Hl_d, Hl[:].rearrange("p c k -> p (c k)"), reads=[t_Hl_s], writes=[t_Hl])
            kb.collective("AllGather", GROUPS, Hl_d, Hg_d.rearrange("r p x -> (r p) x"), reads=[t_Hl], writes=[t_Hg])
        chk("l0_%d" % ti)

        kb.mute = m34
        kb.barrier()
        ar.reset()
        ynT = X
        load_tile(H0_d[:, pos0:pos0 + Wt], Wt, [t_H0])
        if ti == 1:
            ts("dve", halo[:], halo[:], sel[:, 0:1], None, ALU.mult, None, [t_halo, t_const], [t_halo])
            tt("dve", halo[:, :, 0:3], halo[:, :, 0:3], halo_in[:, :, 1:4], ALU.add, [t_halo, t_hin], [t_halo])
            ts("dve", ST[:], ST[:], sel[:, 0:1], None, ALU.mult, None, [t_ST, t_const], [t_ST])
            kb.mute = m4
            Sin_ = ar.get([128, 64, 64])
            t_sin = Tok()
            kb.dma("sp", Sin_[:].rearrange("p h q -> p (h q)"), STin_d, reads=[t_STin], writes=[t_sin])
            tt("dve", ST[:], ST[:], Sin_[:], ALU.add, [t_ST, t_sin], [t_ST])
            kb.mute = m34
            cp("act", STb[:], ST[:], [t_ST], [t_STb])
            kb.barrier()
            ar.reset()
        dtT = ar.get([128, W])
        daT = ar.get([128, W])
        dt_tok = ar.get([128, 4, 64])
        da_tok = ar.get([128, 4, 64])
        cs_tok = ar.get([128, 4, 64])
        wg_tok = ar.get([128, 4, 64])
        ece = ar.get([128, 4, 64])
        t_dt = Tok()
        sl, tsl = load_slab("w1in", 0, 2048)
        for kc in range(16):
            kb.op("pe", lambda: PE.matmul(banks[0][:, 0:Wt], lhsT=sl[:, kc * 128:(kc + 1) * 128], rhs=hb[:, kc, 0:Wt], start=(kc == 0), stop=(kc == 15)),
                  reads=[tsl, t_hbg[kc // 4]], writes=[t_bank[0]])
        act(dtT[0:64, 0:Wt], banks[0][0:64, 0:Wt], AF.Exp, [t_bank[0], t_const], [t_dt], bias=dtb[0:64, 0:1])
        act(dtT[0:64, 0:Wt], dtT[0:64, 0:Wt], AF.Ln, [t_dt], [t_dt], bias=1.0)
        ts("dve", daT[0:64, 0:Wt], dtT[0:64, 0:Wt], aneg[0:64, 0:1], None, ALU.mult, None, [t_dt, t_const], [t_dt])
        for sj in range(nsub):
            L = min(128, Wt - sj * 128)
            cs_ = slice(sj * 128, sj * 128 + L)
            kb.op("pe", lambda: PE.transpose(banks[2][0:L, 0:64], dtT[0:64, cs_], ident[0:64, 0:64]), reads=[t_dt, t_const], writes=[t_bank[2]])
            kb.op("pe", lambda: PE.transpose(banks[2][0:L, 64:128], daT[0:64, cs_], ident[0:64, 0:64]), reads=[t_dt, t_const], writes=[t_bank[2]])
            cp("act", dt_tok[0:L, sj, :], banks[2][0:L, 0:64], [t_bank[2]], [t_dt])
            cp("dve", da_tok[0:L, sj, :], banks[2][0:L, 64:128], [t_bank[2]], [t_dt])
            kb.op("pe", lambda: PE.matmul(banks[3][0:L, 0:64], lhsT=U2[0:L, 0:L], rhs=da_tok[0:L, sj, :], start=True, stop=True),
                  reads=[t_dt, t_const], writes=[t_bank[3]])
            kb.op("pe", lambda: PE.matmul(banks[3][:, 64:128], lhsT=ones32[0:L, :], rhs=da_tok[0:L, sj, :], start=True, stop=True),
                  reads=[t_dt, t_const], writes=[t_bank[3]])
            cp("act", cs_tok[0:L, sj, :], banks[3][0:L, 0:64], [t_bank[3]], [t_dt])
            act(ece[:, sj, :], banks[3][:, 64:128], AF.Exp, [t_bank[3]], [t_dt])
            if ti > 0:
                kb.mute = m3
                tt("dve", dsum[:], dsum[:], banks[3][:, 64:128], ALU.add, [t_dsum, t_bank[3]], [t_dsum])
                kb.mute = m34
            tt("dve", wg_tok[0:L, sj, :], banks[3][0:L, 64:128], cs_tok[0:L, sj, :], ALU.subtract, [t_bank[3], t_dt], [t_dt])
            act(wg_tok[0:L, sj, :], wg_tok[0:L, sj, :], AF.Exp, [t_dt], [t_dt])
            tt("dve", wg_tok[0:L, sj, :], wg_tok[0:L, sj, :], dt_tok[0:L, sj, :], ALU.mult, [t_dt], [t_dt])

        chk("l1dt%d" % ti)
        xg = ar.get([128, 4, W])
        xgb = ar.get([128, 4, W], BF16)
        szg = ar.get([128, 4, W])
        yvg = ar.get([128, 4, W])
        BCf = ar.get([128, 2, W])
        BCb = ar.get([128, 2, W], BF16)
        xin = [ar.get([128, W + 4]) for _ in range(2)]
        cacc = [ar.get([128, W]) for _ in range(2)]
        xdt = ar.get([128, 512], BF16)
        xwg = ar.get([128, 512], BF16)
        Btok = ar.get([128, 128], BF16)
        CC = ar.get([128, 256])
        Lh = [ar.get([128, 256]) for _ in range(2)]
        Eh = [ar.get([128, 256]) for _ in range(2)]
        MC = [ar.get([128, 256], BF16) for _ in range(2)]
        sqg = ar.get([128, W])
        rsg = ar.get([128, W])
        t_xg, t_xgb, t_szg, t_yvg, t_BC = Tok(), Tok(), Tok(), Tok(), Tok()
        t_xin, t_cacc = [Tok(), Tok()], [Tok(), Tok()]
        t_xdt, t_xwg, t_Btok, t_CC = Tok(), Tok(), Tok(), Tok()
        t_Lh, t_Eh, t_MC = [Tok(), Tok()], [Tok(), Tok()], [Tok(), Tok()]
        t_sqg = Tok()
        cvi = [0]

        def conv_silu(b, ch, outs):
            i = cvi[0] % 2
            cvi[0] += 1
            xi, txi, ca, tca = xin[i], t_xin[i], cacc[i], t_cacc[i]
            cp("pool", xi[:, 0:3], halo[:, ch, 0:3], [t_halo], [txi])
            cp("act", xi[:, 3:3 + Wt], banks[b][:, 0:Wt], [t_bank[b]], [txi])
            cp("pool", halo[:, ch, 0:3], xi[:, Wt:Wt + 3], [txi], [t_halo])
            ts("dve", ca[:, 0:Wt], xi[:, 0:Wt], cw[:, ch, 0:1], None, ALU.mult, None, [txi, t_const], [tca])
            for k in range(1, 4):
                stt("dve" if k < 3 else "pool", ca[:, 0:Wt], xi[:, k:k + Wt], cw[:, ch, k:k + 1], ca[:, 0:Wt], ALU.mult, ALU.add,
                    [txi, t_const], [tca])
            for (o, to) in outs[:1]:
                act(o, ca[:, 0:Wt], AF.Silu, [tca, t_const], [to], bias=cb[:, ch:ch + 1])
            for (o, to) in outs[1:]:
                cp("pool", o, outs[0][0], [outs[0][1]], [to])

        for g in range(8):
            base = 1 + g * 10
            for r in range(10):
                mb = base + r
                kb.mute = m34 or (phase == 3 and 4 <= r < 8)
                sl, tsl = load_slab("w1in", mb, 2048)
                b = r % 2
                for kc in range(16):
                    kb.op("pe", lambda: PE.matmul(banks[b][:, 0:Wt], lhsT=sl[:, kc * 128:(kc + 1) * 128], rhs=hb[:, kc, 0:Wt],
                                                  start=(kc == 0), stop=(kc == 15)),
                          reads=[tsl, t_hb], writes=[t_bank[b]])
                if r < 4:
                    conv_silu(b, g * 4 + r, [(xg[:, r, 0:Wt], t_xg), (xgb[:, r, 0:Wt], t_xgb)])
                elif r < 8:
                    act(szg[:, r - 4, 0:Wt], banks[b][:, 0:Wt], AF.Silu, [t_bank[b]], [t_szg])
                else:
                    conv_silu(b, 32 + (r - 8) * 8 + g, [(BCf[:, r - 8, 0:Wt], t_BC), (BCb[:, r - 8, 0:Wt], t_BC)])
            kb.mute = m34
            if g == 0:
                chk("l1conv%d" % ti)
            for sj in range(nsub):
                L = min(128, Wt - sj * 128)
                cs_ = slice(sj * 128, sj * 128 + L)
                trb = banks[2].bitcast(BF16) if hasattr(banks[2], "bitcast") else None
                for c in range(4):
                    kb.op("pe", lambda: PE.transpose(trb[0:L, c * 128:(c + 1) * 128], xgb[:, c, cs_], identb), reads=[t_xgb, t_const], writes=[t_bank[2]])
                kb.op("pe", lambda: PE.transpose(trb[0:L, 512:640], BCb[:, 0, cs_], identb), reads=[t_BC, t_const], writes=[t_bank[2]])
                if g == 0 and sj == 0:
                    chk("ssd_a%d" % ti)
                kb.mute = m4
                tt("dve", xdt[0:L, :].rearrange("p (h q) -> p h q", h=8), trb[0:L, 0:512].rearrange("p (h q) -> p h q", h=8),
                   dt_tok[0:L, sj, g * 8:g * 8 + 8].unsqueeze(2).broadcast_to([L, 8, 64]), ALU.mult, [t_bank[2], t_dt], [t_xdt])
                kb.mute = m34
                tt("dve", xwg[0:L, :].rearrange("p (h q) -> p h q", h=8), trb[0:L, 0:512].rearrange("p (h q) -> p h q", h=8),
                   wg_tok[0:L, sj, g * 8:g * 8 + 8].unsqueeze(2).broadcast_to([L, 8, 64]), ALU.mult, [t_bank[2], t_dt], [t_xwg])
                cp("act", Btok[0:L, :], trb[0:L, 512:640], [t_bank[2]], [t_Btok])
                if g == 0 and sj == 0:
                    chk("ssd_b%d" % ti)
                kb.mute = m4
                kb.op("pe", lambda: PE.matmul(banks[3][0:L, 0:L], lhsT=BCb[:, 0, cs_], rhs=BCb[:, 1, cs_], start=True, stop=True),
                      reads=[t_BC], writes=[t_bank[3]])
                tt("dve", CC[0:L, 0:L], banks[3][0:L, 0:L], U2[0:L, 0:L], ALU.mult, [t_bank[3], t_const], [t_CC])
                cp("pool", CC[:, 128:128 + L], BCf[:, 1, cs_], [t_BC], [t_CC])
                if g == 0 and sj == 0:
                    chk("ssd_c%d" % ti)
                def st_a1(hh):
                    h = g * 8 + hh
                    i2 = hh % 2
                    lh = Lh[i2]
                    sgb = 4 if hh % 2 == 0 else 5
                    ts("dve", lh[0:L, :], U1o[0:L, :], da_tok[0:L, sj, h:h + 1], None, ALU.mult, None, [t_const, t_dt], [t_Lh[i2]])
                    kb.op("pe", lambda: PE.matmul(banks[sgb][0:L, 0:L], lhsT=lh[0:L, 0:L], rhs=U2[0:L, 0:L], start=True, stop=True),
                          reads=[t_Lh[i2], t_const], writes=[t_bank[sgb]])
                    kb.op("pe", lambda: PE.matmul(banks[sgb][:, 128:128 + L], lhsT=lh[0:L, 128:256], rhs=U2[0:L, 0:L], start=True, stop=True),
                          reads=[t_Lh[i2], t_const], writes=[t_bank[sgb]])

                def st_a2(hh):
                    i2 = hh % 2
                    eh, mc = Eh[i2], MC[i2]
                    sgb = 4 if hh % 2 == 0 else 5
                    act(eh[0:L, 0:L], banks[sgb][0:L, 0:L], AF.Exp, [t_bank[sgb]], [t_Eh[i2]])
                    act(eh[:, 128:128 + L], banks[sgb][:, 128:128 + L], AF.Exp, [t_bank[sgb]], [t_Eh[i2]])
                    tt("dve", mc[0:L, 0:L], eh[0:L, 0:L], CC[0:L, 0:L], ALU.mult, [t_Eh[i2], t_CC], [t_MC[i2]])
                    tt("pool", mc[:, 128:128 + L], eh[:, 128:128 + L], CC[:, 128:128 + L], ALU.mult, [t_Eh[i2], t_CC], [t_MC[i2]])

                def st_b(hh):
                    h = g * 8 + hh
                    i2 = hh % 2
                    mc = MC[i2]
                    yb = 6 + (hh // 4)
                    ycol = (hh % 4) * 128
                    pr = (hh // 2) * 128
                    kb.op("pe", lambda: PE.matmul(banks[yb][:, ycol:ycol + L], lhsT=xdt[0:L, pr:pr + 128], rhs=mc[0:L, 0:L], start=True, stop=False),
                          reads=[t_xdt, t_MC[i2]], writes=[t_bank[yb]])
                    kb.op("pe", lambda: PE.matmul(banks[yb][:, ycol:ycol + L], lhsT=STb[:, h - (h % 2):h - (h % 2) + 2, :].rearrange("p a b -> p (a b)"),
                                                  rhs=mc[:, 128:128 + L], start=False, stop=True),
                          reads=[t_STb, t_MC[i2]], writes=[t_bank[yb]])
                for st in range(8 + 1):
                    if st < 8:
                        st_a1(st)
                    if st - 1 >= 0:
                        st_a2(st - 1)
                        st_b(st - 1)
                if g == 0 and sj == 0:
                    chk("ssd_d%d" % ti)
                for hh in range(8):
                    yb = 6 + (hh // 4)
                    ycol = (hh % 4) * 128
                    c = hh // 2
                    pr = slice((hh % 2) * 64, (hh % 2) * 64 + 64)
                    stt("dve", yvg[pr, c, cs_], xg[pr, c, cs_], l1d[pr, g * 4 + c:g * 4 + c + 1], banks[yb][pr, ycol:ycol + L], ALU.mult, ALU.add,
                        [t_xg, t_bank[yb], t_const], [t_yvg])
                if g == 0 and sj == 0:
                    chk("ssd_e%d" % ti)
                kb.mute = m34
                kb.op("pe", lambda: PE.matmul(banks[3][:, :], lhsT=Btok[0:L, :], rhs=xwg[0:L, :], start=True, stop=True),
                      reads=[t_Btok, t_xwg], writes=[t_bank[3]])
                tt("dve", ST[:, g * 8:g * 8 + 8, :], ST[:, g * 8:g * 8 + 8, :], ece[:, sj, g * 8:g * 8 + 8].unsqueeze(2).broadcast_to([128, 8, 64]),
                   ALU.mult, [t_ST, t_dt], [t_ST])
                tt("dve", ST[:, g * 8:g * 8 + 8, :], ST[:, g * 8:g * 8 + 8, :], banks[3][:, :].rearrange("p (h q) -> p h q", h=8), ALU.add,
                   [t_ST, t_bank[3]], [t_ST])
                cp("act", STb[:, g * 8:g * 8 + 8, :], ST[:, g * 8:g * 8 + 8, :], [t_ST], [t_STb])
            if g == 0:
                chk("l1ssd%d" % ti)
            kb.mute = m4
            for c in range(4):
                tt("dve" if c % 2 == 0 else "pool", yvg[:, c, 0:Wt], yvg[:, c, 0:Wt], szg[:, c, 0:Wt], ALU.mult, [t_yvg, t_szg], [t_yvg])
                act(sqg[:, 0:Wt], yvg[:, c, 0:Wt], AF.Square, [t_yvg], [t_sqg])
                kb.op("pe", lambda: PE.matmul(banks[2][:, 0:Wt], lhsT=ones32, rhs=sqg[:, 0:Wt], start=(c == 0), stop=(c == 3)),
                      reads=[t_sqg, t_const], writes=[t_bank[2]])
            ts("dve", rsg[:, 0:Wt], banks[2][:, 0:Wt], 1.0 / 512.0, RMS_EPS, ALU.mult, ALU.add, [t_bank[2]], [t_sqg])
            act(rsg[:, 0:Wt], rsg[:, 0:Wt], AF.Sqrt, [t_sqg], [t_sqg])
            kb.op("dve", lambda: V.reciprocal(out=rsg[:, 0:Wt], in_=rsg[:, 0:Wt]), reads=[t_sqg], writes=[t_sqg])
            for c in range(4):
                stt("dve" if c % 2 == 0 else "pool", ynT[:, g * 4 + c, 0:Wt], yvg[:, c, 0:Wt], ng[:, g * 4 + c:g * 4 + c + 1], rsg[:, 0:Wt],
                    ALU.mult, ALU.mult, [t_yvg, t_sqg, t_const], [t_X])
        kb.mute = m3
        if ti == NT:
            for i_ in range(2):
                kb.dma("sp", STl_d[i_], ST[:, 32 * i_:32 * i_ + 32, :].rearrange("p h q -> p (h q)"), reads=[t_ST], writes=[t_STl])
            kb.dma("sp", dsl_d, dsum[:], reads=[t_dsum], writes=[t_dsl])
            for i_ in range(2):
                kb.collective("AllGather", GROUPS, STl_d[i_], STg_d[i_].rearrange("r p x -> (r p) x"), reads=[t_STl], writes=[t_STg])
            kb.collective("AllGather", GROUPS, dsl_d, dsg_d.rearrange("r p x -> (r p) x"), reads=[t_dsl], writes=[t_dsg])
        kb.mute = m4
        chk("l1mix%d" % ti)
        linear("w1out", lambda kc: ynT[:, kc, 0:Wt], [t_X], Wt, [0, 1, 2, 3], epi_res)
        kb.barrier()
        ar.reset()
        wk = [ar.get([128, W]) for _ in range(5)]
        layer_norm(Wt, 4, 5, 4, 5, wk)
        kb.barrier()
        ffn(Wt, "w1g", "w1u", "w1d")
        kb.barrier()
        ar.reset()
        wk = [ar.get([128, W]) for _ in range(5)]
        layer_norm(Wt, 6, 7, 4, 5, wk)
        if ti > 0:
            kb.dma("sp", outT[:, W * (ti - 1):W * ti].rearrange("(c p) w -> p c w", p=128), hT[:, :, 0:Wt], reads=[t_hT], writes=[t_out])
        dump("d_l1mix", pos0, Wt)
    kb.mute = False
    kb.barrier(final=True)
    return nc


def slab(Wm):
    K, M = Wm.shape
    return np.ascontiguousarray(Wm.reshape(K // 128, 128, M // 128, 128).transpose(2, 1, 0, 3)).reshape(M // 128, 128, K)


def pvec(v):
    return np.ascontiguousarray(v.reshape(-1, 128).T)


def prepare(inp, NT):
    f = np.float32
    NTOK = NMETA + W * NT
    com = {}
    w_in = inp["l0_w_in"]
    kr = w_in[:, 1792:1856]
    krs = np.concatenate([kr[:, 32:], kr[:, :32]], axis=1)
    com["w0in"] = slab(np.concatenate([w_in[:, :1792], kr, kr, krs, krs], axis=1))
    com["wglu"] = slab(inp["l0_s5_w_glu"])
    wuq = inp["l0_mla_w_uq"].reshape(512, 8, 192)
    cols = []
    for hp in range(4):
        h0, h1 = 2 * hp, 2 * hp + 1
        r0, r1 = wuq[:, h0, 128:], wuq[:, h1, 128:]
        sw = lambda r: np.concatenate([r[:, 32:], r[:, :32]], axis=1)
        cols += [wuq[:, h0, :128], wuq[:, h1, :128], np.concatenate([r0, r1], 1), np.concatenate([sw(r0), sw(r1)], 1)]
    com["wuq"] = slab(np.concatenate(cols, axis=1))
    wukv = inp["l0_mla_w_ukv"].reshape(256, 8, 256)
    com["wuk"] = slab(np.ascontiguousarray(wukv[:, :, :128]).reshape(256, 1024))
    com["wuv"] = np.ascontiguousarray(np.ascontiguousarray(wukv[:, :, 128:]).reshape(2, 128, 1024).transpose(1, 0, 2))
    com["w0out"] = slab(inp["l0_w_out"])
    com["w0g"] = slab(inp["l0_ffn_w_gate"])
    com["w0u"] = slab(inp["l0_ffn_w_up"])
    com["w0d0"] = slab(inp["l0_ffn_w_down"][:FF // 2])
    com["w0d1"] = slab(inp["l0_ffn_w_down"][FF // 2:])
    com["w1g"] = slab(inp["l1_ffn_w_gate"])
    com["w1u"] = slab(inp["l1_ffn_w_up"])
    com["w1d0"] = slab(inp["l1_ffn_w_down"][:FF // 2])
    com["w1d1"] = slab(inp["l1_ffn_w_down"][FF // 2:])
    w1 = inp["l1_w_in"]
    z, xs, Bm, Cm, dtc = w1[:, :4096], w1[:, 4096:8192], w1[:, 8192:9216], w1[:, 9216:10240], w1[:, 10240:]
    cols = [np.concatenate([dtc, np.zeros((2048, 64), f)], 1)]
    for g in range(8):
        cols += [xs[:, g * 512:(g + 1) * 512], z[:, g * 512:(g + 1) * 512], Bm[:, g * 128:(g + 1) * 128], Cm[:, g * 128:(g + 1) * 128]]
    com["w1in"] = slab(np.concatenate(cols, axis=1))
    com["w1out"] = slab(inp["l1_w_out"])
    com["vec"] = np.ascontiguousarray(np.stack([pvec(inp[k]) for k in (
        "l0_ln1_g", "l0_ln1_b", "l0_ln2_g", "l0_ln2_b", "l1_ln1_g", "l1_ln1_b", "l1_ln2_g", "l1_ln2_b")], axis=1))
    com["s5d"] = pvec(inp["l0_s5_d"])
    com["qn"] = pvec(inp["l0_mla_q_norm"])
    com["kvn"] = pvec(inp["l0_mla_kv_norm"])
    cwm = inp["l1_conv_w"]
    com["cw"] = np.ascontiguousarray(cwm.T.reshape(48, 128, 4).transpose(1, 0, 2))
    com["cb"] = pvec(inp["l1_conv_b"])
    dtb = np.zeros((128, 2), f)
    dtb[:64, 0] = inp["l1_dt_bias"]
    dtb[:64, 1] = inp["l1_a_log"]
    com["dtb"] = dtb
    com["l1d"] = pvec(np.repeat(inp["l1_d"], 64))
    com["ng"] = pvec(inp["l1_norm_g"])
    ldt = inp["l0_s5_log_dt"]
    are, aim = inp["l0_s5_a_re"], inp["l0_s5_a_im"]
    pl = lambda a: np.ascontiguousarray(a.reshape(32, 128).T)
    com["s5p"] = np.ascontiguousarray(np.stack([pl(np.repeat(ldt[:, None], 64, 1)), pl(are), pl(aim)], axis=1))
    row = lambda a: np.ascontiguousarray(np.broadcast_to(a.reshape(1, 32, 128), (128, 32, 128)))
    com["s5r"] = np.stack([row(np.repeat(ldt[:, None], 64, 1)), row(are), row(aim)], axis=0)
    bl = np.zeros((2, 128, 32, 128), f)
    cl = np.zeros((32, 128, 2, 128), f)
    for g in range(64):
        j, a = g // 2, g % 2
        q = j % 4
        r0 = 32 * q + 16 * a
        for c_, (bsrc, csrc) in enumerate(((inp["l0_s5_b_re"], inp["l0_s5_c_re"]), (inp["l0_s5_b_im"], inp["l0_s5_c_im"]))):
            bl[c_, r0:r0 + 16, j, 64 * a:64 * a + 64] = bsrc[g].T
            cl[j, 64 * a:64 * a + 64, c_, r0:r0 + 16] = csrc[g].T
    com["s5b"] = bl
    com["s5c"] = cl
    k = np.arange(128)
    cst = np.zeros((128, 640), f)
    cst[:, 0:128] = np.eye(128)
    cst[:, 128:256] = (k[:, None] > k[None, :])
    cst[:, 256:384] = 1.0
    cst[:, 384:512] = (k[:, None] <= k[None, :])
    cst[:, 512:640] = 1.0
    com["cst"] = cst
    com["iota"] = np.ascontiguousarray(np.broadcast_to(np.arange(W, dtype=f)[None], (128, W)))
    com = {k_: np.ascontiguousarray(v, dtype=f) for k_, v in com.items()}
    NFR = W * NT
    inv = (10000.0 ** (-np.arange(0, 64, 2, dtype=f) / 64)).astype(f)
    maps = []
    for b in range(inp["x"].shape[0]):
        for c in range(4):
            m = dict(com)
            m["xT"] = np.ascontiguousarray(np.concatenate([inp["meta_tokens"].T, inp["x"][b, NFR * c:NFR * (c + 1)].T], axis=1), dtype=f)
            pos = np.concatenate([np.arange(NMETA), NMETA + NFR * c + np.arange(NFR)]).astype(f)
            ang = (pos[None, :] * inv[:, None]).astype(f)
            c32, s32 = np.cos(ang).astype(f), np.sin(ang).astype(f)
            m["ropeC"] = np.ascontiguousarray(np.concatenate([c32, c32, c32, c32], 0))
            m["ropeS"] = np.ascontiguousarray(np.concatenate([-s32, s32, -s32, s32], 0))
            sel = np.zeros((128, 8), f)
            sel[:, c] = 1.0
            m["sel"] = sel
            segb = np.zeros((128, 4), f)
            segb[:, c:] = -30000.0
            m["segb"] = segb
            maps.append(m)
    return maps


def kernel(**inputs):
    inp = {k: np.asarray(v) for k, v in inputs.items()}
    NT = SEQ // W // 4
    nc = build(NT)
    maps = prepare(inp, NT)
    res = run_bass_kernel_spmd(nc, maps, core_ids=list(range(8)))
    out = np.stack([np.concatenate([np.ascontiguousarray(res.results[b * 4 + c]["outT"].T) for c in range(4)], axis=0)
                    for b in range(inp["x"].shape[0])], axis=0)
    return out.astype(np.float32)
```

```python
import math
import numpy as np
import concourse.bass as bass
import concourse.mybir as mybir
from concourse.bass_utils import run_bass_kernel_spmd

F32 = mybir.dt.float32
BF16 = mybir.dt.bfloat16
I32 = mybir.dt.int32
AF = mybir.ActivationFunctionType
ALU = mybir.AluOpType

D = 2048
NMETA = 16
SEQ = 8192
W = 512
FF = 5632
FC = FF // 128
ALPHA = 4.0 ** 0.25
LN_EPS = 1e-5
RMS_EPS = 1e-6
ATT_SCALE = 192.0 ** -0.5
TWO_PI = 6.283185
GELU_C = 1.5957691216057308


class Tok:
    __slots__ = ("w", "r", "excl")

    def __init__(self, excl=False):
        self.w = None
        self.r = {}
        self.excl = excl


class KB:
    NRING = 8

    def __init__(self, nc):
        self.nc = nc
        self.E = {"pe": nc.tensor, "act": nc.scalar, "dve": nc.vector, "pool": nc.gpsimd, "sp": nc.sync}
        self.sems = {}
        self.cnt = {}
        for e in ("pe", "act", "dve", "pool"):
            self.sems[e] = nc.alloc_semaphore("s_" + e)
            self.cnt[e] = 0
        self.seen = {e: {} for e in self.E}
        self.ring = {}
        self.ring_i = {}
        for q in ("sp", "pool"):
            ks = []
            for i in range(self.NRING):
                k = "d_%s_%d" % (q, i)
                self.sems[k] = nc.alloc_semaphore(k)
                self.cnt[k] = 0
                ks.append(k)
            self.ring[q] = ks
            self.ring_i[q] = 0
        self.ninst = 0
        self.dead = False
        self.mute = False
        self.ncc = 0

    def _wait(self, eng, key, val):
        if val <= 0:
            return
        if key == eng and eng == "pe":
            return
        s = self.seen[eng]
        if s.get(key, 0) >= val:
            return
        s[key] = val
        self.E[eng].wait_ge(self.sems[key], val)

    def _deps(self, eng, reads, writes):
        for t in reads:
            if t.w is not None:
                self._wait(eng, t.w[0], t.w[1])
            if t.excl:
                for k, v in t.r.items():
                    if k != eng:
                        self._wait(eng, k, v)
        for t in writes:
            if t.w is not None:
                self._wait(eng, t.w[0], t.w[1])
            for k, v in t.r.items():
                if k == eng:
                    continue
                self._wait(eng, k, v)

    def _mark(self, ev, reads, writes):
        for t in reads:
            if t.r.get(ev[0], 0) < ev[1]:
                t.r[ev[0]] = ev[1]
        for t in writes:
            t.w = ev
            t.r = {}

    def op(self, eng, fn, reads=(), writes=()):
        if self.dead or self.mute:
            return None
        self._deps(eng, reads, writes)
        inst = fn()
        self.cnt[eng] += 1
        inst.then_inc(self.sems[eng], 1)
        self._mark((eng, self.cnt[eng]), reads, writes)
        self.ninst += 1
        return inst

    def dma(self, q, out, in_, reads=(), writes=()):
        if self.dead or self.mute:
            return None
        i = self.ring_i[q]
        self.ring_i[q] = i + 1
        key = self.ring[q][i % self.NRING]
        self._wait(q, key, self.cnt[key])
        self._deps(q, reads, writes)
        inst = self.E[q].dma_start(out=out, in_=in_)
        self.cnt[key] += 16
        inst.then_inc(self.sems[key], 16)
        self._mark((key, self.cnt[key]), reads, writes)
        self.ninst += 1
        return inst

    def collective(self, kind, groups, in_ap, out_ap, reads=(), writes=()):
        if self.dead or self.mute:
            return None
        key = "cc%d" % self.ncc
        self.ncc += 1
        self.sems[key] = self.nc.alloc_semaphore(key)
        self.cnt[key] = 0
        self._deps("pool", reads, writes)
        inst = self.nc.gpsimd.collective_compute(kind, ALU.bypass, replica_groups=groups, ins=[in_ap], outs=[out_ap])
        self.cnt[key] = 1
        inst.then_inc(self.sems[key], 1)
        self._mark((key, 1), reads, writes)
        return inst

    def barrier(self, final=False):
        if self.dead or self.mute:
            return
        for e in ("pe", "act", "dve", "pool", "sp"):
            for k in self.sems:
                if k.startswith("d_pool") and not final:
                    continue
                self._wait(e, k, self.cnt[k])


def build(NT, dbg=False, stop=None):
    NTOK = NMETA + W * NT
    NKT = 1 + 4 * NT
    NFR = W * NT
    NPK = 4 * NT
    GROUPS = [[0, 1, 2, 3], [4, 5, 6, 7]]
    nc = bass.Bass("TRN2", target_bir_lowering=False)
    kb = KB(nc)
    V = nc.vector
    A = nc.scalar
    P = nc.gpsimd
    PE = nc.tensor

    def chk(name):
        if stop == name and not kb.dead:
            kb.barrier()
            kb.dead = True
            print("STOP at", name, "ninst", kb.ninst)

    def din(name, shape, dt=F32):
        return nc.dram_tensor(name, list(shape), dt, kind="ExternalInput").ap()

    def dscr(name, shape, dt=BF16):
        return nc.dram_tensor(name, list(shape), dt, kind="Internal").ap()

    xT = din("xT", [D, NTOK])
    outT = nc.dram_tensor("outT", [D, W * NT], F32, kind="ExternalOutput").ap()
    ropeC_d = din("ropeC", [128, NTOK])
    ropeS_d = din("ropeS", [128, NTOK])
    cst_d = din("cst", [128, 5 * 128])
    iota_d = din("iota", [128, W])
    WSPEC = {
        "w0in": (16, 2048), "wglu": (8, 1024), "wuq": (16, 512), "wuk": (8, 256),
        "w0out": (16, 2048), "w0g": (FC, 2048), "w0u": (FC, 2048), "w0d0": (16, FF // 2), "w0d1": (16, FF // 2),
        "w1in": (81, 2048), "w1out": (16, 4096), "w1g": (FC, 2048), "w1u": (FC, 2048), "w1d0": (16, FF // 2), "w1d1": (16, FF // 2),
    }
    wf = {}
    wbf = {}
    for n, (mb, k) in WSPEC.items():
        wf[n] = din(n, [mb, 128, k])
        wbf[n] = dscr(n + "_b", [mb, 128, k])
    wuv_d = din("wuv", [128, 2, 1024])
    vec_d = din("vec", [128, 8, 16])
    s5d_d = din("s5d", [128, 8])
    qn_d = din("qn", [128, 4])
    kvn_d = din("kvn", [128, 2])
    cw_d = din("cw", [128, 48, 4])
    cb_d = din("cb", [128, 48])
    dtb_d = din("dtb", [128, 2])
    l1d_d = din("l1d", [128, 32])
    ng_d = din("ng", [128, 32])
    s5p_d = din("s5p", [128, 3, 32])
    s5r_d = din("s5r", [3, 128, 32, 128])
    s5b_d = din("s5b", [2, 128, 32, 128])
    s5c_d = din("s5c", [32, 128, 2, 128])
    BL_d = dscr("BL", [32, 128, 2, 128])
    CL_d = dscr("CL", [32, 128, 3, 128])
    TAB_d = dscr("TAB", [32, 128, 2, W], F32)
    Kc_d = [dscr("Kc%d" % h, [128, NTOK]) for h in range(8)]
    Kg_d = [dscr("Kg%d" % h, [4, 128, NTOK]) for h in range(8)]
    Vm_d = dscr("Vm", [128, 1024])
    Vf_d = [dscr("Vf%d" % q, [128, 4, 1024]) for q in range(NT)]
    Vgf_d = [dscr("Vgf%d" % q, [4, 128, 4, 1024]) for q in range(NT)]
    KR_d = dscr("KR", [128, NTOK])
    KRg_d = dscr("KRg", [4, 128, NTOK])
    Gl_d = dscr("Gl", [128, 64], F32)
    Gg_d = dscr("Gg", [4, 128, 64], F32)
    Hl_d = dscr("Hl", [128, 192], F32)
    Hg_d = dscr("Hg", [4, 128, 192], F32)
    STl_d = [dscr("STl%d" % i, [128, 2048], F32) for i in range(2)]
    STg_d = [dscr("STg%d" % i, [4, 128, 2048], F32) for i in range(2)]
    STin_d = dscr("STin", [128, 4096], F32)
    dsl_d = dscr("dsl", [128, 64], F32)
    dsg_d = dscr("dsg", [4, 128, 64], F32)
    H0_d = dscr("H0", [D, NTOK], F32)
    sel_d = din("sel", [128, 8])
    segb_d = din("segb", [128, 4])
    dbg_d = {}
    if dbg:
        for n in ("d_l0", "d_mix", "d_l0a", "d_l1mix"):
            dbg_d[n] = nc.dram_tensor(n, [D, NTOK], F32, kind="ExternalOutput").ap()

    def sb(name, shape, dt=F32):
        return nc.alloc_sbuf_tensor("sb_" + name, list(shape), dt)

    hT = sb("hT", [128, 16, W])
    hb = sb("hb", [128, 16, W], BF16)
    X = sb("X", [128, 32, W], BF16)
    NSLAB = 3
    slabs = [sb("slab%d" % i, [128, 4096], BF16) for i in range(NSLAB)]
    t_slab = [Tok() for _ in range(NSLAB)]
    slab_i = [0]
    ST = sb("ST", [128, 64, 64])
    STb = sb("STb", [128, 64, 64], BF16)
    cst = sb("cst32", [128, 5 * 128])
    cstb = sb("cstb", [128, 5 * 128], BF16)
    iota = sb("iota", [128, W])
    wuv = sb("wuv", [128, 2, 1024], BF16)
    vec = sb("vec", [128, 8, 16])
    s5d = sb("s5d", [128, 8])
    qn = sb("qn", [128, 4])
    kvn_g = sb("kvn_g", [128, 2])
    cw = sb("cw", [128, 48, 4])
    cb = sb("cb", [128, 48])
    dtb = sb("dtb", [128, 2])
    aneg = sb("aneg", [128, 1])
    l1d = sb("l1d", [128, 32])
    ng = sb("ng", [128, 32])
    halo = sb("halo", [128, 48, 4])
    halo_in = sb("halo_in", [128, 48, 4])
    Gin = sb("Gin", [128, 2, 32])
    ANc = sb("ANc", [128, 2, 32])
    sel = sb("sel", [128, 8])
    segb = sb("segb", [128, 4])
    dsum = sb("dsum", [128, 64])
    s5R = sb("s5R", [128, 32])
    s5F = sb("s5F", [128, 32])
    rotc = sb("rotc", [128, 2, 32])
    rots = sb("rots", [128, 2, 32])
    ZR = sb("ZR", [128, 32])
    ZI = sb("ZI", [128, 32])
    ZRi = sb("ZRi", [128, 32])
    ZIi = sb("ZIi", [128, 32])
    ztmp = sb("ztmp", [128, 4, 32])
    t_const = Tok()
    t_hT, t_hb, t_X = Tok(), Tok(), Tok()
    t_ST, t_STb, t_halo, t_Z = Tok(), Tok(), Tok(), Tok()
    ident = cst[:, 0:128]
    U1o = cst[:, 128:384]
    U2 = cst[:, 384:512]
    ones32 = cst[:, 512:640]
    identb = cstb[:, 0:128]
    onesb = cstb[:, 512:640]

    a0, a1 = nc.bump_sbuf(nc.sbuf_bytes_remaining - 64)
    ARENA = a1 - a0

    class Arena:
        def __init__(self):
            self.off = 0
            self.n = 0

        def reset(self, off=0):
            self.off = off

        def get(self, shape, dt=F32):
            size = int(np.prod(shape[1:])) * (4 if dt in (F32, I32) else 2)
            size = (size + 31) // 32 * 32
            assert self.off + size <= ARENA, ("arena overflow", self.off, size, ARENA)
            self.n += 1
            t = nc.alloc_sbuf_tensor_at("ar%d" % self.n, list(shape), dt, offset=a0 + self.off)
            self.off += size
            return t

    ar = Arena()

    banks = [nc.alloc_psum_tensor("bank%d" % i, [128, 512], F32) for i in range(8)]
    t_bank = [Tok(excl=True) for _ in range(8)]

    def load_slab(name, mb, K):
        i = slab_i[0] % NSLAB
        slab_i[0] += 1
        kb.dma("sp", slabs[i][:, 0:K], wbf[name][mb], reads=[t_wbn[name]], writes=[t_slab[i]])
        return slabs[i], t_slab[i]

    def linear(names, rhs_fn, rhs_toks, Wt, bank_ids, epi, skip=()):
        if isinstance(names, str):
            names = [names]
        MB = WSPEC[names[0]][0]
        KCT = sum(WSPEC[n][1] // 128 for n in names)
        for mb in range(MB):
            if mb in skip:
                continue
            b = bank_ids[mb % len(bank_ids)]
            k0 = 0
            for n in names:
                K = WSPEC[n][1]
                sl, tsl = load_slab(n, mb, K)
                for kc in range(K // 128):
                    kb.op("pe", lambda: PE.matmul(banks[b][:, 0:Wt], lhsT=sl[:, kc * 128:(kc + 1) * 128], rhs=rhs_fn(k0 + kc),
                                                  start=(k0 + kc == 0), stop=(k0 + kc == KCT - 1)),
                          reads=[tsl] + (rhs_toks(k0 + kc) if callable(rhs_toks) else rhs_toks), writes=[t_bank[b]])
                k0 += K // 128
            epi(mb, b)

    t_hTg = [Tok() for _ in range(4)]
    t_hbg = [Tok() for _ in range(4)]

    def load_tile(src_ap, Wt, src_toks):
        for g_ in range(4):
            kb.dma("sp", hT[:, 4 * g_:4 * g_ + 4, 0:Wt], src_ap[4 * g_ * 128:(4 * g_ + 4) * 128, :].rearrange("(c p) w -> p c w", p=128),
                   reads=src_toks, writes=[t_hT, t_hTg[g_]])
        for g_ in range(4):
            cp("dve" if g_ % 2 == 0 else "pool", hb[:, 4 * g_:4 * g_ + 4, 0:Wt], hT[:, 4 * g_:4 * g_ + 4, 0:Wt], [t_hTg[g_]], [t_hbg[g_], t_hb])

    def act(out, in_, func, reads, writes, eng="act", **kw):
        return kb.op("act", lambda: A.activation(out=out, in_=in_, func=func, **kw), reads=reads, writes=writes)

    nopool = [False]

    def tt(eng, out, in0, in1, op, reads, writes):
        if nopool[0]:
            eng = "dve"
        e = V if eng == "dve" else P
        return kb.op(eng, lambda: e.tensor_tensor(out=out, in0=in0, in1=in1, op=op), reads=reads, writes=writes)

    def ts(eng, out, in0, s1, s2, op0, op1, reads, writes):
        if nopool[0]:
            eng = "dve"
        e = V if eng == "dve" else P
        if op1 is None:
            return kb.op(eng, lambda: e.tensor_scalar(out=out, in0=in0, scalar1=s1, scalar2=None, op0=op0), reads=reads, writes=writes)
        return kb.op(eng, lambda: e.tensor_scalar(out=out, in0=in0, scalar1=s1, scalar2=s2, op0=op0, op1=op1), reads=reads, writes=writes)

    def stt(eng, out, in0, scalar, in1, op0, op1, reads, writes):
        eng = "dve"
        e = V
        return kb.op(eng, lambda: e.scalar_tensor_tensor(out=out, in0=in0, scalar=scalar, in1=in1, op0=op0, op1=op1), reads=reads, writes=writes)

    def cp(eng, out, in_, reads, writes):
        if eng == "act":
            return kb.op("act", lambda: A.copy(out=out, in_=in_), reads=reads, writes=writes)
        if nopool[0]:
            eng = "dve"
        e = V if eng == "dve" else P
        return kb.op(eng, lambda: e.tensor_copy(out=out, in_=in_), reads=reads, writes=writes)

    def frac_sin(out, x, tmpi, tmpf, toks, eng="dve"):
        cp(eng, tmpi, x, toks, toks)
        cp(eng, tmpf, tmpi, toks, toks)
        tt(eng, tmpf, x, tmpf, ALU.subtract, toks, toks)
        act(out, tmpf, AF.Sin, toks, toks, scale=TWO_PI)

    t_wb = Tok()
    t_wbn = {n: Tok() for n in WSPEC}
    kb.dma("pool", wuv[:], wuv_d, writes=[t_const])
    for n in ("w0in", "wuk", "wuq", "wglu", "w0out", "w0g", "w0u", "w0d0", "w0d1", "w1in", "w1out", "w1g", "w1u", "w1d0", "w1d1"):
        for m in range(WSPEC[n][0]):
            kb.dma("pool", wbf[n][m], wf[n][m], writes=[t_wbn[n]])
    for (dst, src) in ((cst, cst_d), (iota, iota_d), (vec, vec_d), (s5d, s5d_d), (qn, qn_d), (kvn_g, kvn_d),
                       (cw, cw_d), (cb, cb_d), (dtb, dtb_d), (l1d, l1d_d), (ng, ng_d), (sel, sel_d), (segb, segb_d)):
        kb.dma("sp", dst[:], src, writes=[t_const])
    cp("dve", cstb[:], cst[:], [t_const], [t_const])
    act(aneg[:], dtb[:, 1:2], AF.Exp, [t_const], [t_const])
    ts("dve", aneg[:], aneg[:], -1.0, None, ALU.mult, None, [t_const], [t_const])
    kb.op("dve", lambda: V.memset(ST[:], 0.0), writes=[t_ST])
    kb.op("dve", lambda: V.memset(STb[:], 0.0), writes=[t_STb])
    kb.op("dve", lambda: V.memset(halo[:], 0.0), writes=[t_halo])
    kb.op("dve", lambda: V.memset(ZR[:], 0.0), writes=[t_Z])
    kb.op("dve", lambda: V.memset(ZI[:], 0.0), writes=[t_Z])

    ar.reset()
    t_s = Tok()
    sp_ = ar.get([128, 3, 32])
    kb.dma("sp", sp_[:], s5p_d, writes=[t_s])
    dtp = ar.get([128, 32])
    xim = ar.get([128, 32])
    tmpi = ar.get([128, 32], I32)
    tmpf = ar.get([128, 32])
    tmpx = ar.get([128, 32])
    act(dtp[:], sp_[:, 0, :], AF.Exp, [t_s], [t_s])
    tt("dve", tmpx[:], dtp[:], sp_[:, 1, :], ALU.mult, [t_s], [t_s])
    act(s5R[:], tmpx[:], AF.Exp, [t_s], [t_s, t_const])
    tt("dve", xim[:], dtp[:], sp_[:, 2, :], ALU.mult, [t_s], [t_s])
    ts("dve", s5F[:], xim[:], 1.0 / (2 * math.pi), None, ALU.mult, None, [t_s], [t_s, t_const])
    for wi, wd in enumerate((16.0, float(W))):
        ts("dve", tmpx[:], s5F[:], wd, None, ALU.mult, None, [t_s], [t_s])
        frac_sin(rots[:, wi, :], tmpx[:], tmpi[:], tmpf[:], [t_s])
        ts("dve", tmpx[:], s5F[:], wd, 0.25, ALU.mult, ALU.add, [t_s], [t_s])
        frac_sin(rotc[:, wi, :], tmpx[:], tmpi[:], tmpf[:], [t_s])
    tt("dve", tmpx[:], dtp[:], sp_[:, 1, :], ALU.mult, [t_s], [t_s])
    act(tmpf[:], tmpx[:], AF.Exp, [t_s], [t_s], scale=float(NFR))
    cN, sN, tA, tB = ar.get([128, 32]), ar.get([128, 32]), ar.get([128, 32]), ar.get([128, 32])
    cp("dve", cN[:], rotc[:, 1, :], [t_s], [t_s])
    cp("dve", sN[:], rots[:, 1, :], [t_s], [t_s])
    for _ in range(int(round(math.log2(NT)))):
        tt("dve", tA[:], cN[:], cN[:], ALU.mult, [t_s], [t_s])
        tt("dve", tB[:], sN[:], sN[:], ALU.mult, [t_s], [t_s])
        tt("dve", sN[:], cN[:], sN[:], ALU.mult, [t_s], [t_s])
        ts("dve", sN[:], sN[:], 2.0, None, ALU.mult, None, [t_s], [t_s])
        tt("dve", cN[:], tA[:], tB[:], ALU.subtract, [t_s], [t_s])
    tt("dve", ANc[:, 0, :], tmpf[:], cN[:], ALU.mult, [t_s], [t_s, t_const])
    tt("dve", ANc[:, 1, :], tmpf[:], sN[:], ALU.mult, [t_s], [t_s, t_const])
    tabw = [ar.get([128, 2, W]) for _ in range(2)]
    t_tabw = [Tok(), Tok()]
    xw_ = ar.get([128, W])
    xi_ = ar.get([128, W], I32)
    xf_ = ar.get([128, W])
    t_tab = Tok()
    for j in range(32):
        tb, ttb = tabw[j % 2], t_tabw[j % 2]
        ts("dve", xw_[:], iota[:], s5F[:, j:j + 1], None, ALU.mult, None, [t_s, t_const], [t_s])
        cp("dve", xi_[:], xw_[:], [t_s], [t_s])
        cp("dve", xf_[:], xi_[:], [t_s], [t_s])
        tt("dve", xf_[:], xw_[:], xf_[:], ALU.subtract, [t_s], [t_s])
        act(tb[:, 1, :], xf_[:], AF.Sin, [t_s], [ttb], scale=TWO_PI)
        ts("dve", xw_[:], xw_[:], 0.25, None, ALU.add, None, [t_s], [t_s])
        cp("dve", xi_[:], xw_[:], [t_s], [t_s])
        cp("dve", xf_[:], xi_[:], [t_s], [t_s])
        tt("dve", xf_[:], xw_[:], xf_[:], ALU.subtract, [t_s], [t_s])
        act(tb[:, 0, :], xf_[:], AF.Sin, [t_s], [ttb], scale=TWO_PI)
        kb.dma("sp", TAB_d[j], tb[:], reads=[ttb], writes=[t_tab])
    kb.barrier()
    ar.reset()
    RW = 8 * 128
    r_ = [ar.get([128, RW]) for _ in range(3)]
    braw = [ar.get([128, RW]) for _ in range(2)]
    e_ = [ar.get([128, RW]) for _ in range(8)]
    ei_ = ar.get([128, RW], I32)
    bout = ar.get([128, 8, 2, 128], BF16)
    t_r = Tok()
    for q4 in range(4):
        js = slice(q4 * 8, q4 * 8 + 8)
        for i in range(3):
            kb.dma("sp", r_[i][:].rearrange("p (j s) -> p j s", j=8), s5r_d[i][:, js, :], writes=[t_r])
        for i in range(2):
            kb.dma("sp", braw[i][:].rearrange("p (j s) -> p j s", j=8), s5b_d[i][:, js, :], writes=[t_r])
        T = [t_r]
        dt_, xre, xim_, mag, cs_, sn_, t7, t8 = [e[:] for e in e_]
        act(dt_, r_[0][:], AF.Exp, T, T)
        tt("dve", xre, dt_, r_[1][:], ALU.mult, T, T)
        act(mag, xre, AF.Exp, T, T)
        tt("dve", xim_, dt_, r_[2][:], ALU.mult, T, T)
        ts("dve", xim_, xim_, 1.0 / (2 * math.pi), None, ALU.mult, None, T, T)
        frac_sin(sn_, xim_, ei_[:], t7, T)
        ts("dve", xim_, xim_, 0.25, None, ALU.add, None, T, T)
        frac_sin(cs_, xim_, ei_[:], t7, T)
        abre, abim = cs_, sn_
        tt("dve", abre, mag, cs_, ALU.mult, T, T)
        tt("dve", abim, mag, sn_, ALU.mult, T, T)
        den = dt_
        tt("dve", den, r_[1][:], r_[1][:], ALU.mult, T, T)
        tt("dve", t7, r_[2][:], r_[2][:], ALU.mult, T, T)
        tt("dve", den, den, t7, ALU.add, T, T)
        kb.op("dve", lambda: V.reciprocal(out=den, in_=den), reads=T, writes=T)
        nr = xre
        ts("dve", nr, abre, -1.0, None, ALU.add, None, T, T)
        fre, fim = mag, xim_
        tt("dve", t7, nr, r_[1][:], ALU.mult, T, T)
        tt("dve", t8, abim, r_[2][:], ALU.mult, T, T)
        tt("dve", t7, t7, t8, ALU.add, T, T)
        tt("dve", fre, t7, den, ALU.mult, T, T)
        tt("dve", t7, abim, r_[1][:], ALU.mult, T, T)
        tt("dve", t8, nr, r_[2][:], ALU.mult, T, T)
        tt("dve", t7, t7, t8, ALU.subtract, T, T)
        tt("dve", fim, t7, den, ALU.mult, T, T)
        tt("dve", t7, fre, braw[0][:], ALU.mult, T, T)
        tt("dve", t8, fim, braw[1][:], ALU.mult, T, T)
        tt("dve", bout[:, :, 0, :], t7.rearrange("p (j s) -> p j s", j=8), t8.rearrange("p (j s) -> p j s", j=8), ALU.subtract, T, T)
        tt("dve", t7, fre, braw[1][:], ALU.mult, T, T)
        tt("dve", t8, fim, braw[0][:], ALU.mult, T, T)
        tt("dve", bout[:, :, 1, :], t7.rearrange("p (j s) -> p j s", j=8), t8.rearrange("p (j s) -> p j s", j=8), ALU.add, T, T)
        kb.dma("sp", BL_d[js].rearrange("j p c s -> p j c s"), bout[:], reads=T, writes=[t_tab])
    kb.barrier()
    ar.reset()
    craw = ar.get([128, 8, 2, 128])
    cout = ar.get([128, 8, 3, 128], BF16)
    t_c = Tok()
    for q4 in range(4):
        js = slice(q4 * 8, q4 * 8 + 8)
        kb.dma("sp", craw[:], s5c_d[js].rearrange("j p c s -> p j c s"), writes=[t_c])
        cp("dve", cout[:, :, 0, :], craw[:, :, 0, :], [t_c], [t_c])
        ts("dve", cout[:, :, 1, :], craw[:, :, 0, :], -1.0, None, ALU.mult, None, [t_c], [t_c])
        ts("dve", cout[:, :, 2, :], craw[:, :, 1, :], -1.0, None, ALU.mult, None, [t_c], [t_c])
        kb.dma("sp", CL_d[js].rearrange("j p c s -> p j c s"), cout[:], reads=[t_c], writes=[t_tab])
    kb.barrier()
    chk("setup")

    def layer_norm(Wt, gi, bi, bq_s1, bq_s2, wk):
        sq, mean, rstd, nmr, tmp = wk
        tw = Tok()
        for c in range(16):
            act(sq[:, 0:Wt], hT[:, c, 0:Wt], AF.Square, [t_hT], [tw])
            kb.op("pe", lambda: PE.matmul(banks[bq_s1][:, 0:Wt], lhsT=ones32, rhs=hT[:, c, 0:Wt], start=(c == 0), stop=(c == 15)),
                  reads=[t_hT, t_const], writes=[t_bank[bq_s1]])
            kb.op("pe", lambda: PE.matmul(banks[bq_s2][:, 0:Wt], lhsT=ones32, rhs=sq[:, 0:Wt], start=(c == 0), stop=(c == 15)),
                  reads=[tw, t_const], writes=[t_bank[bq_s2]])
        ts("dve", mean[:, 0:Wt], banks[bq_s1][:, 0:Wt], 1.0 / D, None, ALU.mult, None, [t_bank[bq_s1]], [tw])
        tt("dve", tmp[:, 0:Wt], mean[:, 0:Wt], mean[:, 0:Wt], ALU.mult, [tw], [tw])
        stt("dve", tmp[:, 0:Wt], banks[bq_s2][:, 0:Wt], 1.0 / D, tmp[:, 0:Wt], ALU.mult, ALU.subtract, [t_bank[bq_s2], tw], [tw])
        ts("dve", tmp[:, 0:Wt], tmp[:, 0:Wt], LN_EPS, None, ALU.add, None, [tw], [tw])
        act(tmp[:, 0:Wt], tmp[:, 0:Wt], AF.Sqrt, [tw], [tw])
        kb.op("dve", lambda: V.reciprocal(out=rstd[:, 0:Wt], in_=tmp[:, 0:Wt]), reads=[tw], writes=[tw])
        stt("dve", nmr[:, 0:Wt], mean[:, 0:Wt], -1.0, rstd[:, 0:Wt], ALU.mult, ALU.mult, [tw], [tw])
        for c in range(16):
            e = "dve" if c % 2 == 0 else "pool"
            tt(e, hT[:, c, 0:Wt], hT[:, c, 0:Wt], rstd[:, 0:Wt], ALU.mult, [tw, t_hT], [t_hT])
            tt(e, hT[:, c, 0:Wt], hT[:, c, 0:Wt], nmr[:, 0:Wt], ALU.add, [tw, t_hT], [t_hT])
            act(hT[:, c, 0:Wt], hT[:, c, 0:Wt], AF.Identity, [t_hT, t_const], [t_hT],
                scale=vec[:, gi, c:c + 1], bias=vec[:, bi, c:c + 1])
            cp("pool" if c % 2 == 0 else "dve", hb[:, c, 0:Wt], hT[:, c, 0:Wt], [t_hT], [t_hb])

    def ffn(Wt, gname, uname, dname):
        ar.reset()
        actb = ar.get([128, FC, W], BF16)
        sa = [ar.get([128, W]) for _ in range(2)]
        t_act, t_sa = Tok(), [Tok(), Tok()]
        for fb in range(FC):
            bg, bu = (0, 1) if fb % 2 == 0 else (2, 3)
            for (nm, b) in ((gname, bg), (uname, bu)):
                sl, tsl = load_slab(nm, fb, 2048)
                for kc in range(16):
                    kb.op("pe", lambda: PE.matmul(banks[b][:, 0:Wt], lhsT=sl[:, kc * 128:(kc + 1) * 128], rhs=hb[:, kc, 0:Wt],
                                                  start=(kc == 0), stop=(kc == 15)),
                          reads=[tsl, t_hb], writes=[t_bank[b]])
            s_, ts_ = sa[fb % 2], t_sa[fb % 2]
            act(s_[:, 0:Wt], banks[bg][:, 0:Wt], AF.Silu, [t_bank[bg]], [ts_])
            tt("dve", actb[:, fb, 0:Wt], s_[:, 0:Wt], banks[bu][:, 0:Wt], ALU.mult, [ts_, t_bank[bu]], [t_act])

        def epi(mb, b):
            stt("dve", hT[:, mb, 0:Wt], hT[:, mb, 0:Wt], ALPHA, banks[b][:, 0:Wt], ALU.mult, ALU.add, [t_hT, t_bank[b]], [t_hT])
        linear([dname + "0", dname + "1"], lambda kc: actb[:, kc, 0:Wt], [t_act], Wt, [4, 5, 6, 7], epi)
        return [sa[0], sa[1]]

    def dump(name, pos0, Wt):
        if dbg:
            kb.dma("sp", dbg_d[name][:, pos0:pos0 + Wt].rearrange("(c p) w -> p c w", p=128), hT[:, :, 0:Wt], reads=[t_hT])

    t_Kc = [Tok() for _ in range(8)]
    t_Vc, t_KR = Tok(), Tok()
    t_out = Tok()

    t_Kg, t_Vg, t_KRg, t_Gl, t_Gg, t_Hl, t_Hg = Tok(), Tok(), Tok(), Tok(), Tok(), Tok(), Tok()
    t_STl, t_STg, t_STin, t_dsl, t_dsg, t_H0 = Tok(), Tok(), Tok(), Tok(), Tok(), Tok()
    t_hin, t_Gin, t_dsum = Tok(), Tok(), Tok()
    sched = [(ph, t) for ph in (1, 2, 3, 4) for t in range(NT + 1)]
    for (phase, ti) in sched:
        Wt = NMETA if ti == 0 else W
        pos0 = 0 if ti == 0 else NMETA + W * (ti - 1)
        wi_prev = 0 if ti == 1 else 1
        nsub = (Wt + 127) // 128
        m12, m1, m2 = phase not in (1, 2), phase != 1, phase != 2
        m34, m3, m4 = phase not in (3, 4), phase != 3, (phase != 4 or ti == 0)
        kb.mute = False
        nopool[0] = (phase == 1)
        kb.barrier()
        ar.reset()
        if ti == 0 and phase in (1, 2):
            kb.op("dve", lambda: V.memset(ZR[:], 0.0), writes=[t_Z])
            kb.op("dve", lambda: V.memset(ZI[:], 0.0), writes=[t_Z])
        if ti == 0 and phase == 2:
            Gg = ar.get([128, 4, 2, 32])
            Tc = ar.get([128, 2, 32])
            t4 = [ar.get([128, 32]) for _ in range(4)]
            tg = Tok()
            kb.dma("sp", Gg[:].rearrange("p r c j -> p r (c j)"), Gg_d.rearrange("r p x -> p r x"), reads=[t_Gg], writes=[tg])
            kb.op("dve", lambda: V.memset(Gin[:], 0.0), writes=[t_Gin])
            cp("dve", Tc[:], Gg[:, 0, :, :], [tg], [tg])
            for k in range(1, 4):
                stt("dve", Gin[:], Tc[:], sel[:, k:k + 1], Gin[:], ALU.mult, ALU.add, [tg, t_const, t_Gin], [t_Gin])
                if k < 3:
                    tt("dve", t4[0][:], Tc[:, 0, :], ANc[:, 0, :], ALU.mult, [tg, t_const], [tg])
                    tt("dve", t4[1][:], Tc[:, 1, :], ANc[:, 1, :], ALU.mult, [tg, t_const], [tg])
                    tt("dve", t4[2][:], Tc[:, 0, :], ANc[:, 1, :], ALU.mult, [tg, t_const], [tg])
                    tt("dve", t4[3][:], Tc[:, 1, :], ANc[:, 0, :], ALU.mult, [tg, t_const], [tg])
                    tt("dve", Tc[:, 0, :], t4[0][:], t4[1][:], ALU.subtract, [tg], [tg])
                    tt("dve", Tc[:, 1, :], t4[2][:], t4[3][:], ALU.add, [tg], [tg])
                    tt("dve", Tc[:], Tc[:], Gg[:, k, :, :], ALU.add, [tg], [tg])
            kb.barrier()
            ar.reset()
        if ti == 0 and phase in (3, 4):
            kb.op("dve", lambda: V.memset(ST[:], 0.0), writes=[t_ST])
            kb.op("dve", lambda: V.memset(STb[:], 0.0), writes=[t_STb])
            kb.op("dve", lambda: V.memset(halo[:], 0.0), writes=[t_halo])
            kb.op("dve", lambda: V.memset(dsum[:], 0.0), writes=[t_dsum])
        if ti == 0 and phase == 3:
            Hg = ar.get([128, 4, 192])
            th = Tok()
            kb.dma("sp", Hg[:], Hg_d.rearrange("r p x -> p r x"), reads=[t_Hg], writes=[th])
            kb.op("dve", lambda: V.memset(halo_in[:], 0.0), writes=[t_hin])
            for k in range(1, 4):
                stt("dve", halo_in[:].rearrange("p c k -> p (c k)"), Hg[:, k - 1, :], sel[:, k:k + 1], halo_in[:].rearrange("p c k -> p (c k)"),
                    ALU.mult, ALU.add, [th, t_const, t_hin], [t_hin])
            kb.barrier()
            ar.reset()
        if ti == 0 and phase == 4:
            Eg = ar.get([128, 1, 64, 64])
            Tst = ar.get([128, 64, 64])
            Sin = ar.get([128, 64, 64])
            Dg = ar.get([128, 4, 64])
            te = Tok()
            kb.dma("sp", Dg[:], dsg_d.rearrange("r p x -> p r x"), reads=[t_dsg], writes=[te])
            act(Dg[:], Dg[:], AF.Exp, [te], [te])
            for i_ in range(2):
                kb.dma("sp", Tst[:, 32 * i_:32 * i_ + 32, :].rearrange("p h q -> p (h q)"), STg_d[i_][0], reads=[t_STg], writes=[te])
            kb.op("dve", lambda: V.memset(Sin[:], 0.0), writes=[te])
            for k in range(1, 4):
                stt("dve", Sin[:], Tst[:], sel[:, k:k + 1], Sin[:], ALU.mult, ALU.add, [te, t_const], [te])
                if k < 3:
                    for i_ in range(2):
                        kb.dma("sp", Eg[:, 0, 32 * i_:32 * i_ + 32, :].rearrange("p h q -> p (h q)"), STg_d[i_][k], reads=[t_STg], writes=[te])
                    tt("dve", Tst[:], Tst[:], Dg[:, k, :].unsqueeze(2).broadcast_to([128, 64, 64]), ALU.mult, [te], [te])
                    tt("dve", Tst[:], Tst[:], Eg[:, 0], ALU.add, [te], [te])
            kb.dma("sp", STin_d, Sin[:].rearrange("p h q -> p (h q)"), reads=[te], writes=[t_STin])
            kb.barrier()
            ar.reset()
        kb.mute = m12
        load_tile(xT[:, pos0:pos0 + Wt], Wt, [])
        qT = ar.get([128, 12, W], BF16)
        q_end = ar.off
        u32 = ar.get([128, 8, W])
        ub = ar.get([128, 8, W], BF16)
        pers_end = ar.off
        ql = ar.get([128, 4, W])
        kvl = ar.get([128, 2, W])
        krr = ar.get([128, 2, W])
        rC = ar.get([128, W])
        rS = ar.get([128, W])
        yv = ar.get([128, W])
        g2 = ar.get([128, W])
        t_qT, t_yv = Tok(), Tok()
        t_u32, t_ub, t_ql, t_kvl, t_krr, t_rope = Tok(), Tok(), Tok(), Tok(), Tok(), Tok()
        kb.dma("sp", rC[:, 0:Wt], ropeC_d[:, pos0:pos0 + Wt], writes=[t_rope])
        kb.dma("sp", rS[:, 0:Wt], ropeS_d[:, pos0:pos0 + Wt], writes=[t_rope])

        def epi_in(mb, b):
            src = banks[b][:, 0:Wt]
            if mb < 8:
                cp("act", u32[:, mb, 0:Wt], src, [t_bank[b]], [t_u32])
                cp("dve", ub[:, mb, 0:Wt], src, [t_bank[b]], [t_ub])
            elif mb < 12:
                cp("act", ql[:, mb - 8, 0:Wt], src, [t_bank[b]], [t_ql])
            elif mb < 14:
                cp("act", kvl[:, mb - 12, 0:Wt], src, [t_bank[b]], [t_kvl])
            else:
                cp("act", krr[:, mb - 14, 0:Wt], src, [t_bank[b]], [t_krr])
        linear("w0in", lambda kc: hb[:, kc, 0:Wt], lambda kc: [t_hbg[kc // 4]], Wt, [0, 1, 2, 3], epi_in,
               skip=(range(8, 12) if phase == 1 else range(12, 16)))
        mixT = X
        chk("inproj%d" % ti)

        def rms(src, t_src, nch, dim, gain, dstb, t_dst, bk):
            for c in range(nch):
                act(yv[:, 0:Wt], src[:, c, 0:Wt], AF.Square, [t_src], [t_yv])
                kb.op("pe", lambda: PE.matmul(banks[bk][:, 0:Wt], lhsT=ones32, rhs=yv[:, 0:Wt], start=(c == 0), stop=(c == nch - 1)),
                      reads=[t_yv, t_const], writes=[t_bank[bk]])
            ts("dve", g2[:, 0:Wt], banks[bk][:, 0:Wt], 1.0 / dim, RMS_EPS, ALU.mult, ALU.add, [t_bank[bk]], [t_yv])
            act(g2[:, 0:Wt], g2[:, 0:Wt], AF.Sqrt, [t_yv], [t_yv])
            kb.op("dve", lambda: V.reciprocal(out=g2[:, 0:Wt], in_=g2[:, 0:Wt]), reads=[t_yv], writes=[t_yv])
            for c in range(nch):
                stt("dve", dstb[:, c, 0:Wt], src[:, c, 0:Wt], gain[:, c:c + 1], g2[:, 0:Wt], ALU.mult, ALU.mult,
                    [t_src, t_yv, t_const], [t_dst])

        qnb = X[:, 16:20, :]
        kvb = X[:, 20:22, :]
        t_qnb, t_kvb = Tok(), Tok()
        kb.mute = m2
        rms(ql, t_ql, 4, 512.0, qn, qnb, t_qnb, 4)
        kb.mute = m1
        rms(kvl, t_kvl, 2, 256.0, kvn_g, kvb, t_kvb, 5)
        kr2 = X[:, 22, :]
        t_kr2 = Tok()
        tt("dve", yv[:, 0:Wt], krr[:, 0, 0:Wt], rC[:, 0:Wt], ALU.mult, [t_krr, t_rope], [t_yv])
        tt("dve", g2[:, 0:Wt], krr[:, 1, 0:Wt], rS[:, 0:Wt], ALU.mult, [t_krr, t_rope], [t_yv])
        tt("dve", kr2[:, 0:Wt], yv[:, 0:Wt], g2[:, 0:Wt], ALU.add, [t_yv], [t_kr2])
        kb.dma("sp", KR_d[:, pos0:pos0 + Wt], kr2[:, 0:Wt], reads=[t_kr2], writes=[t_KR])
        knew = [X[:, 23, :], X[:, 24, :]]
        t_knew = [Tok(), Tok()]

        def epi_k(mb, b):
            cp("act", knew[mb % 2][:, 0:Wt], banks[b][:, 0:Wt], [t_bank[b]], [t_knew[mb % 2]])
            kb.dma("sp", Kc_d[mb][:, pos0:pos0 + Wt], knew[mb % 2][:, 0:Wt], reads=[t_knew[mb % 2]], writes=[t_Kc[mb]])
        linear("wuk", lambda kc: kvb[:, kc, 0:Wt], [t_kvb], Wt, [0, 1], epi_k)
        vnew = [X[:, 25:27, :].rearrange("p a w -> p (a w)"), X[:, 27:29, :].rearrange("p a w -> p (a w)")]
        t_vnew = [Tok(), Tok()]
        for sj in range(nsub):
            L = min(128, Wt - sj * 128)
            kt = 0 if ti == 0 else 1 + 4 * (ti - 1) + sj
            vb, tvb = vnew[sj % 2], t_vnew[sj % 2]
            for half in range(2):
                b = 2 + half
                for kc in range(2):
                    kb.op("pe", lambda: PE.matmul(banks[b][0:L, :], lhsT=kvb[:, kc, sj * 128:sj * 128 + L],
                                                  rhs=wuv[:, kc, half * 512:(half + 1) * 512], start=(kc == 0), stop=(kc == 1)),
                          reads=[t_kvb, t_const], writes=[t_bank[b]])
                cp("act" if half == 0 else "dve", vb[0:L, half * 512:(half + 1) * 512], banks[b][0:L, :], [t_bank[b]], [tvb])
            if ti == 0:
                kb.dma("sp", Vm_d[0:L, :], vb[0:L, :], reads=[tvb], writes=[t_Vc])
            else:
                kb.dma("sp", Vf_d[ti - 1][0:L, sj, :], vb[0:L, :], reads=[tvb], writes=[t_Vc])
        kb.mute = m2

        def epi_q(mb, b):
            hp, r = mb // 4, mb % 4
            src = banks[b][:, 0:Wt]
            if r < 2:
                act(qT[:, hp * 3 + r, 0:Wt], src, AF.Copy, [t_bank[b]], [t_qT], scale=ATT_SCALE)
            elif r == 2:
                tt("dve", yv[:, 0:Wt], src, rC[:, 0:Wt], ALU.mult, [t_bank[b], t_rope], [t_yv])
            else:
                tt("dve", g2[:, 0:Wt], src, rS[:, 0:Wt], ALU.mult, [t_bank[b], t_rope], [t_yv])
                tt("dve", g2[:, 0:Wt], g2[:, 0:Wt], yv[:, 0:Wt], ALU.add, [t_yv], [t_yv])
                act(qT[:, hp * 3 + 2, 0:Wt], g2[:, 0:Wt], AF.Copy, [t_yv], [t_qT], scale=ATT_SCALE)
        linear("wuq", lambda kc: qnb[:, kc, 0:Wt], [t_qnb], Wt, [0, 1, 2, 3], epi_q)

        kb.mute = m12
        chk("a1_%d" % ti)
        kb.barrier()
        ar.reset(pers_end)
        if ti > 0:
            c_, s_ = rotc[:, wi_prev, :], rots[:, wi_prev, :]
            tt("dve", ztmp[:, 0, :], ZR[:], c_, ALU.mult, [t_Z, t_const], [t_Z])
            tt("dve", ztmp[:, 1, :], ZI[:], s_, ALU.mult, [t_Z, t_const], [t_Z])
            tt("dve", ztmp[:, 2, :], ZR[:], s_, ALU.mult, [t_Z, t_const], [t_Z])
            tt("dve", ztmp[:, 3, :], ZI[:], c_, ALU.mult, [t_Z, t_const], [t_Z])
            tt("dve", ZRi[:], ztmp[:, 0, :], ztmp[:, 1, :], ALU.subtract, [t_Z], [t_Z])
            tt("dve", ZIi[:], ztmp[:, 2, :], ztmp[:, 3, :], ALU.add, [t_Z], [t_Z])
            if ti == 1:
                ts("dve", ZRi[:], ZRi[:], sel[:, 0:1], None, ALU.mult, None, [t_Z, t_const], [t_Z])
                ts("dve", ZIi[:], ZIi[:], sel[:, 0:1], None, ALU.mult, None, [t_Z, t_const], [t_Z])
                kb.mute = m2
                tt("dve", ZRi[:], ZRi[:], Gin[:, 0, :], ALU.add, [t_Z, t_Gin], [t_Z])
                tt("dve", ZIi[:], ZIi[:], Gin[:, 1, :], ALU.add, [t_Z, t_Gin], [t_Z])
                kb.mute = m12
        else:
            kb.op("dve", lambda: V.memset(ZRi[:], 0.0), writes=[t_Z])
            kb.op("dve", lambda: V.memset(ZIi[:], 0.0), writes=[t_Z])
        NS = 2
        tabs = [ar.get([128, 2, W]) for _ in range(NS)]
        BLs = [X[:, 28, i * 256:(i + 1) * 256].rearrange("p (c s) -> p c s", c=2) for i in range(NS)]
        CLs = [X[:, 29 + i, 0:384].rearrange("p (c s) -> p c s", c=3) for i in range(NS)]
        t_ld = [Tok() for _ in range(NS)]
        w1 = [ar.get([128, W]) for _ in range(4)]
        zz = [ar.get([128, W]) for _ in range(2)]
        pp = [X[:, 24 + i, :] for i in range(4)]
        t_w1 = [Tok() for _ in range(4)]
        t_zz = [Tok() for _ in range(2)]
        t_pp = [Tok() for _ in range(4)]
        yv = ar.get([128, W])
        g2 = ar.get([128, W])
        gb = X[:, 16:24, :]
        t_yv, t_gb = Tok(), Tok()
        for cc in range(8):
            yb = 4 + (cc % 2)
            for q in range(4):
                j = cc * 4 + q
                s = j % NS
                kb.dma("sp", tabs[s][:, :, 0:Wt], TAB_d[j][:, :, 0:Wt], reads=[t_tab], writes=[t_ld[s]])
                kb.dma("sp", BLs[s], BL_d[j], reads=[t_tab], writes=[t_ld[s]])
                kb.mute = m2
                kb.dma("sp", CLs[s], CL_d[j], reads=[t_tab], writes=[t_ld[s]])
                kb.mute = m12
                cs_t, sn_t = tabs[s][:, 0, 0:Wt], tabs[s][:, 1, 0:Wt]
                bre, bim = (0, 1) if j % 2 == 0 else (2, 3)
                kb.op("pe", lambda: PE.matmul(banks[bre][:, 0:Wt], lhsT=BLs[s][:, 0, :], rhs=ub[:, cc, 0:Wt], start=True, stop=True),
                      reads=[t_ld[s], t_ub], writes=[t_bank[bre]])
                kb.op("pe", lambda: PE.matmul(banks[bim][:, 0:Wt], lhsT=BLs[s][:, 1, :], rhs=ub[:, cc, 0:Wt], start=True, stop=True),
                      reads=[t_ld[s], t_ub], writes=[t_bank[bim]])
                tt("dve", w1[0][:, 0:Wt], banks[bre][:, 0:Wt], cs_t, ALU.mult, [t_bank[bre], t_ld[s]], [t_w1[0]])
                tt("dve", w1[1][:, 0:Wt], banks[bim][:, 0:Wt], sn_t, ALU.mult, [t_bank[bim], t_ld[s]], [t_w1[1]])
                tt("dve", w1[2][:, 0:Wt], banks[bim][:, 0:Wt], cs_t, ALU.mult, [t_bank[bim], t_ld[s]], [t_w1[2]])
                tt("dve", w1[3][:, 0:Wt], banks[bre][:, 0:Wt], sn_t, ALU.mult, [t_bank[bre], t_ld[s]], [t_w1[3]])
                tt("pool", w1[0][:, 0:Wt], w1[0][:, 0:Wt], w1[1][:, 0:Wt], ALU.add, [t_w1[1]], [t_w1[0]])
                tt("pool", w1[2][:, 0:Wt], w1[2][:, 0:Wt], w1[3][:, 0:Wt], ALU.subtract, [t_w1[3]], [t_w1[2]])
                kb.op("dve", lambda: V.tensor_tensor_scan(out=zz[0][:, 0:Wt], data0=s5R[:, j:j + 1].broadcast_to([128, Wt]),
                                                          data1=w1[0][:, 0:Wt], initial=ZRi[:, j:j + 1], op0=ALU.mult, op1=ALU.add),
                      reads=[t_w1[0], t_Z, t_const], writes=[t_zz[0]])
                kb.op("dve", lambda: V.tensor_tensor_scan(out=zz[1][:, 0:Wt], data0=s5R[:, j:j + 1].broadcast_to([128, Wt]),
                                                          data1=w1[2][:, 0:Wt], initial=ZIi[:, j:j + 1], op0=ALU.mult, op1=ALU.add),
                      reads=[t_w1[2], t_Z, t_const], writes=[t_zz[1]])
                cp("act", ZR[:, j:j + 1], zz[0][:, Wt - 1:Wt], [t_zz[0]], [t_Z])
                cp("act", ZI[:, j:j + 1], zz[1][:, Wt - 1:Wt], [t_zz[1]], [t_Z])
                kb.mute = m2
                tt("dve", pp[0][:, 0:Wt], zz[0][:, 0:Wt], cs_t, ALU.mult, [t_zz[0], t_ld[s]], [t_pp[0]])
                tt("dve", pp[1][:, 0:Wt], zz[1][:, 0:Wt], sn_t, ALU.mult, [t_zz[1], t_ld[s]], [t_pp[1]])
                tt("pool", pp[2][:, 0:Wt], zz[0][:, 0:Wt], sn_t, ALU.mult, [t_zz[0], t_ld[s]], [t_pp[2]])
                tt("pool", pp[3][:, 0:Wt], zz[1][:, 0:Wt], cs_t, ALU.mult, [t_zz[1], t_ld[s]], [t_pp[3]])
                for k in range(4):
                    kb.op("pe", lambda: PE.matmul(banks[yb][:, 0:Wt], lhsT=CLs[s][:, (0, 1, 2, 2)[k], :], rhs=pp[k][:, 0:Wt],
                                                  start=(q == 0 and k == 0), stop=(q == 3 and k == 3)),
                          reads=[t_ld[s], t_pp[k]], writes=[t_bank[yb]])
                kb.mute = m12
            kb.mute = m2
            stt("dve", yv[:, 0:Wt], u32[:, cc, 0:Wt], s5d[:, cc:cc + 1], banks[yb][:, 0:Wt], ALU.mult, ALU.add,
                [t_u32, t_bank[yb], t_const], [t_yv])
            act(g2[:, 0:Wt], yv[:, 0:Wt], AF.Square, [t_yv], [t_yv])
            ts("dve", g2[:, 0:Wt], g2[:, 0:Wt], 0.044715, 1.0, ALU.mult, ALU.add, [t_yv], [t_yv])
            tt("dve", g2[:, 0:Wt], g2[:, 0:Wt], yv[:, 0:Wt], ALU.mult, [t_yv], [t_yv])
            act(g2[:, 0:Wt], g2[:, 0:Wt], AF.Sigmoid, [t_yv], [t_yv], scale=GELU_C)
            tt("dve", u32[:, cc, 0:Wt], g2[:, 0:Wt], yv[:, 0:Wt], ALU.mult, [t_yv, t_u32], [t_u32])
            cp("pool", gb[:, cc, 0:Wt], u32[:, cc, 0:Wt], [t_u32], [t_gb])
            kb.mute = m12

        def epi_glu(mb, b):
            act(g2[:, 0:Wt], banks[b][:, 0:Wt], AF.Sigmoid, [t_bank[b]], [t_yv])
            tt("dve", mixT[:, mb, 0:Wt], g2[:, 0:Wt], u32[:, mb, 0:Wt], ALU.mult, [t_yv, t_u32], [t_X])
        kb.mute = m2
        linear("wglu", lambda kc: gb[:, kc, 0:Wt], [t_gb], Wt, [6, 7], epi_glu)
        kb.mute = m1
        if ti == NT:
            Gl = ar.get([128, 2, 32])
            c_, s_ = rotc[:, 1, :], rots[:, 1, :]
            tt("dve", ztmp[:, 0, :], ZR[:], c_, ALU.mult, [t_Z, t_const], [t_Z])
            tt("dve", ztmp[:, 1, :], ZI[:], s_, ALU.mult, [t_Z, t_const], [t_Z])
            tt("dve", ztmp[:, 2, :], ZR[:], s_, ALU.mult, [t_Z, t_const], [t_Z])
            tt("dve", ztmp[:, 3, :], ZI[:], c_, ALU.mult, [t_Z, t_const], [t_Z])
            tt("dve", Gl[:, 0, :], ztmp[:, 0, :], ztmp[:, 1, :], ALU.subtract, [t_Z], [t_Z])
            tt("dve", Gl[:, 1, :], ztmp[:, 2, :], ztmp[:, 3, :], ALU.add, [t_Z], [t_Z])
            kb.dma("sp", Gl_d, Gl[:].rearrange("p c j -> p (c j)"), reads=[t_Z], writes=[t_Gl])
            kb.collective("AllGather", GROUPS, Gl_d, Gg_d.rearrange("r p x -> (r p) x"), reads=[t_Gl], writes=[t_Gg])
            for h_ in range(8):
                kb.collective("AllGather", GROUPS, Kc_d[h_], Kg_d[h_].rearrange("r p c -> (r p) c"), reads=[t_Kc[h_]], writes=[t_Kg])
            for q_ in range(NT):
                kb.collective("AllGather", GROUPS, Vf_d[q_].rearrange("p k d -> p (k d)"), Vgf_d[q_].rearrange("r p k d -> (r p) (k d)"),
                              reads=[t_Vc], writes=[t_Vg])
            kb.collective("AllGather", GROUPS, KR_d, KRg_d.rearrange("r p c -> (r p) c"), reads=[t_KR], writes=[t_KRg])
        kb.mute = m2

        chk("a2_%d" % ti)
        kb.barrier()
        nk = pos0 + Wt
        nkt = 1 if ti == 0 else 1 + 4 * ti
        ar.reset(q_end)
        NKA = NTOK + 3 * NFR
        KRs = ar.get([128, NKA], BF16)
        KBf = [ar.get([128, NTOK], BF16) for _ in range(2)]
        VBf = [ar.get([128, NKT, 128], BF16) for _ in range(2)]
        PT = [X[:, 16 + i, :] for i in range(4)]
        rec = ar.get([128, W])
        t_KRs, t_rec = Tok(), Tok()
        t_KBf, t_VBf = [Tok(), Tok()], [Tok(), Tok()]
        t_PT = [Tok() for _ in range(4)]
        kb.dma("sp", KRs[:, 0:nk], KR_d[:, 0:nk], reads=[t_KR], writes=[t_KRs])
        if ti > 0:
            kb.dma("sp", KRs[:, NTOK:NKA].rearrange("p (r c) -> p r c", r=3), KRg_d[0:3, :, NMETA:NTOK].rearrange("r p c -> p r c"),
                   reads=[t_KRg], writes=[t_KRs])
        steps = []
        for h in range(8):
            steps.append((h, -1))
            if ti > 0:
                for r in range(3):
                    steps.append((h, r))

        def seg_load(si):
            h, r = steps[si]
            kbuf, vbuf, tk, tv = KBf[si % 2], VBf[si % 2], t_KBf[si % 2], t_VBf[si % 2]
            if r < 0:
                kb.dma("sp", kbuf[:, 0:nk], Kc_d[h][:, 0:nk], reads=[t_Kc[h]], writes=[tk])
                kb.dma("sp", vbuf[:, 0, :], Vm_d[:, h * 128:(h + 1) * 128], reads=[t_Vc], writes=[tv])
                for q_ in range(ti):
                    kb.dma("sp", vbuf[:, 1 + 4 * q_:5 + 4 * q_, :], Vf_d[q_][:, :, h * 128:(h + 1) * 128], reads=[t_Vc], writes=[tv])
            else:
                kb.dma("sp", kbuf[:, 0:NFR], Kg_d[h][r, :, NMETA:NTOK], reads=[t_Kg], writes=[tk])
                for q_ in range(NT):
                    kb.dma("sp", vbuf[:, 4 * q_:4 * q_ + 4, :], Vgf_d[q_][r, :, :, h * 128:(h + 1) * 128], reads=[t_Vg], writes=[tv])

        pti = 0

        def seg_compute(si):
            nonlocal_pti = pti_box
            h, r = steps[si]
            hp, jh = h // 2, h % 2
            kbuf, vbuf, tk, tv = KBf[si % 2], VBf[si % 2], t_KBf[si % 2], t_VBf[si % 2]
            ob, sbk = (4, 5) if h % 2 == 0 else (6, 7)
            qn_ = qT[:, hp * 3 + jh, 0:Wt]
            qr_ = qT[jh * 64:(jh + 1) * 64, hp * 3 + 2, 0:Wt]
            first_seg = (r < 0)
            last_seg = (ti == 0) or (r == 2)
            tiles = []
            if r < 0:
                tiles.append((0, 0, 0, NMETA, 0, None, None))
                for kt in range(1, nkt):
                    kc0 = NMETA + 128 * (kt - 1)
                    rr = kt - (1 + 4 * (ti - 1))
                    if rr < 0:
                        tiles.append((kt, kc0, kc0, 128, 0, None, None))
                    else:
                        tiles.append((kt, kc0, kc0, 128, 128 * rr, (64, 128 * rr), None))
            else:
                for k in range(NPK):
                    tiles.append((k, 128 * k, NTOK + r * NFR + 128 * k, 128, 0, None, r))
            nmm = len(tiles)
            slot0 = nonlocal_pti[0]
            nonlocal_pti[0] += nmm

            def qk_exp(im):
                (vt, kc0, krc, nkeys, c0, zero, bcol) = tiles[im]
                sl_ = (slot0 + im) % 4
                p_, tp_ = PT[sl_], t_PT[sl_]
                kb.op("pe", lambda: PE.matmul(banks[sl_][0:nkeys, 0:Wt], lhsT=kbuf[:, kc0:kc0 + nkeys], rhs=qn_, start=True, stop=False),
                      reads=[tk, t_qT], writes=[t_bank[sl_]])
                kb.op("pe", lambda: PE.matmul(banks[sl_][0:nkeys, 0:Wt], lhsT=KRs[jh * 64:(jh + 1) * 64, krc:krc + nkeys], rhs=qr_,
                                              start=False, stop=True),
                      reads=[t_KRs, t_qT], writes=[t_bank[sl_]])
                if bcol is None:
                    act(p_[0:nkeys, 0:Wt], banks[sl_][0:nkeys, 0:Wt], AF.Exp, [t_bank[sl_]], [tp_])
                else:
                    act(p_[0:nkeys, 0:Wt], banks[sl_][0:nkeys, 0:Wt], AF.Exp, [t_bank[sl_], t_const], [tp_], bias=segb[0:nkeys, bcol:bcol + 1])
                if zero is not None:
                    kb.op("pool", lambda: P.memset(p_[zero[0]:zero[0] + 64, zero[1]:zero[1] + 64], 0.0), reads=[], writes=[tp_])

            def pv_sum(im):
                (vt, kc0, krc, nkeys, c0, zero, bcol) = tiles[im]
                sl_ = (slot0 + im) % 4
                p_, tp_ = PT[sl_], t_PT[sl_]
                st_ = first_seg and im == 0
                sp_ = last_seg and im == nmm - 1
                kb.op("pe", lambda: PE.matmul(banks[ob][:, c0:Wt], lhsT=vbuf[0:nkeys, vt, :], rhs=p_[0:nkeys, c0:Wt], start=st_, stop=sp_),
                      reads=[tv, tp_], writes=[t_bank[ob]])
                kb.op("pe", lambda: PE.matmul(banks[sbk][:, c0:Wt], lhsT=onesb[0:nkeys, :], rhs=p_[0:nkeys, c0:Wt], start=st_, stop=sp_),
                      reads=[t_const, tp_], writes=[t_bank[sbk]])
            LAG = 2
            for im in range(nmm + LAG):
                if im < nmm:
                    qk_exp(im)
                if im - LAG >= 0:
                    pv_sum(im - LAG)
            if last_seg:
                kb.op("dve", lambda: V.reciprocal(out=rec[:, 0:Wt], in_=banks[sbk][:, 0:Wt]), reads=[t_bank[sbk]], writes=[t_rec])
                tt("dve", mixT[:, 8 + h, 0:Wt], banks[ob][:, 0:Wt], rec[:, 0:Wt], ALU.mult, [t_bank[ob], t_rec], [t_X])

        pti_box = [0]
        seg_load(0)
        for si in range(len(steps)):
            if si + 1 < len(steps):
                seg_load(si + 1)
            seg_compute(si)

        chk("attn%d" % ti)
        kb.barrier()

        def epi_res(mb, b):
            stt("dve", hT[:, mb, 0:Wt], hT[:, mb, 0:Wt], ALPHA, banks[b][:, 0:Wt], ALU.mult, ALU.add, [t_hT, t_bank[b]], [t_hT])
        linear("w0out", lambda kc: mixT[:, kc, 0:Wt], [t_X], Wt, [0, 1, 2, 3], epi_res)
        ar.reset()
        wk = [ar.get([128, W]) for _ in range(5)]
        layer_norm(Wt, 0, 1, 4, 5, wk)
        kb.barrier()
        ffn(Wt, "w0g", "w0u", "w0d")
        kb.barrier()
        ar.reset()
        wk = [ar.get([128, W]) for _ in range(5)]
        layer_norm(Wt, 2, 3, 4, 5, wk)
        dump("d_l0", pos0, Wt)
        kb.dma("sp", H0_d[:, pos0:pos0 + Wt].rearrange("(c p) w -> p c w", p=128), hT[:, :, 0:Wt], reads=[t_hT], writes=[t_H0])
        if ti == NT:
            Hl = ar.get([128, 48, 4])
            t_Hl_s = Tok()
            for g_ in range(8):
                for r_ in (0, 1, 2, 3, 8, 9):
                    ch = g_ * 4 + r_ if r_ < 4 else 32 + (r_ - 8) * 8 + g_
                    sl, tsl = load_slab("w1in", 1 + g_ * 10 + r_, 2048)
                    b = ch % 2
                    for kc in range(16):
                        kb.op("pe", lambda: PE.matmul(banks[b][:, 0:4], lhsT=sl[:, kc * 128:(kc + 1) * 128], rhs=hb[:, kc, Wt - 4:Wt],
                                                      start=(kc == 0), stop=(kc == 15)),
                              reads=[tsl, t_hb], writes=[t_bank[b]])
                    cp("act", Hl[:, ch, :], banks[b][:, 0:4], [t_bank[b]], [t_Hl_s])
            kb.dma("sp", Hl_d, Hl[:].rearrange("p c k -> p (c k)"), reads=[t_Hl_s], writes=[t_Hl])
            kb.collective("AllGather", GROUPS, Hl_d, Hg_d.rearrange("r p x -> (r p) x"), reads=[t_Hl], writes=[t_Hg])
        chk("l0_%d" % ti)

        kb.mute = m34
        kb.barrier()
        ar.reset()
        ynT = X
        load_tile(H0_d[:, pos0:pos0 + Wt], Wt, [t_H0])
        if ti == 1:
            ts("dve", halo[:], halo[:], sel[:, 0:1], None, ALU.mult, None, [t_halo, t_const], [t_halo])
            tt("dve", halo[:, :, 0:3], halo[:, :, 0:3], halo_in[:, :, 1:4], ALU.add, [t_halo, t_hin], [t_halo])
            ts("dve", ST[:], ST[:], sel[:, 0:1], None, ALU.mult, None, [t_ST, t_const], [t_ST])
            kb.mute = m4
            Sin_ = ar.get([128, 64, 64])
            t_sin = Tok()
            kb.dma("sp", Sin_[:].rearrange("p h q -> p (h q)"), STin_d, reads=[t_STin], writes=[t_sin])
            tt("dve", ST[:], ST[:], Sin_[:], ALU.add, [t_ST, t_sin], [t_ST])
            kb.mute = m34
            cp("act", STb[:], ST[:], [t_ST], [t_STb])
            kb.barrier()
            ar.reset()
        dtT = ar.get([128, W])
        daT = ar.get([128, W])
        dt_tok = ar.get([128, 4, 64])
        da_tok = ar.get([128, 4, 64])
        cs_tok = ar.get([128, 4, 64])
        wg_tok = ar.get([128, 4, 64])
        ece = ar.get([128, 4, 64])
        t_dt = Tok()
        sl, tsl = load_slab("w1in", 0, 2048)
        for kc in range(16):
            kb.op("pe", lambda: PE.matmul(banks[0][:, 0:Wt], lhsT=sl[:, kc * 128:(kc + 1) * 128], rhs=hb[:, kc, 0:Wt], start=(kc == 0), stop=(kc == 15)),
                  reads=[tsl, t_hbg[kc // 4]], writes=[t_bank[0]])
        act(dtT[0:64, 0:Wt], banks[0][0:64, 0:Wt], AF.Exp, [t_bank[0], t_const], [t_dt], bias=dtb[0:64, 0:1])
        act(dtT[0:64, 0:Wt], dtT[0:64, 0:Wt], AF.Ln, [t_dt], [t_dt], bias=1.0)
        ts("dve", daT[0:64, 0:Wt], dtT[0:64, 0:Wt], aneg[0:64, 0:1], None, ALU.mult, None, [t_dt, t_const], [t_dt])
        for sj in range(nsub):
            L = min(128, Wt - sj * 128)
            cs_ = slice(sj * 128, sj * 128 + L)
            kb.op("pe", lambda: PE.transpose(banks[2][0:L, 0:64], dtT[0:64, cs_], ident[0:64, 0:64]), reads=[t_dt, t_const], writes=[t_bank[2]])
            kb.op("pe", lambda: PE.transpose(banks[2][0:L, 64:128], daT[0:64, cs_], ident[0:64, 0:64]), reads=[t_dt, t_const], writes=[t_bank[2]])
            cp("act", dt_tok[0:L, sj, :], banks[2][0:L, 0:64], [t_bank[2]], [t_dt])
            cp("dve", da_tok[0:L, sj, :], banks[2][0:L, 64:128], [t_bank[2]], [t_dt])
            kb.op("pe", lambda: PE.matmul(banks[3][0:L, 0:64], lhsT=U2[0:L, 0:L], rhs=da_tok[0:L, sj, :], start=True, stop=True),
                  reads=[t_dt, t_const], writes=[t_bank[3]])
            kb.op("pe", lambda: PE.matmul(banks[3][:, 64:128], lhsT=ones32[0:L, :], rhs=da_tok[0:L, sj, :], start=True, stop=True),
                  reads=[t_dt, t_const], writes=[t_bank[3]])
            cp("act", cs_tok[0:L, sj, :], banks[3][0:L, 0:64], [t_bank[3]], [t_dt])
            act(ece[:, sj, :], banks[3][:, 64:128], AF.Exp, [t_bank[3]], [t_dt])
            if ti > 0:
                kb.mute = m3
                tt("dve", dsum[:], dsum[:], banks[3][:, 64:128], ALU.add, [t_dsum, t_bank[3]], [t_dsum])
                kb.mute = m34
            tt("dve", wg_tok[0:L, sj, :], banks[3][0:L, 64:128], cs_tok[0:L, sj, :], ALU.subtract, [t_bank[3], t_dt], [t_dt])
            act(wg_tok[0:L, sj, :], wg_tok[0:L, sj, :], AF.Exp, [t_dt], [t_dt])
            tt("dve", wg_tok[0:L, sj, :], wg_tok[0:L, sj, :], dt_tok[0:L, sj, :], ALU.mult, [t_dt], [t_dt])

        chk("l1dt%d" % ti)
        xg = ar.get([128, 4, W])
        xgb = ar.get([128, 4, W], BF16)
        szg = ar.get([128, 4, W])
        yvg = ar.get([128, 4, W])
        BCf = ar.get([128, 2, W])
        BCb = ar.get([128, 2, W], BF16)
        xin = [ar.get([128, W + 4]) for _ in range(2)]
        cacc = [ar.get([128, W]) for _ in range(2)]
        xdt = ar.get([128, 512], BF16)
        xwg = ar.get([128, 512], BF16)
        Btok = ar.get([128, 128], BF16)
        CC = ar.get([128, 256])
        Lh = [ar.get([128, 256]) for _ in range(2)]
        Eh = [ar.get([128, 256]) for _ in range(2)]
        MC = [ar.get([128, 256], BF16) for _ in range(2)]
        sqg = ar.get([128, W])
        rsg = ar.get([128, W])
        t_xg, t_xgb, t_szg, t_yvg, t_BC = Tok(), Tok(), Tok(), Tok(), Tok()
        t_xin, t_cacc = [Tok(), Tok()], [Tok(), Tok()]
        t_xdt, t_xwg, t_Btok, t_CC = Tok(), Tok(), Tok(), Tok()
        t_Lh, t_Eh, t_MC = [Tok(), Tok()], [Tok(), Tok()], [Tok(), Tok()]
        t_sqg = Tok()
        cvi = [0]

        def conv_silu(b, ch, outs):
            i = cvi[0] % 2
            cvi[0] += 1
            xi, txi, ca, tca = xin[i], t_xin[i], cacc[i], t_cacc[i]
            cp("pool", xi[:, 0:3], halo[:, ch, 0:3], [t_halo], [txi])
            cp("act", xi[:, 3:3 + Wt], banks[b][:, 0:Wt], [t_bank[b]], [txi])
            cp("pool", halo[:, ch, 0:3], xi[:, Wt:Wt + 3], [txi], [t_halo])
            ts("dve", ca[:, 0:Wt], xi[:, 0:Wt], cw[:, ch, 0:1], None, ALU.mult, None, [txi, t_const], [tca])
            for k in range(1, 4):
                stt("dve" if k < 3 else "pool", ca[:, 0:Wt], xi[:, k:k + Wt], cw[:, ch, k:k + 1], ca[:, 0:Wt], ALU.mult, ALU.add,
                    [txi, t_const], [tca])
            for (o, to) in outs[:1]:
                act(o, ca[:, 0:Wt], AF.Silu, [tca, t_const], [to], bias=cb[:, ch:ch + 1])
            for (o, to) in outs[1:]:
                cp("pool", o, outs[0][0], [outs[0][1]], [to])

        for g in range(8):
            base = 1 + g * 10
            for r in range(10):
                mb = base + r
                kb.mute = m34 or ((phase == 3 or ti == 0) and 4 <= r < 8)
                sl, tsl = load_slab("w1in", mb, 2048)
                b = r % 2
                for kc in range(16):
                    kb.op("pe", lambda: PE.matmul(banks[b][:, 0:Wt], lhsT=sl[:, kc * 128:(kc + 1) * 128], rhs=hb[:, kc, 0:Wt],
                                                  start=(kc == 0), stop=(kc == 15)),
                          reads=[tsl, t_hb], writes=[t_bank[b]])
                if r < 4:
                    conv_silu(b, g * 4 + r, [(xg[:, r, 0:Wt], t_xg), (xgb[:, r, 0:Wt], t_xgb)])
                elif r < 8:
                    act(szg[:, r - 4, 0:Wt], banks[b][:, 0:Wt], AF.Silu, [t_bank[b]], [t_szg])
                else:
                    conv_silu(b, 32 + (r - 8) * 8 + g, [(BCf[:, r - 8, 0:Wt], t_BC), (BCb[:, r - 8, 0:Wt], t_BC)])
            kb.mute = m34
            if g == 0:
                chk("l1conv%d" % ti)
            for sj in range(nsub):
                L = min(128, Wt - sj * 128)
                cs_ = slice(sj * 128, sj * 128 + L)
                trb = banks[2].bitcast(BF16) if hasattr(banks[2], "bitcast") else None
                for c in range(4):
                    kb.op("pe", lambda: PE.transpose(trb[0:L, c * 128:(c + 1) * 128], xgb[:, c, cs_], identb), reads=[t_xgb, t_const], writes=[t_bank[2]])
                kb.op("pe", lambda: PE.transpose(trb[0:L, 512:640], BCb[:, 0, cs_], identb), reads=[t_BC, t_const], writes=[t_bank[2]])
                if g == 0 and sj == 0:
                    chk("ssd_a%d" % ti)
                kb.mute = m4
                tt("dve", xdt[0:L, :].rearrange("p (h q) -> p h q", h=8), trb[0:L, 0:512].rearrange("p (h q) -> p h q", h=8),
                   dt_tok[0:L, sj, g * 8:g * 8 + 8].unsqueeze(2).broadcast_to([L, 8, 64]), ALU.mult, [t_bank[2], t_dt], [t_xdt])
                kb.mute = m34
                tt("dve", xwg[0:L, :].rearrange("p (h q) -> p h q", h=8), trb[0:L, 0:512].rearrange("p (h q) -> p h q", h=8),
                   wg_tok[0:L, sj, g * 8:g * 8 + 8].unsqueeze(2).broadcast_to([L, 8, 64]), ALU.mult, [t_bank[2], t_dt], [t_xwg])
                cp("act", Btok[0:L, :], trb[0:L, 512:640], [t_bank[2]], [t_Btok])
                if g == 0 and sj == 0:
                    chk("ssd_b%d" % ti)
                kb.mute = m4
                kb.op("pe", lambda: PE.matmul(banks[3][0:L, 0:L], lhsT=BCb[:, 0, cs_], rhs=BCb[:, 1, cs_], start=True, stop=True),
                      reads=[t_BC], writes=[t_bank[3]])
                tt("dve", CC[0:L, 0:L], banks[3][0:L, 0:L], U2[0:L, 0:L], ALU.mult, [t_bank[3], t_const], [t_CC])
                cp("pool", CC[:, 128:128 + L], BCf[:, 1, cs_], [t_BC], [t_CC])
                if g == 0 and sj == 0:
                    chk("ssd_c%d" % ti)
                def st_a1(hh):
                    h = g * 8 + hh
                    i2 = hh % 2
                    lh = Lh[i2]
                    sgb = 4 if hh % 2 == 0 else 5
                    ts("dve", lh[0:L, :], U1o[0:L, :], da_tok[0:L, sj, h:h + 1], None, ALU.mult, None, [t_const, t_dt], [t_Lh[i2]])
                    kb.op("pe", lambda: PE.matmul(banks[sgb][0:L, 0:L], lhsT=lh[0:L, 0:L], rhs=U2[0:L, 0:L], start=True, stop=True),
                          reads=[t_Lh[i2], t_const], writes=[t_bank[sgb]])
                    kb.op("pe", lambda: PE.matmul(banks[sgb][:, 128:128 + L], lhsT=lh[0:L, 128:256], rhs=U2[0:L, 0:L], start=True, stop=True),
                          reads=[t_Lh[i2], t_const], writes=[t_bank[sgb]])

                def st_a2(hh):
                    i2 = hh % 2
                    eh, mc = Eh[i2], MC[i2]
                    sgb = 4 if hh % 2 == 0 else 5
                    act(eh[0:L, 0:L], banks[sgb][0:L, 0:L], AF.Exp, [t_bank[sgb]], [t_Eh[i2]])
                    act(eh[:, 128:128 + L], banks[sgb][:, 128:128 + L], AF.Exp, [t_bank[sgb]], [t_Eh[i2]])
                    tt("dve", mc[0:L, 0:L], eh[0:L, 0:L], CC[0:L, 0:L], ALU.mult, [t_Eh[i2], t_CC], [t_MC[i2]])
                    tt("pool", mc[:, 128:128 + L], eh[:, 128:128 + L], CC[:, 128:128 + L], ALU.mult, [t_Eh[i2], t_CC], [t_MC[i2]])

                def st_b(hh):
                    h = g * 8 + hh
                    i2 = hh % 2
                    mc = MC[i2]
                    yb = 6 + (hh // 4)
                    ycol = (hh % 4) * 128
                    pr = (hh // 2) * 128
                    kb.op("pe", lambda: PE.matmul(banks[yb][:, ycol:ycol + L], lhsT=xdt[0:L, pr:pr + 128], rhs=mc[0:L, 0:L], start=True, stop=False),
                          reads=[t_xdt, t_MC[i2]], writes=[t_bank[yb]])
                    kb.op("pe", lambda: PE.matmul(banks[yb][:, ycol:ycol + L], lhsT=STb[:, h - (h % 2):h - (h % 2) + 2, :].rearrange("p a b -> p (a b)"),
                                                  rhs=mc[:, 128:128 + L], start=False, stop=True),
                          reads=[t_STb, t_MC[i2]], writes=[t_bank[yb]])
                for st in range(8 + 1):
                    if st < 8:
                        st_a1(st)
                    if st - 1 >= 0:
                        st_a2(st - 1)
                        st_b(st - 1)
                if g == 0 and sj == 0:
                    chk("ssd_d%d" % ti)
                for hh in range(8):
                    yb = 6 + (hh // 4)
                    ycol = (hh % 4) * 128
                    c = hh // 2
                    pr = slice((hh % 2) * 64, (hh % 2) * 64 + 64)
                    stt("dve", yvg[pr, c, cs_], xg[pr, c, cs_], l1d[pr, g * 4 + c:g * 4 + c + 1], banks[yb][pr, ycol:ycol + L], ALU.mult, ALU.add,
                        [t_xg, t_bank[yb], t_const], [t_yvg])
                if g == 0 and sj == 0:
                    chk("ssd_e%d" % ti)
                kb.mute = m34
                kb.op("pe", lambda: PE.matmul(banks[3][:, :], lhsT=Btok[0:L, :], rhs=xwg[0:L, :], start=True, stop=True),
                      reads=[t_Btok, t_xwg], writes=[t_bank[3]])
                tt("dve", ST[:, g * 8:g * 8 + 8, :], ST[:, g * 8:g * 8 + 8, :], ece[:, sj, g * 8:g * 8 + 8].unsqueeze(2).broadcast_to([128, 8, 64]),
                   ALU.mult, [t_ST, t_dt], [t_ST])
                tt("dve", ST[:, g * 8:g * 8 + 8, :], ST[:, g * 8:g * 8 + 8, :], banks[3][:, :].rearrange("p (h q) -> p h q", h=8), ALU.add,
                   [t_ST, t_bank[3]], [t_ST])
                cp("act", STb[:, g * 8:g * 8 + 8, :], ST[:, g * 8:g * 8 + 8, :], [t_ST], [t_STb])
            if g == 0:
                chk("l1ssd%d" % ti)
            kb.mute = m4
            for c in range(4):
                tt("dve" if c % 2 == 0 else "pool", yvg[:, c, 0:Wt], yvg[:, c, 0:Wt], szg[:, c, 0:Wt], ALU.mult, [t_yvg, t_szg], [t_yvg])
                act(sqg[:, 0:Wt], yvg[:, c, 0:Wt], AF.Square, [t_yvg], [t_sqg])
                kb.op("pe", lambda: PE.matmul(banks[2][:, 0:Wt], lhsT=ones32, rhs=sqg[:, 0:Wt], start=(c == 0), stop=(c == 3)),
                      reads=[t_sqg, t_const], writes=[t_bank[2]])
            ts("dve", rsg[:, 0:Wt], banks[2][:, 0:Wt], 1.0 / 512.0, RMS_EPS, ALU.mult, ALU.add, [t_bank[2]], [t_sqg])
            act(rsg[:, 0:Wt], rsg[:, 0:Wt], AF.Sqrt, [t_sqg], [t_sqg])
            kb.op("dve", lambda: V.reciprocal(out=rsg[:, 0:Wt], in_=rsg[:, 0:Wt]), reads=[t_sqg], writes=[t_sqg])
            for c in range(4):
                stt("dve" if c % 2 == 0 else "pool", ynT[:, g * 4 + c, 0:Wt], yvg[:, c, 0:Wt], ng[:, g * 4 + c:g * 4 + c + 1], rsg[:, 0:Wt],
                    ALU.mult, ALU.mult, [t_yvg, t_sqg, t_const], [t_X])
        kb.mute = m3
        if ti == NT:
            for i_ in range(2):
                kb.dma("sp", STl_d[i_], ST[:, 32 * i_:32 * i_ + 32, :].rearrange("p h q -> p (h q)"), reads=[t_ST], writes=[t_STl])
            kb.dma("sp", dsl_d, dsum[:], reads=[t_dsum], writes=[t_dsl])
            for i_ in range(2):
                kb.collective("AllGather", GROUPS, STl_d[i_], STg_d[i_].rearrange("r p x -> (r p) x"), reads=[t_STl], writes=[t_STg])
            kb.collective("AllGather", GROUPS, dsl_d, dsg_d.rearrange("r p x -> (r p) x"), reads=[t_dsl], writes=[t_dsg])
        kb.mute = m4
        chk("l1mix%d" % ti)
        linear("w1out", lambda kc: ynT[:, kc, 0:Wt], [t_X], Wt, [0, 1, 2, 3], epi_res)
        kb.barrier()
        ar.reset()
        wk = [ar.get([128, W]) for _ in range(5)]
        layer_norm(Wt, 4, 5, 4, 5, wk)
        kb.barrier()
        ffn(Wt, "w1g", "w1u", "w1d")
        kb.barrier()
        ar.reset()
        wk = [ar.get([128, W]) for _ in range(5)]
        layer_norm(Wt, 6, 7, 4, 5, wk)
        if ti > 0:
            kb.dma("sp", outT[:, W * (ti - 1):W * ti].rearrange("(c p) w -> p c w", p=128), hT[:, :, 0:Wt], reads=[t_hT], writes=[t_out])
        dump("d_l1mix", pos0, Wt)
    kb.mute = False
    kb.barrier(final=True)
    return nc


def slab(Wm):
    K, M = Wm.shape
    return np.ascontiguousarray(Wm.reshape(K // 128, 128, M // 128, 128).transpose(2, 1, 0, 3)).reshape(M // 128, 128, K)


def pvec(v):
    return np.ascontiguousarray(v.reshape(-1, 128).T)


def prepare(inp, NT):
    f = np.float32
    NTOK = NMETA + W * NT
    com = {}
    w_in = inp["l0_w_in"]
    kr = w_in[:, 1792:1856]
    krs = np.concatenate([kr[:, 32:], kr[:, :32]], axis=1)
    com["w0in"] = slab(np.concatenate([w_in[:, :1792], kr, kr, krs, krs], axis=1))
    com["wglu"] = slab(inp["l0_s5_w_glu"])
    wuq = inp["l0_mla_w_uq"].reshape(512, 8, 192)
    cols = []
    for hp in range(4):
        h0, h1 = 2 * hp, 2 * hp + 1
        r0, r1 = wuq[:, h0, 128:], wuq[:, h1, 128:]
        sw = lambda r: np.concatenate([r[:, 32:], r[:, :32]], axis=1)
        cols += [wuq[:, h0, :128], wuq[:, h1, :128], np.concatenate([r0, r1], 1), np.concatenate([sw(r0), sw(r1)], 1)]
    com["wuq"] = slab(np.concatenate(cols, axis=1))
    wukv = inp["l0_mla_w_ukv"].reshape(256, 8, 256)
    com["wuk"] = slab(np.ascontiguousarray(wukv[:, :, :128]).reshape(256, 1024))
    com["wuv"] = np.ascontiguousarray(np.ascontiguousarray(wukv[:, :, 128:]).reshape(2, 128, 1024).transpose(1, 0, 2))
    com["w0out"] = slab(inp["l0_w_out"])
    com["w0g"] = slab(inp["l0_ffn_w_gate"])
    com["w0u"] = slab(inp["l0_ffn_w_up"])
    com["w0d0"] = slab(inp["l0_ffn_w_down"][:FF // 2])
    com["w0d1"] = slab(inp["l0_ffn_w_down"][FF // 2:])
    com["w1g"] = slab(inp["l1_ffn_w_gate"])
    com["w1u"] = slab(inp["l1_ffn_w_up"])
    com["w1d0"] = slab(inp["l1_ffn_w_down"][:FF // 2])
    com["w1d1"] = slab(inp["l1_ffn_w_down"][FF // 2:])
    w1 = inp["l1_w_in"]
    z, xs, Bm, Cm, dtc = w1[:, :4096], w1[:, 4096:8192], w1[:, 8192:9216], w1[:, 9216:10240], w1[:, 10240:]
    cols = [np.concatenate([dtc, np.zeros((2048, 64), f)], 1)]
    for g in range(8):
        cols += [xs[:, g * 512:(g + 1) * 512], z[:, g * 512:(g + 1) * 512], Bm[:, g * 128:(g + 1) * 128], Cm[:, g * 128:(g + 1) * 128]]
    com["w1in"] = slab(np.concatenate(cols, axis=1))
    com["w1out"] = slab(inp["l1_w_out"])
    com["vec"] = np.ascontiguousarray(np.stack([pvec(inp[k]) for k in (
        "l0_ln1_g", "l0_ln1_b", "l0_ln2_g", "l0_ln2_b", "l1_ln1_g", "l1_ln1_b", "l1_ln2_g", "l1_ln2_b")], axis=1))
    com["s5d"] = pvec(inp["l0_s5_d"])
    com["qn"] = pvec(inp["l0_mla_q_norm"])
    com["kvn"] = pvec(inp["l0_mla_kv_norm"])
    cwm = inp["l1_conv_w"]
    com["cw"] = np.ascontiguousarray(cwm.T.reshape(48, 128, 4).transpose(1, 0, 2))
    com["cb"] = pvec(inp["l1_conv_b"])
    dtb = np.zeros((128, 2), f)
    dtb[:64, 0] = inp["l1_dt_bias"]
    dtb[:64, 1] = inp["l1_a_log"]
    com["dtb"] = dtb
    com["l1d"] = pvec(np.repeat(inp["l1_d"], 64))
    com["ng"] = pvec(inp["l1_norm_g"])
    ldt = inp["l0_s5_log_dt"]
    are, aim = inp["l0_s5_a_re"], inp["l0_s5_a_im"]
    pl = lambda a: np.ascontiguousarray(a.reshape(32, 128).T)
    com["s5p"] = np.ascontiguousarray(np.stack([pl(np.repeat(ldt[:, None], 64, 1)), pl(are), pl(aim)], axis=1))
    row = lambda a: np.ascontiguousarray(np.broadcast_to(a.reshape(1, 32, 128), (128, 32, 128)))
    com["s5r"] = np.stack([row(np.repeat(ldt[:, None], 64, 1)), row(are), row(aim)], axis=0)
    bl = np.zeros((2, 128, 32, 128), f)
    cl = np.zeros((32, 128, 2, 128), f)
    for g in range(64):
        j, a = g // 2, g % 2
        q = j % 4
        r0 = 32 * q + 16 * a
        for c_, (bsrc, csrc) in enumerate(((inp["l0_s5_b_re"], inp["l0_s5_c_re"]), (inp["l0_s5_b_im"], inp["l0_s5_c_im"]))):
            bl[c_, r0:r0 + 16, j, 64 * a:64 * a + 64] = bsrc[g].T
            cl[j, 64 * a:64 * a + 64, c_, r0:r0 + 16] = csrc[g].T
    com["s5b"] = bl
    com["s5c"] = cl
    k = np.arange(128)
    cst = np.zeros((128, 640), f)
    cst[:, 0:128] = np.eye(128)
    cst[:, 128:256] = (k[:, None] > k[None, :])
    cst[:, 256:384] = 1.0
    cst[:, 384:512] = (k[:, None] <= k[None, :])
    cst[:, 512:640] = 1.0
    com["cst"] = cst
    com["iota"] = np.ascontiguousarray(np.broadcast_to(np.arange(W, dtype=f)[None], (128, W)))
    com = {k_: np.ascontiguousarray(v, dtype=f) for k_, v in com.items()}
    NFR = W * NT
    inv = (10000.0 ** (-np.arange(0, 64, 2, dtype=f) / 64)).astype(f)
    maps = []
    for b in range(inp["x"].shape[0]):
        for c in range(4):
            m = dict(com)
            m["xT"] = np.ascontiguousarray(np.concatenate([inp["meta_tokens"].T, inp["x"][b, NFR * c:NFR * (c + 1)].T], axis=1), dtype=f)
            pos = np.concatenate([np.arange(NMETA), NMETA + NFR * c + np.arange(NFR)]).astype(f)
            ang = (pos[None, :] * inv[:, None]).astype(f)
            c32, s32 = np.cos(ang).astype(f), np.sin(ang).astype(f)
            m["ropeC"] = np.ascontiguousarray(np.concatenate([c32, c32, c32, c32], 0))
            m["ropeS"] = np.ascontiguousarray(np.concatenate([-s32, s32, -s32, s32], 0))
            sel = np.zeros((128, 8), f)
            sel[:, c] = 1.0
            m["sel"] = sel
            segb = np.zeros((128, 4), f)
            segb[:, c:] = -30000.0
            m["segb"] = segb
            maps.append(m)
    return maps


def kernel(**inputs):
    inp = {k: np.asarray(v) for k, v in inputs.items()}
    NT = SEQ // W // 4
    nc = build(NT)
    maps = prepare(inp, NT)
    res = run_bass_kernel_spmd(nc, maps, core_ids=list(range(8)))
    out = np.stack([np.concatenate([np.ascontiguousarray(res.results[b * 4 + c]["outT"].T) for c in range(4)], axis=0)
                    for b in range(inp["x"].shape[0])], axis=0)
    return out.astype(np.float32)
```
